# Optimizing a Trainium2 kernel written in Bass

```python
import math
import jax, jax.numpy as jnp
from jax import lax
import numpy as np

D_MODEL = 1024
BATCH = 8
SEQ = 2048
DEPTH = 1

MEM_LEN = 256
HEAD_DIM = 64
NSA_HEADS = 16
NSA_KV_GROUPS = 4
NSA_HPG = NSA_HEADS // NSA_KV_GROUPS
CMP_LEN = 32
CMP_STRIDE = 16
CMP_HIDDEN = 256
SEL_BLOCK = 64
SEL_TOPK = 8
WINDOW = 512
Q_BLOCK = 128
ROPE_THETA = 500000.0
ROPE_DIM = HEAD_DIM // 4
RNN_WIDTH = D_MODEL
RNN_BLOCKS = 16
RNN_BLOCK_DIM = RNN_WIDTH // RNN_BLOCKS
CONV_WIDTH = 4
RGLRU_C = 8.0
XATTN_HEADS = 4
XATTN_WIDTH = XATTN_HEADS * HEAD_DIM
D_FF = 4 * D_MODEL
N_BRANCHES = 3
RMS_EPS = 1e-6

NSA_Q = NSA_HEADS * HEAD_DIM
NSA_KV = NSA_KV_GROUPS * HEAD_DIM
NSA_GATES = NSA_HEADS * 3
IN_WIDTHS = (NSA_Q, NSA_KV, NSA_KV, NSA_KV, NSA_KV, NSA_KV, NSA_KV, NSA_GATES,
             RNN_WIDTH, RNN_WIDTH, XATTN_WIDTH, N_BRANCHES * D_MODEL)
D_IN = sum(IN_WIDTHS)

kernel_name = "hybrid_nsa_rglru_gated_block"


def rms_norm(x, g):
    xf = x.astype(jnp.float32)
    y = xf * lax.rsqrt(jnp.mean(xf * xf, axis=-1, keepdims=True) + RMS_EPS)
    return (y * g.astype(jnp.float32)).astype(x.dtype)


def masked_softmax(s, mask):
    s = jnp.where(mask, s.astype(jnp.float32), -jnp.inf)
    m = jnp.max(s, axis=-1, keepdims=True)
    m = jnp.where(jnp.isfinite(m), m, 0.0)
    e = jnp.exp(s - m)
    return e / jnp.maximum(jnp.sum(e, axis=-1, keepdims=True), 1e-30)


def rope_tables(pos):
    inv = 1.0 / (ROPE_THETA ** (jnp.arange(0, ROPE_DIM, 2, dtype=jnp.float32) / ROPE_DIM))
    ang = pos.astype(jnp.float32)[:, None] * inv[None, :]
    return jnp.cos(ang), jnp.sin(ang)


def apply_partial_rope(x, cos, sin):
    half = ROPE_DIM // 2
    xf = x.astype(jnp.float32)
    x1 = xf[..., :half]
    x2 = xf[..., half:ROPE_DIM]
    out = jnp.concatenate([x1 * cos - x2 * sin, x2 * cos + x1 * sin, xf[..., ROPE_DIM:]], axis=-1)
    return out.astype(x.dtype)


def cmp_to_sel_weights(n_cmp, n_sel):
    c0 = np.arange(n_cmp)[:, None] * CMP_STRIDE
    s0 = np.arange(n_sel)[None, :] * SEL_BLOCK
    ov = np.clip(np.minimum(c0 + CMP_LEN, s0 + SEL_BLOCK) - np.maximum(c0, s0), 0, None)
    return (ov / CMP_LEN).astype(np.float32)


def compress_blocks(kv, pos_emb, w1, w2):
    B, G, S, hd = kv.shape
    n_cmp = (S - CMP_LEN) // CMP_STRIDE + 1
    idx = jnp.arange(n_cmp)[:, None] * CMP_STRIDE + jnp.arange(CMP_LEN)[None, :]
    blocks = kv[:, :, idx, :] + pos_emb
    flat = blocks.reshape(B, G, n_cmp, CMP_LEN * hd)
    return jax.nn.gelu(flat @ w1) @ w2


def nsa_mixer(q, k_c, v_c, k_s, v_s, k_w, v_w, gates):
    B, G, Hg, S, hd = q.shape
    n_cmp = k_c.shape[2]
    n_sel = S // SEL_BLOCK
    n_top = min(SEL_TOPK, n_sel)
    nq = S // Q_BLOCK
    scale = HEAD_DIM ** -0.5
    c2s = jnp.asarray(cmp_to_sel_weights(n_cmp, n_sel))
    cmp_end = jnp.arange(n_cmp) * CMP_STRIDE + (CMP_LEN - 1)
    ks_blocks = k_s.reshape(B, G, n_sel, SEL_BLOCK, hd)
    vs_blocks = v_s.reshape(B, G, n_sel, SEL_BLOCK, hd)
    kw_pad = jnp.pad(k_w, ((0, 0), (0, 0), (WINDOW, 0), (0, 0)))
    vw_pad = jnp.pad(v_w, ((0, 0), (0, 0), (WINDOW, 0), (0, 0)))
    bi = jnp.arange(B)[:, None, None, None]
    gi = jnp.arange(G)[None, :, None, None]
    blk_ids = jnp.arange(n_sel)

    def block(args):
        i, q_b, g_b = args
        t = i * Q_BLOCK + jnp.arange(Q_BLOCK)
        s_c = jnp.einsum('bghqd,bgnd->bghqn', q_b, k_c) * scale
        p_c = masked_softmax(s_c, cmp_end[None, :] <= t[:, None])
        o_c = jnp.einsum('bghqn,bgnd->bghqd', p_c.astype(q_b.dtype), v_c)
        imp = jnp.einsum('bghqn,ns->bgqs', p_c, c2s)
        cur = (t // SEL_BLOCK)[:, None]
        forced = (blk_ids == 0) | (blk_ids == cur) | (blk_ids == cur - 1)
        future = blk_ids * SEL_BLOCK > t[:, None]
        imp = jnp.where(forced, jnp.inf, jnp.where(future, -jnp.inf, imp))
        _, idx = lax.top_k(imp, n_top)
        k_g = ks_blocks[bi, gi, idx]
        v_g = vs_blocks[bi, gi, idx]
        key_pos = (idx[..., None] * SEL_BLOCK + jnp.arange(SEL_BLOCK)).reshape(B, G, 1, Q_BLOCK, n_top * SEL_BLOCK)
        s_s = jnp.einsum('bghqd,bgqkld->bghqkl', q_b, k_g).reshape(B, G, Hg, Q_BLOCK, n_top * SEL_BLOCK) * scale
        p_s = masked_softmax(s_s, key_pos <= t[:, None])
        o_s = jnp.einsum('bghqkl,bgqkld->bghqd',
                         p_s.astype(q_b.dtype).reshape(B, G, Hg, Q_BLOCK, n_top, SEL_BLOCK), v_g)
        start = i * Q_BLOCK
        k_b = lax.dynamic_slice_in_dim(kw_pad, start, Q_BLOCK + WINDOW, axis=2)
        v_b = lax.dynamic_slice_in_dim(vw_pad, start, Q_BLOCK + WINDOW, axis=2)
        w_pos = start - WINDOW + jnp.arange(Q_BLOCK + WINDOW)
        mask_w = (w_pos[None, :] <= t[:, None]) & (w_pos[None, :] > t[:, None] - WINDOW)
        s_w = jnp.einsum('bghqd,bgkd->bghqk', q_b, k_b) * scale
        p_w = masked_softmax(s_w, mask_w)
        o_w = jnp.einsum('bghqk,bgkd->bghqd', p_w.astype(q_b.dtype), v_b)
        return g_b[..., 0:1] * o_c + g_b[..., 1:2] * o_s + g_b[..., 2:3] * o_w

    q_blocks = jnp.moveaxis(q.reshape(B, G, Hg, nq, Q_BLOCK, hd), 3, 0)
    g_blocks = jnp.moveaxis(gates.reshape(B, G, Hg, nq, Q_BLOCK, 3), 3, 0)
    out = lax.map(block, (jnp.arange(nq), q_blocks, g_blocks))
    out = jnp.moveaxis(out, 0, 3).reshape(B, G, Hg, S, hd)
    return out.transpose(0, 3, 1, 2, 4).reshape(B, S, G * Hg * hd)


def causal_depthwise_conv(x, w, b):
    S = x.shape[1]
    xp = jnp.pad(x, ((0, 0), (CONV_WIDTH - 1, 0), (0, 0)))
    y = b
    for k in range(CONV_WIDTH):
        y = y + xp[:, k:k + S, :] * w[k]
    return y


def rg_lru(xr, w_a, b_a, w_i, b_i, lam):
    B, S, C = xr.shape
    xb = xr.reshape(B, S, RNN_BLOCKS, RNN_BLOCK_DIM)
    r = jax.nn.sigmoid(jnp.einsum('bsnk,nkj->bsnj', xb, w_a).reshape(B, S, C) + b_a)
    gi = jax.nn.sigmoid(jnp.einsum('bsnk,nkj->bsnj', xb, w_i).reshape(B, S, C) + b_i)
    log_a = -RGLRU_C * r.astype(jnp.float32) * jax.nn.softplus(-lam.astype(jnp.float32))
    a = jnp.exp(log_a)
    mult = jnp.sqrt(-jnp.expm1(2.0 * log_a))
    u = mult * (gi * xr).astype(jnp.float32)

    def combine(c1, c2):
        a1, b1 = c1
        a2, b2 = c2
        return a1 * a2, a2 * b1 + b2

    _, h = lax.associative_scan(combine, (a, u), axis=1)
    return h.astype(xr.dtype)


def cross_attention(q_x, mem, g_mem, w_mem_kv, w_xo):
    B, S, _ = q_x.shape
    M = mem.shape[1]
    q = q_x.reshape(B, S, XATTN_HEADS, HEAD_DIM)
    kv = (rms_norm(mem, g_mem) @ w_mem_kv).reshape(B, M, 2, XATTN_HEADS, HEAD_DIM)
    s = jnp.einsum('bshd,bmhd->bhsm', q, kv[:, :, 0]) * (HEAD_DIM ** -0.5)
    p = jax.nn.softmax(s.astype(jnp.float32), axis=-1).astype(q.dtype)
    o = jnp.einsum('bhsm,bmhd->bshd', p, kv[:, :, 1]).reshape(B, S, XATTN_WIDTH)
    return o @ w_xo


def setup_inputs(seed: int = 0) -> dict:
    key = jax.random.key(seed)
    ks = jax.random.split(key, 32)
    L = DEPTH

    def nrm(k, shape, fan_in):
        return jax.random.normal(k, shape, jnp.float32) * (fan_in ** -0.5)

    def gain(k, shape):
        return 1.0 + 0.02 * jax.random.normal(k, shape, jnp.float32)

    def small(k, shape):
        return 0.02 * jax.random.normal(k, shape, jnp.float32)

    u = jax.random.uniform(ks[20], (L, RNN_WIDTH), jnp.float32, 0.9, 0.999) ** (1.0 / RGLRU_C)
    rg_lambda = jnp.log(u) - jnp.log1p(-u)
    return {
        "x": jax.random.normal(ks[0], (BATCH, SEQ, D_MODEL), jnp.float32),
        "mem": jax.random.normal(ks[1], (BATCH, MEM_LEN, D_MODEL), jnp.float32),
        "g_mix": gain(ks[2], (L, D_MODEL)),
        "w_in": nrm(ks[3], (L, D_MODEL, D_IN), D_MODEL),
        "cmp_pos_k": small(ks[4], (L, CMP_LEN, HEAD_DIM)),
        "cmp_pos_v": small(ks[5], (L, CMP_LEN, HEAD_DIM)),
        "w_cmp_k1": nrm(ks[6], (L, CMP_LEN * HEAD_DIM, CMP_HIDDEN), CMP_LEN * HEAD_DIM),
        "w_cmp_k2": nrm(ks[7], (L, CMP_HIDDEN, HEAD_DIM), CMP_HIDDEN),
        "w_cmp_v1": nrm(ks[8], (L, CMP_LEN * HEAD_DIM, CMP_HIDDEN), CMP_LEN * HEAD_DIM),
        "w_cmp_v2": nrm(ks[9], (L, CMP_HIDDEN, HEAD_DIM), CMP_HIDDEN),
        "conv_w": nrm(ks[10], (L, CONV_WIDTH, RNN_WIDTH), CONV_WIDTH),
        "conv_b": small(ks[11], (L, RNN_WIDTH)),
        "w_rg_a": nrm(ks[12], (L, RNN_BLOCKS, RNN_BLOCK_DIM, RNN_BLOCK_DIM), RNN_BLOCK_DIM),
        "b_rg_a": small(ks[13], (L, RNN_WIDTH)),
        "w_rg_i": nrm(ks[14], (L, RNN_BLOCKS, RNN_BLOCK_DIM, RNN_BLOCK_DIM), RNN_BLOCK_DIM),
        "b_rg_i": small(ks[15], (L, RNN_WIDTH)),
        "rg_lambda": rg_lambda,
        "g_mem": gain(ks[16], (L, D_MODEL)),
        "w_mem_kv": nrm(ks[17], (L, D_MODEL, 2 * XATTN_WIDTH), D_MODEL),
        "w_xo": nrm(ks[18], (L, XATTN_WIDTH, D_MODEL), XATTN_WIDTH),
        "w_o": nrm(ks[19], (L, D_MODEL, D_MODEL), D_MODEL),
        "g_mlp": gain(ks[21], (L, D_MODEL)),
        "w_up": nrm(ks[22], (L, D_MODEL, D_FF), D_MODEL),
        "w_down": nrm(ks[23], (L, D_FF, D_MODEL), D_FF),
        "g_final": gain(ks[24], (D_MODEL,)),
    }


def reference(x, mem, g_mix, w_in, cmp_pos_k, cmp_pos_v, w_cmp_k1, w_cmp_k2, w_cmp_v1, w_cmp_v2,
              conv_w, conv_b, w_rg_a, b_rg_a, w_rg_i, b_rg_i, rg_lambda, g_mem, w_mem_kv, w_xo,
              w_o, g_mlp, w_up, w_down, g_final):
    B, S, D = x.shape
    G, Hg, hd = NSA_KV_GROUPS, NSA_HPG, HEAD_DIM
    offsets = np.cumsum(IN_WIDTHS)[:-1].tolist()
    cos, sin = rope_tables(jnp.arange(S))
    n_cmp = (S - CMP_LEN) // CMP_STRIDE + 1
    cos_c, sin_c = rope_tables(jnp.arange(n_cmp) * CMP_STRIDE + (CMP_LEN - 1))

    def to_groups(t):
        return t.reshape(B, S, G, hd).transpose(0, 2, 1, 3)

    h = x
    for l in range(DEPTH):
        u = rms_norm(h, g_mix[l])
        z = u @ w_in[l]
        (q_n, kc_raw, vc_raw, k_s, v_s, k_w, v_w, nsa_g,
         x_rnn, g_rnn, q_x, merge_g) = jnp.split(z, offsets, axis=-1)

        q = q_n.reshape(B, S, G, Hg, hd).transpose(0, 2, 3, 1, 4)
        q = apply_partial_rope(q, cos, sin)
        k_s = apply_partial_rope(to_groups(k_s), cos, sin)
        k_w = apply_partial_rope(to_groups(k_w), cos, sin)
        k_c = compress_blocks(to_groups(kc_raw), cmp_pos_k[l], w_cmp_k1[l], w_cmp_k2[l])
        k_c = apply_partial_rope(k_c, cos_c, sin_c)
        v_c = compress_blocks(to_groups(vc_raw), cmp_pos_v[l], w_cmp_v1[l], w_cmp_v2[l])
        nsa_gates = jax.nn.sigmoid(nsa_g).reshape(B, S, G, Hg, 3).transpose(0, 2, 3, 1, 4)
        y_nsa = nsa_mixer(q, k_c, v_c, k_s, to_groups(v_s), k_w, to_groups(v_w), nsa_gates)

        xc = causal_depthwise_conv(x_rnn, conv_w[l], conv_b[l])
        hr = rg_lru(xc, w_rg_a[l], b_rg_a[l], w_rg_i[l], b_rg_i[l], rg_lambda[l])
        y_rnn = jax.nn.gelu(g_rnn) * hr

        y_x = cross_attention(q_x, mem, g_mem[l], w_mem_kv[l], w_xo[l])

        gm = jax.nn.sigmoid(merge_g).reshape(B, S, N_BRANCHES, D)
        y = gm[:, :, 0] * y_nsa + gm[:, :, 1] * y_rnn + gm[:, :, 2] * y_x
        h = h + y @ w_o[l]

        v = rms_norm(h, g_mlp[l])
        h = h + jnp.square(jax.nn.relu(v @ w_up[l])) @ w_down[l]

    return rms_norm(h, g_final)
```

```python
import contextlib
import numpy as np
import ml_dtypes
import concourse.bass as bass
import concourse.mybir as mybir
from concourse.bass_utils import run_bass_kernel_spmd

F32 = mybir.dt.float32
BF16 = mybir.dt.bfloat16
AF = mybir.ActivationFunctionType
ALU = mybir.AluOpType
NPBF = ml_dtypes.bfloat16

T = 2048
D = 1024
NT = 16
KD = 8
NCMP = 127
C_Q, C_KC, C_VC, C_KS, C_VS, C_KW, C_VW, C_NG, C_XR, C_GR, C_QX, C_MG = (
    0, 1024, 1280, 1536, 1792, 2048, 2304, 2560, 2608, 3632, 4656, 4912)
D_IN = 7984
SLOT_HH = [0, 2, 1, 3]


class _Eng:
    def __init__(self, name):
        self.name = name
        self.sem = None
        self.count = 0
        self.seen = {}
        self.ops = []


class Sched:
    NS = 6

    def __init__(self, nc, stack):
        self.nc = nc
        self.E = {}
        for n in ("pe", "act", "dve", "pool", "sp"):
            e = _Eng(n)
            e.sem = stack.enter_context(nc.semaphore("sem_" + n))
            e.miles = set()
            self.E[n] = e
        self.dsem = {}
        self.dk = {}
        for q in ("sp", "pool", "act"):
            self.dsem[q] = [stack.enter_context(nc.semaphore(f"dq_{q}_{i}")) for i in range(self.NS)]
            self.dk[q] = 0
        self.last_w = {}
        self.readers = {}

    def _toks(self, reads, writes):
        toks = []
        for k in reads:
            t = self.last_w.get(k)
            if t is not None:
                toks.append(t)
        for k in writes:
            t = self.last_w.get(k)
            if t is not None:
                toks.append(t)
            r = self.readers.get(k)
            if r:
                toks.extend(r.values())
        return toks

    def _waits(self, e, toks):
        need = {}
        for t in toks:
            if t[0] == "c":
                if t[1] == e.name and e.name == "pe":
                    continue
                key, v = t[1], t[2]
            else:
                key, v = t[1].num, t[2]
            if e.seen.get(key, 0) >= v:
                continue
            if key not in need or need[key][2] < v:
                need[key] = t
        for key, t in need.items():
            e.ops.append(("wait", t))
            e.seen[key] = t[2]
            if t[0] == "c":
                self.E[t[1]].miles.add(t[2])

    def _update(self, tok, reads, writes):
        ok = (tok[0], tok[1] if tok[0] == "c" else tok[1].num)
        for k in reads:
            self.readers.setdefault(k, {})[ok] = tok
        for k in writes:
            self.last_w[k] = tok
            self.readers[k] = {}

    def op(self, eng, fn, reads=(), writes=()):
        e = self.E[eng]
        self._waits(e, self._toks(reads, writes))
        e.count += 1
        e.ops.append(("op", fn, e.count))
        self._update(("c", e.name, e.count), reads, writes)

    def dma(self, q, out, in_, reads=(), writes=(), **kw):
        e = self.E[q]
        k = self.dk[q]
        self.dk[q] += 1
        slot, rnd = k % self.NS, k // self.NS
        dsem = self.dsem[q][slot]
        toks = self._toks(reads, writes)
        if rnd > 0:
            toks.append(("d", dsem, 16 * rnd))
        self._waits(e, toks)
        e.ops.append(("dma", out, in_, kw, dsem))
        self._update(("d", dsem, 16 * (rnd + 1)), reads, writes)

    def wait_keys(self, eng, keys):
        self._waits(self.E[eng], [self.last_w[k] for k in keys if k in self.last_w])

    def barrier(self):
        toks = []
        for e in self.E.values():
            if e.count > 0:
                toks.append(("c", e.name, e.count))
        for q in self.dsem:
            k = self.dk[q]
            for slot in range(self.NS):
                n = (k - slot + self.NS - 1) // self.NS
                if n > 0:
                    toks.append(("d", self.dsem[q][slot], 16 * n))
        for e in self.E.values():
            self._waits(e, [t for t in toks if not (t[0] == "c" and t[1] == e.name)])

    def emit(self):
        nc, E = self.nc, self.E
        rank = {}
        for n, e in E.items():
            rank[n] = {idx: r + 1 for r, idx in enumerate(sorted(e.miles))}

        def run(e, h):
            for item in e.ops:
                if item[0] == "wait":
                    t = item[1]
                    if t[0] == "c":
                        h.wait_ge(E[t[1]].sem, rank[t[1]][t[2]])
                    else:
                        h.wait_ge(t[1], t[2])
                elif item[0] == "op":
                    ins = item[1](h)
                    if item[2] in e.miles:
                        ins.then_inc(e.sem, 1)
                else:
                    _, out, in_, kw, dsem = item
                    h.dma_start(out=out, in_=in_, **kw).then_inc(dsem, 16)

        with nc.Block() as block:
            @block.tensor
            def _(h):
                run(E["pe"], h)

            @block.scalar
            def _(h):
                run(E["act"], h)

            @block.vector
            def _(h):
                run(E["dve"], h)

            @block.gpsimd
            def _(h):
                run(E["pool"], h)

            @block.sync
            def _(h):
                run(E["sp"], h)


def f_mm(out, lhsT, rhs, start=True, stop=True, skip=False):
    return lambda h: h.matmul(out, lhsT=lhsT, rhs=rhs, start=start, stop=stop, skip_group_check=skip)


def f_tr(out, in_, ident):
    return lambda h: h.transpose(out=out, in_=in_, identity=ident)


def f_act(out, in_, func, bias=None, scale=None, accum_out=None):
    kw = {}
    if bias is not None:
        kw["bias"] = bias
    if scale is not None:
        kw["scale"] = scale
    if accum_out is not None:
        kw["accum_out"] = accum_out
    return lambda h: h.activation(out=out, in_=in_, func=func, **kw)


def f_copy(out, in_):
    return lambda h: (h.copy(out=out, in_=in_) if hasattr(h, "activation") else h.tensor_copy(out=out, in_=in_))


def f_tt(out, in0, in1, op):
    return lambda h: h.tensor_tensor(out=out, in0=in0, in1=in1, op=op)


def f_ts(out, in0, s1, s2, op0, op1=None):
    if op1 is None:
        return lambda h: h.tensor_scalar(out=out, in0=in0, scalar1=s1, scalar2=None, op0=op0)
    return lambda h: h.tensor_scalar(out=out, in0=in0, scalar1=s1, scalar2=s2, op0=op0, op1=op1)


def f_stt(out, in0, scalar, in1, op0, op1):
    return lambda h: h.scalar_tensor_tensor(out=out, in0=in0, scalar=scalar, in1=in1, op0=op0, op1=op1)


def f_memset(ap, c):
    return lambda h: h.memset(ap, c)


def _consts():
    c = {}
    c["ident"] = np.eye(128, dtype=np.float32).astype(NPBF)
    inv = (1.0 / (500000.0 ** (np.arange(0, 16, 2, dtype=np.float32) / 16.0))).astype(np.float32)

    def tabs(pos):
        ang = pos.astype(np.float32)[:, None] * inv[None, :]
        cs, sn = np.cos(ang).astype(np.float32), np.sin(ang).astype(np.float32)
        Ct = np.ones((128, len(pos)), np.float32)
        St = np.zeros((128, len(pos)), np.float32)
        for half in (0, 64):
            Ct[half:half + 8] = cs.T
            Ct[half + 8:half + 16] = cs.T
            St[half:half + 8] = -sn.T
            St[half + 8:half + 16] = sn.T
        return Ct, St
    c["ropeC"], c["ropeS"] = tabs(np.arange(T))
    c["ropeCc"], c["ropeSc"] = tabs(np.arange(NCMP) * 16 + 31)
    RT = np.zeros((128, 128), np.float32)
    for half in (0, 64):
        for j in range(8):
            RT[half + j + 8, half + j] = 1.0
            RT[half + j, half + j + 8] = 1.0
    c["RT"] = RT.astype(NPBF)
    kk = np.arange(128)[:, None]
    qq = np.arange(128)[None, :]
    c["mcaus"] = (kk <= qq).astype(np.float32).astype(NPBF)
    c["mwlow"] = (kk > qq).astype(np.float32).astype(NPBF)
    n = np.arange(128)[:, None]
    t = np.arange(T)[None, :]
    c["cmpmask"] = ((16 * n + 31 <= t) & (n < NCMP)).astype(np.float32).astype(NPBF)
    c0 = np.arange(NCMP)[:, None] * 16
    s0 = np.arange(32)[None, :] * 64
    ov = np.clip(np.minimum(c0 + 32, s0 + 64) - np.maximum(c0, s0), 0, None) / 32.0
    c2s = np.zeros((128, 33), np.float32)
    c2s[:NCMP, 0] = 1.0
    c2s[:NCMP, 1:] = ov
    c["c2s33"] = c2s.astype(NPBF)
    E = np.zeros((128, T), np.float32)
    E[np.arange(T) // 64, np.arange(T)] = 1.0
    E[64 + np.arange(T) // 64, np.arange(T)] = 1.0
    c["E64"] = E.astype(NPBF)
    keep = np.zeros((128, 16, 32), np.float32)
    bias = np.zeros((128, 16, 32), np.float32)
    for i in range(16):
        tt = 128 * i + np.arange(128)
        cur = (tt // 64)[:, None]
        blk = np.arange(32)[None, :]
        forced = (blk == 0) | (blk == cur) | (blk == cur - 1)
        future = blk * 64 > tt[:, None]
        keep[:, i, :] = (~forced & ~future).astype(np.float32)
        bias[:, i, :] = np.where(forced, 10.0, np.where(future, -10.0, 0.0))
    c["selkeep"] = keep
    c["selbias"] = bias
    ph = np.zeros((128, 16), np.float32)
    for i in range(16):
        ph[:, i] = np.maximum(0, 511 - (128 * i + np.arange(128)))
    c["phantom"] = ph
    return c


CONST_SPECS = [("ident", [128, 128], BF16), ("ropeC", [128, T], F32), ("ropeS", [128, T], F32),
               ("ropeCc", [128, NCMP], F32), ("ropeSc", [128, NCMP], F32), ("RT", [128, 128], BF16),
               ("mcaus", [128, 128], BF16), ("mwlow", [128, 128], BF16), ("cmpmask", [128, T], BF16),
               ("c2s33", [128, 33], BF16), ("E64", [128, T], BF16), ("selkeep", [128, 16, 32], F32),
               ("selbias", [128, 16, 32], F32), ("phantom", [128, 16], F32)]

IN_SPECS = [("x", [T, D]), ("mem", [256, D]), ("g_mix", [D]), ("w_in", [D, D_IN]),
            ("cmp_pos_k", [32, 64]), ("cmp_pos_v", [32, 64]), ("w_cmp_k1", [2048, 256]), ("w_cmp_k2", [256, 64]),
            ("w_cmp_v1", [2048, 256]), ("w_cmp_v2", [256, 64]), ("conv_w", [4, D]), ("conv_b", [D]),
            ("w_rg_a", [16, 64, 64]), ("b_rg_a", [D]), ("w_rg_i", [16, 64, 64]), ("b_rg_i", [D]),
            ("rg_lambda", [D]), ("g_mem", [D]), ("w_mem_kv", [D, 512]), ("w_xo", [256, D]),
            ("w_o", [D, D]), ("g_mlp", [D]), ("w_up", [D, 4096]), ("w_down", [4096, D]), ("g_final", [D])]


def attention(stQ, S, nc, sb, sap, PP, A, dh, ident, QT, gates, KsT, KwT, Vsw, KcT, Vc, yT, GORD):
    with contextlib.ExitStack() as st:
        cmpm = sb(st, "cmpm", [128, T], BF16)
        E64t = sb(st, "E64t", [128, T], BF16)
        keep = sb(st, "keep", [128, 16, 32], F32)
        sbias = sb(st, "sbias", [128, 16, 32], F32)
        phant = sb(st, "phant", [128, 16], F32)
        mcaus = sb(st, "mcaus", [128, 128], BF16)
        mwlow = sb(st, "mwlow", [128, 128], BF16)
        for t_, nm in ((cmpm, "cmpmask"), (E64t, "E64"), (keep, "selkeep"), (sbias, "selbias"), (phant, "phantom"),
                       (mcaus, "mcaus"), (mwlow, "mwlow")):
            S.dma("sp", t_[:], A["c_" + nm], writes=["c_" + nm])
        PT = [sb(st, f"PT{i}", [128, 8, 128], BF16) for i in range(6)]
        selm = [sb(st, f"selm{i}", [128, 128], BF16) for i in range(2)]
        selT = [sb(st, f"selT{i}", [128, 128], BF16) for i in range(2)]
        for i in range(2):
            S.op("pool", f_memset(selm[i][:], 0.0), writes=[("selm", i)])
        ytiles = [sb(st, f"ytile{i}", [128, D], F32) for i in range(2)]
        ybf = [sb(st, f"ybf{i}", [128, D], BF16) for i in range(2)]
        scr = [sb(st, f"scr{i}", [128, 128], F32) for i in range(2)]
        scr2 = [sb(st, f"scr2{i}", [128, 4, 32], F32) for i in range(2)]
        tmpn = [sb(st, f"tmpn{i}", [128, 256], F32) for i in range(2)]
        ACCB = {"c": (4, 97), "s": (5, 65), "w": (6, 65)}
        BR = {"c": 0, "s": 1, "w": 2}

        def acc_slot(br, s_):
            j, w = ACCB[br]
            return PP[j // 2][:, j % 2, s_ * w:(s_ + 1) * w]

        def acc_data(br):
            j, w = ACCB[br]
            return sap(PP[j // 2], 1024, 0, 128, (j % 2) * 512, [[2 * w, 2], [w, 2], [1, 64]])

        def acc_sums(br):
            j, w = ACCB[br]
            return sap(PP[j // 2], 1024, 0, 128, (j % 2) * 512 + 64, [[w, 4]])

        def mask_sb(mt, np_):
            return bass.AP(mt, 0, [[128, np_], [0, 4], [1, 128]])

        steps = []
        parts = []
        deferred = []
        nig = [0]
        tmpc = [0]
        mreg = [0]

        for i in range(NT):
            for g in range(4):
                ig = nig[0]
                nig[0] += 1
                sk = ig % 2
                sc = scr[sk]
                gslot = GORD.index(g)
                qkeys = [("QT", 2 * g, i // 4), ("QT", 2 * g + 1, i // 4)]
                QA = QT[0:64, 2 * g:2 * g + 2, i * 128:(i + 1) * 128]
                QB = QT[64:128, 2 * g:2 * g + 2, i * 128:(i + 1) * 128]

                def mk_step(tiles, after, QA=QA, QB=QB, qkeys=qkeys):
                    nt_ = len(tiles)
                    np_ = tiles[0]["np"]
                    w_ = nt_ * 256

                    def scores(pj):
                        for j2, t in enumerate(tiles):
                            c0 = j2 * 256
                            last = t["bias"] is None
                            S.op("pe", f_mm(PP[pj][0:np_, 0, c0:c0 + 256], t["lA"], QA, start=True, stop=last),
                                 reads=t["kkeys"] + qkeys, writes=[("ps", 2 * pj)])
                            S.op("pe", f_mm(PP[pj][0:np_, 1, c0:c0 + 256], t["lB"], QB, start=True, stop=last),
                                 reads=t["kkeys"] + qkeys, writes=[("ps", 2 * pj + 1)])
                            if not last:
                                kt_b, sk_b = t["bias"]
                                for a_ in range(2):
                                    S.op("pe", f_mm(PP[pj][:, a_, c0:c0 + 256],
                                                    E64t[a_ * 64:(a_ + 1) * 64, kt_b * 128:(kt_b + 1) * 128],
                                                    sap(selT[sk_b], 128, a_ * 64, 64, 0, [[0, 2], [1, 128]]),
                                                    start=False, stop=True),
                                         reads=["c_E64", ("selT", sk_b)], writes=[("ps", 2 * pj + a_)])

                    def post(pj, pk):
                        S.op("act", f_act(sap(PT[pk], 1024, 0, np_, 0, [[512, 2], [1, w_]]),
                                          PP[pj][0:np_, :, 0:w_], AF.Exp, scale=0.125),
                             reads=[("ps", 2 * pj), ("ps", 2 * pj + 1)], writes=[("PT", pk)])
                        for j2, t in enumerate(tiles):
                            if t["mask"] is not None:
                                map_, mkey = t["mask"]
                                v_ = sap(PT[pk], 1024, 0, np_, j2 * 256, [[512, 2], [128, 2], [1, 128]])
                                S.op("dve", f_tt(v_, v_, map_, ALU.mult), reads=[("PT", pk), mkey], writes=[("PT", pk)])

                    def pv(pk):
                        for j2, t in enumerate(tiles):
                            for s_ in range(4):
                                S.op("pe", f_mm(acc_slot(t["br"], s_), PT[pk][0:np_, (s_ // 2) * 4 + j2 * 2 + (s_ % 2), :],
                                                t["vrhs"], start=(t["first"] and s_ == 0), stop=True, skip=True),
                                     reads=[("PT", pk)] + t["vkeys"], writes=[("acc", t["br"])])
                    return (scores, post, pv, after)

                def mask4(mt, np_):
                    return bass.AP(mt, 0, [[128, np_], [0, 2], [0, 2], [1, 128]])

                def gate_ap(br, i=i, g=g):
                    return sap(gates, NT * 48, 0, 128, i * 48 + g * 12 + BR[br], [[3, 2], [6, 2]])

                def y_ap(g=g, i=i):
                    return sap(ytiles[i % 2], D, 0, 128, g * 256, [[64, 2], [128, 2], [1, 64]])

                def mk_norm(br, i=i, g=g, sc=sc, sk=sk, y_ap=y_ap, gate_ap=gate_ap, part=0):
                    o = {"c": 0, "s": 12, "w": 24}[br]

                    def norm():
                        sm, rc, cf = sc[:, o:o + 4], sc[:, o + 4:o + 8], sc[:, o + 8:o + 12]
                        kk = ("scr", sk, br)
                        if part in (0, 1):
                            if br == "w":
                                S.op("dve", f_ts(sm, acc_sums(br), phant[:, i:i + 1], 1e-30, ALU.add, ALU.max),
                                     reads=[("acc", br), "c_phantom"], writes=[kk])
                            else:
                                S.op("dve", f_ts(sm, acc_sums(br), 1e-30, None, ALU.max), reads=[("acc", br)], writes=[kk])
                            S.op("dve", lambda h: h.reciprocal(out=rc, in_=sm), reads=[kk], writes=[kk])
                        if part == 1:
                            return
                        S.op("dve", f_tt(cf.rearrange("p (a b) -> p a b", a=2), rc.rearrange("p (a b) -> p a b", a=2),
                                         gate_ap(br), ALU.mult), reads=[kk, ("gates", i)], writes=[kk])
                        cfb = sap(sc, 128, 0, 128, o + 8, [[2, 2], [1, 2], [0, 64]])
                        if br == "c":
                            S.op("dve", f_tt(y_ap(), acc_data(br), cfb, ALU.mult), reads=[("acc", br), kk],
                                 writes=[("ytile", i % 2, g)])
                        else:
                            tk = tmpc[0] % 2
                            tmpc[0] += 1
                            tv = sap(tmpn[tk], 256, 0, 128, 0, [[128, 2], [64, 2], [1, 64]])
                            S.op("dve", f_tt(tv, acc_data(br), cfb, ALU.mult), reads=[("acc", br), kk],
                                 writes=[("tmpn", tk)])
                            S.op("pool", f_tt(y_ap(), y_ap(), tv, ALU.add), reads=[("tmpn", tk), ("ytile", i % 2, g)],
                                 writes=[("ytile", i % 2, g)])
                    return norm

                def mk_topk(i=i, g=g, sc=sc, sk=sk):
                    def topk():
                        j, w = ACCB["c"]
                        rc = sc[:, 4:8]
                        imp, adj, mx8 = sc[:, 40:72], sc[:, 72:104], sc[:, 104:112]
                        kk = ("scr", sk, "c")
                        kt_ = ("scr", sk, "t")
                        src4 = sap(PP[j // 2], 1024, 0, 128, (j % 2) * 512 + 65, [[w, 4], [1, 32]])
                        S.op("dve", f_tt(scr2[sk][:], src4, sap(sc, 128, 0, 128, 4, [[1, 4], [0, 32]]), ALU.mult),
                             reads=[("acc", "c"), kk], writes=[("scr2", sk)])
                        S.op("dve", lambda h: h.tensor_reduce(out=imp, in_=sap(scr2[sk], 128, 0, 128, 0, [[1, 32], [32, 4]]),
                                                              axis=mybir.AxisListType.X, op=ALU.add),
                             reads=[("scr2", sk)], writes=[kt_])
                        S.op("dve", f_tt(adj, imp, sbias[:, i, :], ALU.add), reads=[kt_, "c_selbias"], writes=[kt_])
                        S.op("dve", lambda h: h.max(out=mx8, in_=adj), reads=[kt_], writes=[kt_])
                        S.op("dve", f_ts(sap(selm[sk], 128, 0, 128, 0, [[64, 2], [1, 32]]),
                                         sap(sc, 128, 0, 128, 72, [[0, 2], [1, 32]]), mx8[:, 7:8], -30000.0, ALU.is_lt, ALU.mult),
                             reads=[kt_], writes=[("selm", sk)])
                        def tail(sk=sk):
                            trv = PP[3][:, 1, 0:64].bitcast(BF16)
                            S.op("pe", f_tr(trv, selm[sk][:], ident[:]), reads=[("selm", sk), "ident"], writes=["ps7"])
                            S.op("dve", f_copy(selT[sk][:], trv), reads=["ps7"], writes=[("selT", sk)])
                        deferred.append(("selT", i * 4 + g, tail))
                    return topk

                cm_ap = bass.AP(cmpm, i * 128, [[T, NCMP], [0, 2], [0, 2], [1, 128]])
                nc1_, nc_, tk_ = mk_norm("c", part=1), mk_norm("c", part=2), mk_topk()
                partF, partS = [], []
                partF.append(mk_step([dict(lA=KcT[0:64, gslot, 0:NCMP], lB=KcT[64:128, gslot, 0:NCMP], kkeys=["KcT"],
                                           np=NCMP, bias=None, mask=(cm_ap, "c_cmpmask"), br="c", vrhs=Vc[0:NCMP, g, :],
                                           vkeys=["Vc", "Vc2"], first=True)],
                                     (lambda nc1_=nc1_, nc_=nc_, tk_=tk_: (nc1_(), tk_(), nc_()))))
                wkts = [kt for kt in range(i - 4, i + 1) if kt >= 0]
                skts = [i] + list(range(i))
                for br_, kts, KT, vo in (("w", wkts, KwT, 4), ("s", skts, KsT, 0)):
                    tl_ = []
                    for n_, kt in enumerate(kts):
                        mk_, bs = None, None
                        if kt == i:
                            mk_ = (mask4(mcaus, 128), "c_mcaus")
                        elif br_ == "w" and kt == i - 4:
                            mk_ = (mask4(mwlow, 128), "c_mwlow")
                        elif br_ == "s":
                            bs = (kt, sk)
                        tl_.append(dict(lA=KT[0:64, g, kt * 128:(kt + 1) * 128], lB=KT[64:128, g, kt * 128:(kt + 1) * 128],
                                        kkeys=[("KsT" if br_ == "s" else "KwT", g, kt // 4)], np=128, bias=bs, mask=mk_, br=br_,
                                        vrhs=Vsw[:, kt, vo + g, :], vkeys=[("Vsw", kt), "Vsw1"], first=(n_ == 0)))
                    for p0 in range(0, len(tl_), 2):
                        lastp = p0 + 2 >= len(tl_)
                        (partF if br_ == "w" else partS).append(mk_step(tl_[p0:p0 + 2], mk_norm(br_) if lastp else None))
                if g == 3:
                    def fin(i=i):
                        yb = ybf[i % 2]
                        S.op("pool", f_copy(yb[:], ytiles[i % 2][:]), reads=[("ytile", i % 2, g_) for g_ in range(4)],
                             writes=[("ybf", i % 2)])

                        def tail(i=i, yb=yb):
                            trv = PP[3][:, 1, :].bitcast(BF16).rearrange("p (a b) -> p a b", a=8)
                            for kd in range(KD):
                                S.op("pe", f_tr(trv[:, kd, :], yb[:, kd * 128:(kd + 1) * 128], ident[:]),
                                     reads=[("ybf", i % 2), "ident"], writes=["ps7"])
                            S.op("dve", f_copy(yT[:, :, i * 128:(i + 1) * 128], trv), reads=["ps7"],
                                 writes=[("yT", i)])
                        deferred.append(("fin", None, tail))
                    prev = partS[-1]
                    pa = prev[3]
                    partS[-1] = (prev[0], prev[1], prev[2], (lambda pa=pa, fin=fin: ((pa() if pa else None), fin())))
                parts.append((partF, partS))

        idxS = {}
        idxF = {}
        for n_, (pF, pS) in enumerate(parts):
            idxF[n_] = len(steps)
            steps.extend(pF)
            if n_ >= 1:
                idxS[n_ - 1] = len(steps)
                steps.extend(parts[n_ - 1][1])
        idxS[len(parts) - 1] = len(steps)
        steps.extend(parts[-1][1])
        NPT = len(PT)
        DPV = 3
        n_st = len(steps)
        pending = []
        for j in range(n_st + DPV):
            for it_ in [p_ for p_ in pending if p_[0] <= j]:
                it_[1]()
                pending.remove(it_)
            if j < n_st:
                steps[j][0](j % 2)
            if 0 <= j - 1 < n_st:
                steps[j - 1][1]((j - 1) % 2, (j - 1) % NPT)
            k = j - DPV
            if 0 <= k < n_st:
                steps[k][2](k % NPT)
                if steps[k][3] is not None:
                    steps[k][3]()
                for kind, unit, fn in deferred:
                    due = j + 3
                    if kind == "selT":
                        due = idxS[unit]
                        if unit + 1 in idxF:
                            due = min(due, idxF[unit + 1] + DPV)
                    pending.append((max(due, j + 1), fn))
                del deferred[:]
        for it_ in pending:
            it_[1]()
        S.barrier()


def build(debug=(), stop_after=None):
    nc = bass.Bass("TRN2", target_bir_lowering=False)
    dh = {}
    for name, shape in IN_SPECS:
        dh[name] = nc.dram_tensor(name, shape, F32, kind="ExternalInput")
    for name, shape, dt in CONST_SPECS:
        dh["c_" + name] = nc.dram_tensor("c_" + name, shape, dt, kind="ExternalInput")
    OUT = nc.dram_tensor("out", [T, D], F32, kind="ExternalOutput")
    A = {k: v.ap() for k, v in dh.items()}
    dbg_outs = {}

    with contextlib.ExitStack() as st0:
        S = Sched(nc, st0)

        _nm = [0]

        def sb(st, name, shape, dt=F32):
            _nm[0] += 1
            return st.enter_context(nc.sbuf_tensor(f"{name}_{_nm[0]}", shape, dt))

        def dump(name, ap, shape, dt, keys):
            if name not in debug:
                return
            o = nc.dram_tensor("dbg_" + name, shape, dt, kind="ExternalOutput")
            dbg_outs[name] = o
            S.dma("sp", o.ap(), ap, reads=keys, writes=[("dbg", name)])

        def bcast_row(name):
            h = dh[name]
            return bass.AP(h, 0, [[0, 128], [1, h.shape[0]]])

        PP = [st0.enter_context(nc.psum_tensor(f"pp{i}", [128, 2, 512], F32)) for i in range(4)]

        def bank(j):
            return PP[j // 2][:, j % 2, :]

        def bkey(j):
            return ("ps", j)

        ident = sb(st0, "ident", [128, 128], BF16)
        S.dma("sp", ident[:], A["c_ident"], writes=["ident"])
        yT = sb(st0, "yT", [128, KD, T], BF16)
        eps_t = sb(st0, "eps_t", [128, 1], F32)
        S.op("dve", f_memset(eps_t[:], 1e-6), writes=["eps"])

        def norm_rows_T(st, src_rows, ntiles, gname, dstT, dkey, tag):
            xst = [sb(st, f"{tag}_xst{i}", [128, D], F32) for i in range(3)]
            gbc = sb(st, f"{tag}_gbc", [128, D], F32)
            xn = [sb(st, f"{tag}_xn{i}", [128, D], BF16) for i in range(2)]
            junk = sb(st, f"{tag}_junk", [128, D], BF16)
            stat = sb(st, f"{tag}_stat", [128, 4, ntiles], F32)
            S.dma("sp", gbc[:], bcast_row(gname), writes=[(tag, "gbc")])
            for tt in range(ntiles):
                xt = xst[tt % 3]
                xk = (tag, "xst", tt % 3)
                S.dma("sp", xt[:], src_rows(tt), writes=[xk])
                S.op("act", f_act(junk[:], xt[:], AF.Square, accum_out=stat[:, 0, tt:tt + 1]),
                     reads=[xk], writes=[(tag, "junk"), (tag, "st0", tt)])
                S.op("act", f_act(stat[:, 1, tt:tt + 1], stat[:, 0, tt:tt + 1], AF.Sqrt, bias=eps_t[:], scale=1.0 / D),
                     reads=[(tag, "st0", tt), "eps"], writes=[(tag, "st1", tt)])
                S.op("dve", lambda h, tt=tt: h.reciprocal(out=stat[:, 2, tt:tt + 1], in_=stat[:, 1, tt:tt + 1]),
                     reads=[(tag, "st1", tt)], writes=[(tag, "st2", tt)])
                xo = xn[tt % 2]
                S.op("dve", f_stt(xo[:], xt[:], stat[:, 2, tt:tt + 1], gbc[:], ALU.mult, ALU.mult),
                     reads=[xk, (tag, "st2", tt), (tag, "gbc")], writes=[(tag, "xn", tt % 2)])
                bj = tt % 2
                trv = bank(bj).bitcast(BF16).rearrange("p (a b) -> p a b", a=8)
                for kd in range(KD):
                    S.op("pe", f_tr(trv[:, kd, :], xo[:, kd * 128:(kd + 1) * 128], ident[:]),
                         reads=[(tag, "xn", tt % 2), "ident"], writes=[bkey(bj)])
                S.op("act", f_copy(dstT[:, :, tt * 128:(tt + 1) * 128], trv),
                     reads=[bkey(bj)], writes=[(dkey, tt)])

        with contextlib.ExitStack() as stU:
            uT = sb(stU, "uT", [128, KD, T], BF16)
            memT = sb(stU, "memT", [128, KD, 256], BF16)
            with contextlib.ExitStack() as st:
                norm_rows_T(st, lambda tt: A["x"][tt * 128:(tt + 1) * 128, :], NT, "g_mix", uT, "uT", "nx")
                norm_rows_T(st, lambda tt: A["mem"][tt * 128:(tt + 1) * 128, :], 2, "g_mem", memT, "memT", "nm")
                S.barrier()
            dump("uT", uT[:], [128, KD, T], BF16, [("uT", t) for t in range(NT)])
            WIN = A["w_in"].rearrange("(k p) c -> p k c", p=128)
            bank_rr = [0]

            def nb():
                j = bank_rr[0] % 8
                bank_rr[0] += 1
                return j

            def sap(t, pstride, p0, np_, off, dims):
                return bass.AP(t, p0 * pstride + off, [[pstride, np_]] + dims)

            def ukeys(tb):
                return [("uT", t) for t in range(tb * 4, tb * 4 + 4)]

            def tblk(tb):
                return slice(tb * 512, (tb + 1) * 512)

            def proj_fm(bj, wl, tb, wkeys, m=128):
                for kd in range(KD):
                    S.op("pe", f_mm(bank(bj)[0:m, :], wl(kd), uT[:, kd, tblk(tb)], start=(kd == 0), stop=(kd == KD - 1)),
                         reads=wkeys + ukeys(tb), writes=[bkey(bj)])

            def rope_tabs(st):
                rC = sb(st, "ropeC", [128, T], F32)
                rS = sb(st, "ropeS", [128, T], F32)
                S.dma("sp", rC[:], A["c_ropeC"], writes=["ropeC"])
                S.dma("sp", rS[:], A["c_ropeS"], writes=["ropeS"])
                return rC, rS

            def make_rope(st, tag, RTt):
                zb = [sb(st, f"{tag}_zb{i}", [128, 512], BF16) for i in range(2)]
                t1 = [sb(st, f"{tag}_t1{i}", [128, 512], F32) for i in range(2)]
                t2 = [sb(st, f"{tag}_t2{i}", [128, 512], F32) for i in range(2)]
                cnt = [0]

                def rope(bj, n, dst, Cap, Sap, dkeys, a3=None, ck="ropeC", sk="ropeS"):
                    k = cnt[0] % 2
                    cnt[0] += 1
                    zk, t1k, t2k = (tag, "zb", k), (tag, "t1", k), (tag, "t2", k)
                    ps = bank(bj)[:, 0:n]
                    S.op("act", f_copy(zb[k][:, 0:n], ps), reads=[bkey(bj)], writes=[zk])
                    br = nb()
                    S.op("pe", f_mm(bank(br)[:, 0:n], RTt[:], zb[k][:, 0:n]), reads=["RT", zk], writes=[bkey(br)])

                    def v(ap):
                        return ap if a3 is None else ap.rearrange("p (a b) -> p a b", a=a3)
                    S.op("dve", f_tt(v(t1[k][:, 0:n]), v(zb[k][:, 0:n]), Cap, ALU.mult), reads=[zk, ck], writes=[t1k])
                    S.op("dve", f_tt(v(t2[k][:, 0:n]), v(bank(br)[:, 0:n]), Sap, ALU.mult), reads=[bkey(br), sk], writes=[t2k])
                    S.op("dve", f_tt(dst, v(t1[k][:, 0:n]), v(t2[k][:, 0:n]), ALU.add), reads=[t1k, t2k], writes=dkeys)
                return rope

            GORD = [0, 2, 1, 3]
            STAGES = ["P0", "P1a", "P1b", "P1c", "P1d", "P1", "P2", "P3", "P4", "P5"]

            def upto(s_):
                return stop_after is None or STAGES.index(stop_after) >= STAGES.index(s_)
            with contextlib.ExitStack() as stATT:
                KsT = sb(stATT, "KsT", [128, 4, T], BF16)
                KwT = sb(stATT, "KwT", [128, 4, T], BF16)
                Vsw = sb(stATT, "Vsw", [128, NT, 8, 65], BF16)
                KcT = sb(stATT, "KcT", [128, 4, 128], BF16)
                Vc = sb(stATT, "Vc", [128, 4, 97], BF16)
                RTt = sb(stATT, "RTt", [128, 128], BF16)
                S.dma("sp", RTt[:], A["c_RT"], writes=["RT"])

                with contextlib.ExitStack() as st:
                    rC, rS = rope_tabs(st)
                    rope = make_rope(st, "r1", RTt)
                    rCc = sb(st, "rCc", [128, NCMP], F32)
                    rSc = sb(st, "rSc", [128, NCMP], F32)
                    S.dma("sp", rCc[:], A["c_ropeCc"], writes=["ropeCc"])
                    S.dma("sp", rSc[:], A["c_ropeSc"], writes=["ropeSc"])
                    wk = [sb(st, f"wk{i}", [128, KD, 128], BF16) for i in range(2)]
                    n_wk = 0
                    for (c0, dst, dk) in ((C_KS, KsT, "KsT"), (C_KW, KwT, "KwT")):
                        for g in range(4):
                            w = wk[n_wk % 2]
                            wkeys = [("wk", n_wk % 2, 0), ("wk", n_wk % 2, 1)]
                            n_wk += 1
                            src = WIN[:, :, c0 + g * 64:c0 + (g + 1) * 64]
                            S.dma("pool", w[:, :, 0:64], src, writes=[wkeys[0]])
                            S.dma("pool", w[:, :, 64:128], src, writes=[wkeys[1]])
                            for tb in range(4):
                                bj = nb()
                                proj_fm(bj, lambda kd, w=w: w[:, kd, :], tb, wkeys)
                                if "norope" in debug:
                                    S.op("act", f_copy(dst[:, g, tblk(tb)], bank(bj)), reads=[bkey(bj)], writes=[(dk, g, tb)])
                                else:
                                    rope(bj, 512, dst[:, g, tblk(tb)], rC[:, tblk(tb)], rS[:, tblk(tb)], [(dk, g, tb)])
                    wv = sb(st, "wv", [128, KD, 512], BF16)
                    if upto("P1b"):
                        S.dma("pool", wv[:, :, 0:256], WIN[:, :, C_VS:C_VS + 256], writes=[("wv", 0)])
                        S.dma("pool", wv[:, :, 256:512], WIN[:, :, C_VW:C_VW + 256], writes=[("wv", 1)])
                        S.op("pool", f_memset(Vsw[:, :, :, 64:65], 1.0), writes=["Vsw1"])
                        for tt in range(NT):
                            bj = nb()
                            for kd in range(KD):
                                S.op("pe", f_mm(bank(bj), uT[:, kd, tt * 128:(tt + 1) * 128], wv[:, kd, :],
                                                start=(kd == 0), stop=(kd == KD - 1)),
                                     reads=[("uT", tt), ("wv", 0), ("wv", 1)], writes=[bkey(bj)])
                            S.op("act", f_copy(Vsw[:, tt, :, 0:64], bank(bj).rearrange("p (a b) -> p a b", a=8)),
                                 reads=[bkey(bj)], writes=[("Vsw", tt)])
                    w1d = sb(st, "w1d", [128, 32, 256], BF16)
                    kcrT = sb(st, "kcrT", [128, 2, T], BF16)
                    wkc = sb(st, "wkc", [128, KD, 256], BF16)
                    posT = sb(st, "posT", [64, 32], BF16)
                    hidT = sb(st, "hidT", [128, 2, 4, NCMP], BF16)
                    biasK = sb(st, "biasK", [128, 2], F32)
                    w2d = sb(st, "w2d", [128, 2, 128], BF16)
                    for which in (("k", "v") if upto("P1d") else (("k",) if upto("P1c") else ())):
                        c0 = C_KC if which == "k" else C_VC
                        S.dma("pool", wkc[:], WIN[:, :, c0:c0 + 256], writes=["wkc"])
                        w1 = A["w_cmp_%s1" % which].rearrange("(l d) j -> d l j", d=64)
                        S.dma("pool", w1d[0:64], w1, writes=[("w1d", 0)])
                        S.dma("pool", w1d[64:128], w1, writes=[("w1d", 1)])
                        S.dma("pool", posT[:], A["cmp_pos_" + which].rearrange("l d -> d l"), writes=["posT"],
                              allow_slow_non_contiguous=True)
                        w2 = A["w_cmp_%s2" % which].rearrange("(jh p) d -> p jh d", p=128)
                        S.dma("pool", w2d[:, :, 0:64], w2, writes=[("w2d", 0)])
                        if which == "k":
                            S.dma("pool", w2d[:, :, 64:128], w2, writes=[("w2d", 1)])
                        for pair in range(2):
                            for tb in range(4):
                                bj = nb()
                                proj_fm(bj, lambda kd, pair=pair: wkc[:, kd, pair * 128:(pair + 1) * 128], tb, ["wkc"])
                                S.op("act", f_copy(kcrT[:, pair, tblk(tb)], bank(bj)), reads=[bkey(bj)],
                                     writes=[("kcrT", pair, tb)])
                        kcr_keys = [("kcrT", p_, tb) for p_ in range(2) for tb in range(4)]
                        bb = nb()
                        for jh in range(2):
                            for l in range(32):
                                S.op("pe", f_mm(bank(bb)[:, jh:jh + 1], w1d[0:64, l, jh * 128:(jh + 1) * 128],
                                                posT[0:64, l:l + 1], start=(l == 0), stop=(l == 31)),
                                     reads=[("w1d", 0), "posT"], writes=[bkey(bb)])
                        S.op("act", f_copy(biasK[:], bank(bb)[:, 0:2]), reads=[bkey(bb)], writes=["biasK"])
                        for jh in range(2):
                            bA, bB = nb(), nb()
                            for l in range(32):
                                rA = sap(kcrT, 2 * T, 0, 64, l, [[T, 2], [16, NCMP]])
                                rB = sap(kcrT, 2 * T, 64, 64, l, [[T, 2], [16, NCMP]])
                                S.op("pe", f_mm(bank(bA)[:, 0:254], w1d[0:64, l, jh * 128:(jh + 1) * 128], rA,
                                                start=(l == 0), stop=(l == 31)),
                                     reads=kcr_keys + [("w1d", 0)], writes=[bkey(bA)])
                                S.op("pe", f_mm(bank(bB)[:, 0:254], w1d[64:128, l, jh * 128:(jh + 1) * 128], rB,
                                                start=(l == 0), stop=(l == 31)),
                                     reads=kcr_keys + [("w1d", 1)], writes=[bkey(bB)])
                            for (bx, sl) in ((bA, 0), (bB, 1)):
                                S.op("act", f_act(hidT[:, jh, 2 * sl:2 * sl + 2, :],
                                                  bank(bx)[:, 0:254].rearrange("p (a b) -> p a b", a=2),
                                                  AF.Gelu_apprx_tanh, bias=biasK[:, jh:jh + 1]),
                                     reads=[bkey(bx), "biasK"], writes=[("hidT", jh, sl)])
                        hkeys = [("hidT", jh, sl) for jh in range(2) for sl in range(2)]
                        if which == "k":
                            bj = nb()
                            for jh in range(2):
                                S.op("pe", f_mm(bank(bj)[:, 0:508], w2d[:, jh, :], hidT[:, jh, :, :],
                                                start=(jh == 0), stop=(jh == 1)),
                                     reads=hkeys + [("w2d", 0), ("w2d", 1)], writes=[bkey(bj)])
                            rope(bj, 508, KcT[:, :, 0:NCMP],
                                 bass.AP(rCc, 0, [[NCMP, 128], [0, 4], [1, NCMP]]),
                                 bass.AP(rSc, 0, [[NCMP, 128], [0, 4], [1, NCMP]]), ["KcT"], a3=4,
                                 ck="ropeCc", sk="ropeSc")
                        else:
                            bj = nb()
                            first = True
                            for slot in range(4):
                                g = GORD[slot]
                                for jh in range(2):
                                    S.op("pe", f_mm(bank(bj)[0:NCMP, g * 64:(g + 1) * 64], hidT[:, jh, slot, :],
                                                    w2d[:, jh, 0:64], start=first, stop=True, skip=True),
                                         reads=hkeys + [("w2d", 0)], writes=[bkey(bj)])
                                    first = False
                            S.op("act", f_copy(Vc[0:NCMP, :, 0:64],
                                               bank(bj)[0:NCMP, 0:256].rearrange("p (a b) -> p a b", a=4)),
                                 reads=[bkey(bj)], writes=["Vc"])
                            S.dma("sp", Vc[:, :, 64:97], bass.AP(dh["c_c2s33"], 0, [[33, 128], [0, 4], [1, 33]]),
                                  writes=["Vc2"])
                    S.barrier()
                dump("KsT", KsT[:], [128, 4, T], BF16, [])
                dump("KwT", KwT[:], [128, 4, T], BF16, [])
                dump("Vsw", Vsw[:], [128, NT, 8, 65], BF16, [])
                dump("KcT", KcT[:], [128, 4, 128], BF16, [])
                dump("Vc", Vc[:], [128, 4, 97], BF16, [])

                with contextlib.ExitStack() as stQ:
                    if upto("P2"):
                        QT = sb(stQ, "QT", [128, 8, T], BF16)
                        gates = sb(stQ, "gates", [128, NT, 48], F32)
                        with contextlib.ExitStack() as st:
                            rC, rS = rope_tabs(st)
                            rope = make_rope(st, "r2", RTt)
                            wq = [sb(st, f"wq{i}", [128, KD, 512], BF16) for i in range(2)]
                            wg = sb(st, "wg", [128, KD, 48], BF16)
                            for hq in range(2):
                                S.dma("pool", wq[hq][:], WIN[:, :, hq * 512:(hq + 1) * 512], writes=[("wq", hq)])
                            S.dma("pool", wg[:], WIN[:, :, C_NG:C_NG + 48], writes=["wg"])
                            for pair in range(8):
                                w = wq[pair // 4]
                                for tb in range(4):
                                    bj = nb()
                                    proj_fm(bj, lambda kd, w=w, pair=pair: w[:, kd, (pair % 4) * 128:(pair % 4 + 1) * 128],
                                            tb, [("wq", pair // 4)])
                                    rope(bj, 512, QT[:, pair, tblk(tb)], rC[:, tblk(tb)], rS[:, tblk(tb)],
                                         [("QT", pair, tb)])
                            for tt in range(NT):
                                bj = nb()
                                for kd in range(KD):
                                    S.op("pe", f_mm(bank(bj)[:, 0:48], uT[:, kd, tt * 128:(tt + 1) * 128], wg[:, kd, :],
                                                    start=(kd == 0), stop=(kd == KD - 1)),
                                         reads=[("uT", tt), "wg"], writes=[bkey(bj)])
                                S.op("act", f_act(gates[:, tt, :], bank(bj)[:, 0:48], AF.Sigmoid), reads=[bkey(bj)],
                                     writes=[("gates", tt)])
                            S.barrier()
                        dump("QT", QT[:], [128, 8, T], BF16, [])
                        dump("gates", gates[:], [128, NT, 48], F32, [])
                        if upto("P3"):
                            attention(stQ, S, nc, sb, sap, PP, A, dh, ident, QT, gates, KsT, KwT, Vsw, KcT, Vc, yT, GORD)
                    S.barrier()
                S.barrier()
            dump("yT", yT[:], [128, KD, T], BF16, [])

            if upto("P4"):
              with contextlib.ExitStack() as stX:
                oxT = sb(stX, "oxT", [128, 2, T], BF16)
                ones1 = sb(stX, "ones1", [128, 1], F32)
                S.op("dve", f_memset(ones1[:], 1.0), writes=["ones1"])
                with contextlib.ExitStack() as st:
                    wkv = sb(st, "wkv", [128, KD, 512], BF16)
                    wqx = sb(st, "wqx", [128, KD, 256], BF16)
                    KxT = sb(st, "KxT", [128, 2, 256], BF16)
                    Vx = sb(st, "Vx", [128, 2, 4, 65], BF16)
                    QxT = sb(st, "QxT", [128, 2, T], BF16)
                    PTx = [sb(st, f"PTx{i}", [128, 8, 512], BF16) for i in range(2)]
                    oxtm = [sb(st, f"oxtm{i}", [128, 256], BF16) for i in range(2)]
                    sx = [sb(st, f"sx{i}", [128, 8], F32) for i in range(2)]
                    S.dma("pool", wkv[:], A["w_mem_kv"].rearrange("(k p) c -> p k c", p=128), writes=["wkv"])
                    S.dma("pool", wqx[:], WIN[:, :, C_QX:C_QX + 256], writes=["wqx"])
                    S.op("pool", f_memset(Vx[:, :, :, 64:65], 1.0), writes=["Vx1"])
                    mkeys = [("memT", 0), ("memT", 1)]
                    for pair in range(2):
                        bj = nb()
                        for kd in range(KD):
                            S.op("pe", f_mm(bank(bj)[:, 0:256], wkv[:, kd, pair * 128:(pair + 1) * 128], memT[:, kd, :],
                                            start=(kd == 0), stop=(kd == KD - 1)), reads=["wkv"] + mkeys, writes=[bkey(bj)])
                        S.op("act", f_copy(KxT[:, pair, :], bank(bj)[:, 0:256]), reads=[bkey(bj)], writes=[("KxT", pair)])
                    for mt in range(2):
                        bj = nb()
                        for kd in range(KD):
                            S.op("pe", f_mm(bank(bj)[:, 0:256], memT[:, kd, mt * 128:(mt + 1) * 128], wkv[:, kd, 256:512],
                                            start=(kd == 0), stop=(kd == KD - 1)), reads=["wkv"] + mkeys, writes=[bkey(bj)])
                        S.op("act", f_copy(Vx[:, mt, :, 0:64], bank(bj)[:, 0:256].rearrange("p (a b) -> p a b", a=4)),
                             reads=[bkey(bj)], writes=[("Vx", mt)])
                    for pair in range(2):
                        for tb in range(4):
                            bj = nb()
                            proj_fm(bj, lambda kd, pair=pair: wqx[:, kd, pair * 128:(pair + 1) * 128], tb, ["wqx"])
                            S.op("act", f_copy(QxT[:, pair, tblk(tb)], bank(bj)), reads=[bkey(bj)], writes=[("QxT", pair, tb)])
                    for qb in range(4):
                        px = PTx[qb % 2]
                        for h_ in range(4):
                            pair, half = h_ // 2, h_ % 2
                            for mt in range(2):
                                bj = nb()
                                S.op("pe", f_mm(bank(bj), KxT[half * 64:(half + 1) * 64, pair, mt * 128:(mt + 1) * 128],
                                                QxT[half * 64:(half + 1) * 64, pair, tblk(qb)]),
                                     reads=[("KxT", pair), ("QxT", pair, qb)], writes=[bkey(bj)])
                                S.op("act", f_act(px[:, h_ * 2 + mt, :], bank(bj), AF.Exp, scale=0.125), reads=[bkey(bj)],
                                     writes=[("PTx", qb % 2, h_ * 2 + mt)])
                        pkeys = [("PTx", qb % 2, j_) for j_ in range(8)]
                        for qt in range(4):
                            tt = qb * 4 + qt
                            bj = nb()
                            first = True
                            for h_ in range(4):
                                for mt in range(2):
                                    S.op("pe", f_mm(bank(bj)[:, h_ * 65:(h_ + 1) * 65], px[:, h_ * 2 + mt, qt * 128:(qt + 1) * 128],
                                                    Vx[:, mt, h_, :], start=first, stop=True, skip=True),
                                         reads=pkeys + [("Vx", mt), "Vx1"], writes=[bkey(bj)])
                                    first = False
                            k2 = tt % 2
                            sums = sap(PP[bj // 2], 1024, 0, 128, (bj % 2) * 512 + 64, [[65, 4]])
                            S.op("dve", lambda h, k2=k2, sums=sums: h.reciprocal(out=sx[k2][:, 0:4], in_=sums),
                                 reads=[bkey(bj)], writes=[("sx", k2)])
                            dat = sap(PP[bj // 2], 1024, 0, 128, (bj % 2) * 512, [[65, 4], [1, 64]])
                            rcb = sap(sx[k2], 8, 0, 128, 0, [[1, 4], [0, 64]])
                            S.op("dve", f_tt(oxtm[k2][:].rearrange("p (a b) -> p a b", a=4), dat, rcb, ALU.mult),
                                 reads=[bkey(bj), ("sx", k2)], writes=[("oxtm", k2)])
                            bt = nb()
                            trv = bank(bt).bitcast(BF16)[:, 0:256].rearrange("p (a b) -> p a b", a=2)
                            for kc in range(2):
                                S.op("pe", f_tr(trv[:, kc, :], oxtm[k2][:, kc * 128:(kc + 1) * 128], ident[:]),
                                     reads=[("oxtm", k2), "ident"], writes=[bkey(bt)])
                            S.op("dve", f_copy(oxT[:, :, tt * 128:(tt + 1) * 128], trv), reads=[bkey(bt)], writes=[("oxT", tt)])
                    S.barrier()
                dump("oxT", oxT[:], [128, 2, T], BF16, [])

                with contextlib.ExitStack() as st:
                    wbd = [sb(st, f"wbd{i}", [128, 8, 128], BF16) for i in range(2)]
                    wxo = sb(st, "wxo", [128, 2, D], BF16)
                    par = sb(st, "par", [128, 12, 8], F32)
                    S.dma("pool", wxo[:], A["w_xo"].rearrange("(kc p) d -> p kc d", p=128), writes=["wxo"])
                    for i_, nm in enumerate(("w_rg_a", "w_rg_i")):
                        S.op("pool", f_memset(wbd[i_][:], 0.0), writes=[("wbd", i_)])
                        for n_ in range(16):
                            c_, hf = n_ // 2, n_ % 2
                            S.dma("pool", wbd[i_][hf * 64:(hf + 1) * 64, c_, hf * 64:(hf + 1) * 64], A[nm][n_],
                                  reads=[], writes=[("wbd", i_)] if n_ == 0 else [("wbd", i_, n_)])
                    wbd_keys = [[("wbd", i_)] + [("wbd", i_, n_) for n_ in range(1, 16)] for i_ in range(2)]
                    S.dma("sp", par[:, 0:4, :], A["conv_w"].rearrange("k (c p) -> p k c", p=128), writes=[("par", 0)],
                          allow_slow_non_contiguous=True)
                    for j_, nm in ((4, "conv_b"), (5, "b_rg_a"), (6, "b_rg_i"), (7, "rg_lambda")):
                        S.dma("sp", par[:, j_, :], A[nm].rearrange("(c p) -> p c", p=128), writes=[("par", j_)],
                              allow_slow_non_contiguous=True)
                    S.op("act", f_act(par[:, 10, :], par[:, 7, :], AF.Exp, scale=-1.0), reads=[("par", 7)], writes=[("par", 10)])
                    S.op("act", f_act(par[:, 11, :], par[:, 10, :], AF.Ln, bias=ones1[:], scale=1.0), reads=[("par", 10), "ones1"],
                         writes=[("par", 11)])
                    S.op("dve", f_ts(par[:, 8, :], par[:, 11, :], -8.0, None, ALU.mult), reads=[("par", 11)], writes=[("par", 8)])
                    S.op("dve", f_ts(par[:, 9, :], par[:, 11, :], -16.0, None, ALU.mult), reads=[("par", 11)], writes=[("par", 9)])
                    pk_all = [("par", j_) for j_ in range(12)]
                    wr = [sb(st, f"wr{i}", [128, KD, 5, 128], BF16) for i in range(2)]
                    xr = sb(st, "xr", [128, T + 3], F32)
                    xc = sb(st, "xc", [128, T], F32)
                    xcb = sb(st, "xcb", [128, T], BF16)
                    Rr = sb(st, "Rr", [128, T], F32)
                    Aa = sb(st, "Aa", [128, T], F32)
                    GI = sb(st, "GI", [128, T], F32)
                    GL = sb(st, "GL", [128, T], F32)
                    GM = [sb(st, f"GM{i}", [128, T], F32) for i in range(3)]
                    ACC = sb(st, "ACC", [128, T], F32)
                    S.op("pool", f_memset(xr[:, 0:3], 0.0), writes=["xr0"])
                    ngm = [0]

                    def K4(nm):
                        return [(nm, tb) for tb in range(4)]
                    def load_wr(c):
                        cols = [C_XR + c * 128, C_GR + c * 128, C_MG + c * 128, C_MG + D + c * 128, C_MG + 2 * D + c * 128]
                        for j_ in range(5):
                            S.dma("pool", wr[c % 2][:, :, j_, :], WIN[:, :, cols[j_]:cols[j_] + 128], writes=[("wr", c % 2, j_)])
                    load_wr(0)
                    for c in range(8):
                        w = wr[c % 2]
                        wkeys = [("wr", c % 2, j_) for j_ in range(5)]
                        if c + 1 < 8:
                            load_wr(c + 1)

                        def proj_c(j_, tb, w=w, wkeys=wkeys):
                            bj = nb()
                            proj_fm(bj, lambda kd: w[:, kd, j_, :], tb, [wkeys[j_]])
                            return bj
                        ykeys = [("yT", t_) for t_ in range(NT)]
                        for tb in range(4):
                            bj = proj_c(0, tb)
                            S.op("act", f_copy(xr[:, 3 + tb * 512:3 + (tb + 1) * 512], bank(bj)), reads=[bkey(bj)],
                                 writes=[("xr", tb)])
                        S.op("act", f_act(xc[:], xr[:, 3:3 + T], AF.Identity, bias=par[:, 4, c:c + 1], scale=par[:, 3, c:c + 1]),
                             reads=K4("xr") + pk_all, writes=K4("xc"))
                        for k_ in range(3):
                            S.op("dve", f_stt(xc[:], xr[:, k_:k_ + T], par[:, k_, c:c + 1], xc[:], ALU.mult, ALU.add),
                                 reads=K4("xr") + ["xr0"] + K4("xc") + pk_all, writes=K4("xc"))
                        S.op("act", f_copy(xcb[:], xc[:]), reads=K4("xc"), writes=K4("xcb"))
                        for tb in range(4):
                            bj = proj_c(1, tb)
                            S.op("act", f_act(GL[:, tblk(tb)], bank(bj), AF.Gelu_apprx_tanh), reads=[bkey(bj)], writes=[("GL", tb)])
                        for b_ in range(3):
                            for tb in range(4):
                                bj = proj_c(2 + b_, tb)
                                S.op("act", f_act(GM[b_][:, tblk(tb)], bank(bj), AF.Sigmoid), reads=[bkey(bj)],
                                     writes=[("GM%d" % b_, tb)])
                        S.op("pool", f_tt(ACC[:], GM[0][:], yT[:, c, :], ALU.mult), reads=K4("GM0") + ykeys, writes=K4("ACC"))
                        for tb in range(4):
                            bj = nb()
                            for kc in range(2):
                                S.op("pe", f_mm(bank(bj), wxo[:, kc, c * 128:(c + 1) * 128], oxT[:, kc, tblk(tb)],
                                                start=(kc == 0), stop=(kc == 1)),
                                     reads=["wxo"] + [("oxT", t_) for t_ in range(tb * 4, tb * 4 + 4)], writes=[bkey(bj)])
                            S.op("dve", f_tt(GM[2][:, tblk(tb)], GM[2][:, tblk(tb)], bank(bj), ALU.mult),
                                 reads=[bkey(bj), ("GM2", tb)], writes=[("GM2", tb)])
                        for (i_, dstt, bcol) in ((0, Rr, 5), (1, GI, 6)):
                            for tb in range(4):
                                bj = nb()
                                S.op("pe", f_mm(bank(bj), wbd[i_][:, c, :], xcb[:, tblk(tb)]), reads=wbd_keys[i_] + K4("xcb"),
                                     writes=[bkey(bj)])
                                S.op("act", f_act(dstt[:, tblk(tb)], bank(bj), AF.Sigmoid, bias=par[:, bcol, c:c + 1]),
                                     reads=[bkey(bj)] + pk_all, writes=[("Rr" if i_ == 0 else "GI", tb)])
                        S.op("act", f_act(Aa[:], Rr[:], AF.Exp, scale=par[:, 8, c:c + 1]), reads=K4("Rr") + pk_all, writes=K4("Aa"))
                        S.op("act", f_act(Rr[:], Rr[:], AF.Exp, scale=par[:, 9, c:c + 1]), reads=K4("Rr") + pk_all, writes=K4("Rr"))
                        S.op("act", f_act(Rr[:], Rr[:], AF.Sqrt, bias=ones1[:], scale=-1.0), reads=K4("Rr") + ["ones1"], writes=K4("Rr"))
                        S.op("dve", f_tt(GI[:], GI[:], xc[:], ALU.mult), reads=K4("GI") + K4("xc"), writes=K4("GI"))
                        S.op("dve", f_tt(GI[:], GI[:], Rr[:], ALU.mult), reads=K4("GI") + K4("Rr"), writes=K4("GI"))
                        S.op("dve", lambda h: h.tensor_tensor_scan(out=xc[:], data0=Aa[:], data1=GI[:], initial=0.0,
                                                                    op0=ALU.mult, op1=ALU.add),
                             reads=K4("Aa") + K4("GI") + K4("xc"), writes=K4("xc"))
                        S.op("dve", f_tt(GL[:], GL[:], xc[:], ALU.mult), reads=K4("GL") + K4("xc"), writes=K4("GL"))
                        S.op("dve", f_tt(GM[1][:], GM[1][:], GL[:], ALU.mult), reads=K4("GM1") + K4("GL"), writes=K4("GM1"))
                        S.op("pool", f_tt(ACC[:], ACC[:], GM[2][:], ALU.add), reads=K4("GM2") + K4("ACC"), writes=K4("ACC"))
                        S.op("dve", f_tt(yT[:, c, :], ACC[:], GM[1][:], ALU.add), reads=K4("GM1") + K4("ACC") + ykeys,
                             writes=ykeys)
                    S.barrier()
              S.barrier()
            dump("ymT", yT[:], [128, KD, T], BF16, [])

        if stop_after is None or stop_after == "P5":
            TBK = 1024
            NFH = 2
            NTL = TBK // 128
            with contextlib.ExitStack() as st:
                wo = sb(st, "wo", [128, KD, D], BF16)
                WO = A["w_o"].rearrange("(k p) d -> p k d", p=128)
                for hf in range(2):
                    S.dma("pool", wo[:, :, hf * 512:(hf + 1) * 512], WO[:, :, hf * 512:(hf + 1) * 512], writes=[("wo", hf)])
                gml = sb(st, "gml", [128, D], F32)
                gfn = sb(st, "gfn", [128, D], F32)
                S.dma("sp", gml[:], bcast_row("g_mlp"), writes=["gml"])
                S.dma("sp", gfn[:], bcast_row("g_final"), writes=["gfn"])
                h2 = sb(st, "h2", [128, NTL, D], F32)
                vT = sb(st, "vT", [128, KD, TBK], BF16)
                ffT = sb(st, "ffT", [128, 16, TBK], BF16)
                wb = [sb(st, f"wb{i}", [128, 4096], BF16) for i in range(5)]
                xs = [sb(st, f"xs{i}", [128, D], F32) for i in range(2)]
                xnb = [sb(st, f"xnb{i}", [128, D], BF16) for i in range(2)]
                rl = [sb(st, f"rl{i}", [128, 512], F32) for i in range(2)]
                junk5 = sb(st, "junk5", [128, D], BF16)
                st5 = sb(st, "st5", [128, 6, 16], F32)
                WUP = A["w_up"].rearrange("(k p) f -> p k f", p=128)
                WDN = A["w_down"].rearrange("(fc p) d -> p fc d", p=128)
                nwb = [0]
                nxs = [0]
                nrl = [0]

                def rms_stats(src, tl, col0, tag):
                    S.op("act", f_act(junk5[:], src, AF.Square, accum_out=st5[:, col0, tl:tl + 1]),
                         reads=[("h2", tl)], writes=["junk5", (tag, 0, tl)])
                    S.op("act", f_act(st5[:, col0 + 1, tl:tl + 1], st5[:, col0, tl:tl + 1], AF.Sqrt, bias=eps_t[:], scale=1.0 / D),
                         reads=[(tag, 0, tl), "eps"], writes=[(tag, 1, tl)])
                    S.op("dve", lambda h: h.reciprocal(out=st5[:, col0 + 2, tl:tl + 1], in_=st5[:, col0 + 1, tl:tl + 1]),
                         reads=[(tag, 1, tl)], writes=[(tag, 2, tl)])

                wsched = []
                for blk in range(T // TBK):
                    for fh in range(NFH):
                        for wbi in range(4):
                            wsched.append(("up", fh * 2048 + wbi * 512))
                        for wbi in range(4):
                            wsched.append(("dn", fh * 16 + wbi * 4))
                nld = [0]

                def load_next():
                    k_ = nld[0]
                    if k_ >= len(wsched):
                        return
                    nld[0] += 1
                    kind, off = wsched[k_]
                    if kind == "up":
                        S.dma("pool", wb[k_ % 5][:].rearrange("p (k f) -> p k f", k=KD), WUP[:, :, off:off + 512], writes=[("wb", k_ % 5)])
                    else:
                        S.dma("pool", wb[k_ % 5][:].rearrange("p (k f) -> p k f", k=4), WDN[:, off:off + 4, :], writes=[("wb", k_ % 5)])
                for _ in range(3):
                    load_next()

                for blk in range(T // TBK):
                    t0 = blk * TBK
                    for tl in range(NTL):
                        tt = blk * NTL + tl
                        xk = nxs[0] % 2
                        nxs[0] += 1
                        S.dma("sp", xs[xk][:], A["x"][tt * 128:(tt + 1) * 128, :], writes=[("xs", xk)])
                        for dh in range(2):
                            bj = nb()
                            for kd in range(KD):
                                S.op("pe", f_mm(bank(bj), yT[:, kd, tt * 128:(tt + 1) * 128], wo[:, kd, dh * 512:(dh + 1) * 512],
                                                start=(kd == 0), stop=(kd == KD - 1)),
                                     reads=[("yT", t_) for t_ in range(NT)] + [("wo", dh)], writes=[bkey(bj)])
                            S.op("dve", f_tt(h2[:, tl, dh * 512:(dh + 1) * 512], bank(bj), xs[xk][:, dh * 512:(dh + 1) * 512], ALU.add),
                                 reads=[bkey(bj), ("xs", xk)], writes=[("h2", tl)])
                        rms_stats(h2[:, tl, :], tl, 0, "sa")
                        nk = tl % 2
                        S.op("dve", f_stt(xnb[nk][:], h2[:, tl, :], st5[:, 2, tl:tl + 1], gml[:], ALU.mult, ALU.mult),
                             reads=[("h2", tl), ("sa", 2, tl), "gml"], writes=[("xnb", nk)])
                        bj = nb()
                        trv = bank(bj).bitcast(BF16).rearrange("p (a b) -> p a b", a=8)
                        for kd in range(KD):
                            S.op("pe", f_tr(trv[:, kd, :], xnb[nk][:, kd * 128:(kd + 1) * 128], ident[:]),
                                 reads=[("xnb", nk), "ident"], writes=[bkey(bj)])
                        S.op("act", f_copy(vT[:, :, tl * 128:(tl + 1) * 128], trv), reads=[bkey(bj)], writes=[("vT", tl)])
                    vkeys = [("vT", tl) for tl in range(NTL)]
                    for fh in range(NFH):
                        for wbi in range(4):
                            wk_ = nwb[0] % 5
                            nwb[0] += 1
                            load_next()
                            wt = wb[wk_][:].rearrange("p (k f) -> p k f", k=KD)
                            for fl in range(4):
                                fc = wbi * 4 + fl
                                for tb in range(TBK // 512):
                                    bj = nb()
                                    for kd in range(KD):
                                        S.op("pe", f_mm(bank(bj), wt[:, kd, fl * 128:(fl + 1) * 128], vT[:, kd, tblk(tb)],
                                                        start=(kd == 0), stop=(kd == KD - 1)),
                                             reads=[("wb", wk_)] + vkeys, writes=[bkey(bj)])
                                    rk = nrl[0] % 2
                                    nrl[0] += 1
                                    S.op("act", f_act(rl[rk][:], bank(bj), AF.Relu), reads=[bkey(bj)], writes=[("rl", rk)])
                                    S.op("dve", f_tt(ffT[:, fc, tblk(tb)], rl[rk][:], rl[rk][:], ALU.mult), reads=[("rl", rk)],
                                         writes=[("ffT", fc, tb)])
                        for wbi in range(4):
                            wk_ = nwb[0] % 5
                            nwb[0] += 1
                            load_next()
                            wt = wb[wk_][:].rearrange("p (k f) -> p k f", k=4)
                            for tl in range(NTL):
                                for dh in range(2):
                                    bj = nb()
                                    for fl in range(4):
                                        fc = wbi * 4 + fl
                                        S.op("pe", f_mm(bank(bj), ffT[:, fc, tl * 128:(tl + 1) * 128], wt[:, fl, dh * 512:(dh + 1) * 512],
                                                        start=(fl == 0), stop=(fl == 3)),
                                             reads=[("wb", wk_), ("ffT", fc, tl // 4)], writes=[bkey(bj)])
                                    S.op("dve", f_tt(h2[:, tl, dh * 512:(dh + 1) * 512], h2[:, tl, dh * 512:(dh + 1) * 512], bank(bj), ALU.add),
                                         reads=[bkey(bj), ("h2", tl)], writes=[("h2", tl)])
                    for tl in range(NTL):
                        tt = blk * NTL + tl
                        rms_stats(h2[:, tl, :], tl, 3, "sb")
                        xk = nxs[0] % 2
                        nxs[0] += 1
                        S.op("dve", f_stt(xs[xk][:], h2[:, tl, :], st5[:, 5, tl:tl + 1], gfn[:], ALU.mult, ALU.mult),
                             reads=[("h2", tl), ("sb", 2, tl), "gfn"], writes=[("xs", xk)])
                        S.dma("sp", OUT.ap()[tt * 128:(tt + 1) * 128, :], xs[xk][:], reads=[("xs", xk)], writes=[("out", tt)])
                S.barrier()

        S.wait_keys("sp", [("dbg", n) for n in dbg_outs] + [("out", t) for t in range(NT)])
        S.emit()
    return nc, dbg_outs


def kernel(**inputs):
    nc, _ = build()
    consts = _consts()
    shared = {}
    for name, shape in IN_SPECS:
        if name in ("x", "mem"):
            continue
        a = np.asarray(inputs[name], dtype=np.float32)
        if name != "g_final":
            a = a[0]
        shared[name] = np.ascontiguousarray(a.reshape(shape))
    for name, shape, dt in CONST_SPECS:
        shared["c_" + name] = consts[name]
    x = np.asarray(inputs["x"], dtype=np.float32)
    mem = np.asarray(inputs["mem"], dtype=np.float32)
    in_maps = []
    for b in range(8):
        m = dict(shared)
        m["x"] = np.ascontiguousarray(x[b])
        m["mem"] = np.ascontiguousarray(mem[b])
        in_maps.append(m)
    res = run_bass_kernel_spmd(nc, in_maps, core_ids=list(range(8)))
    return np.stack([np.asarray(res.results[b]["out"], dtype=np.float32) for b in range(8)], axis=0)
```

```python
import contextlib
import numpy as np
import ml_dtypes
import concourse.bass as bass
import concourse.mybir as mybir
from concourse.bass_utils import run_bass_kernel_spmd

F32 = mybir.dt.float32
BF16 = mybir.dt.bfloat16
AF = mybir.ActivationFunctionType
ALU = mybir.AluOpType
NPBF = ml_dtypes.bfloat16

T = 2048
D = 1024
NT = 16
KD = 8
NCMP = 127
C_Q, C_KC, C_VC, C_KS, C_VS, C_KW, C_VW, C_NG, C_XR, C_GR, C_QX, C_MG = (
    0, 1024, 1280, 1536, 1792, 2048, 2304, 2560, 2608, 3632, 4656, 4912)
D_IN = 7984
SLOT_HH = [0, 2, 1, 3]


class _Eng:
    def __init__(self, name):
        self.name = name
        self.sem = None
        self.count = 0
        self.seen = {}
        self.ops = []


class Sched:
    NS = 6

    def __init__(self, nc, stack):
        self.nc = nc
        self.E = {}
        for n in ("pe", "act", "dve", "pool", "sp"):
            e = _Eng(n)
            e.sem = stack.enter_context(nc.semaphore("sem_" + n))
            e.miles = set()
            self.E[n] = e
        self.dsem = {}
        self.dk = {}
        for q in ("sp", "pool", "act"):
            self.dsem[q] = [stack.enter_context(nc.semaphore(f"dq_{q}_{i}")) for i in range(self.NS)]
            self.dk[q] = 0
        self.last_w = {}
        self.readers = {}

    def _toks(self, reads, writes):
        toks = []
        for k in reads:
            t = self.last_w.get(k)
            if t is not None:
                toks.append(t)
        for k in writes:
            t = self.last_w.get(k)
            if t is not None:
                toks.append(t)
            r = self.readers.get(k)
            if r:
                toks.extend(r.values())
        return toks

    def _waits(self, e, toks):
        need = {}
        for t in toks:
            if t[0] == "c":
                if t[1] == e.name and e.name == "pe":
                    continue
                key, v = t[1], t[2]
            else:
                key, v = t[1].num, t[2]
            if e.seen.get(key, 0) >= v:
                continue
            if key not in need or need[key][2] < v:
                need[key] = t
        for key, t in need.items():
            e.ops.append(("wait", t))
            e.seen[key] = t[2]
            if t[0] == "c":
                self.E[t[1]].miles.add(t[2])

    def _update(self, tok, reads, writes):
        ok = (tok[0], tok[1] if tok[0] == "c" else tok[1].num)
        for k in reads:
            self.readers.setdefault(k, {})[ok] = tok
        for k in writes:
            self.last_w[k] = tok
            self.readers[k] = {}

    def op(self, eng, fn, reads=(), writes=()):
        e = self.E[eng]
        self._waits(e, self._toks(reads, writes))
        e.count += 1
        e.ops.append(("op", fn, e.count))
        self._update(("c", e.name, e.count), reads, writes)

    def dma(self, q, out, in_, reads=(), writes=(), **kw):
        e = self.E[q]
        k = self.dk[q]
        self.dk[q] += 1
        slot, rnd = k % self.NS, k // self.NS
        dsem = self.dsem[q][slot]
        toks = self._toks(reads, writes)
        if rnd > 0:
            toks.append(("d", dsem, 16 * rnd))
        self._waits(e, toks)
        e.ops.append(("dma", out, in_, kw, dsem))
        self._update(("d", dsem, 16 * (rnd + 1)), reads, writes)

    def wait_keys(self, eng, keys):
        self._waits(self.E[eng], [self.last_w[k] for k in keys if k in self.last_w])

    def barrier(self):
        toks = []
        for e in self.E.values():
            if e.count > 0:
                toks.append(("c", e.name, e.count))
        for q in self.dsem:
            k = self.dk[q]
            for slot in range(self.NS):
                n = (k - slot + self.NS - 1) // self.NS
                if n > 0:
                    toks.append(("d", self.dsem[q][slot], 16 * n))
        for e in self.E.values():
            self._waits(e, [t for t in toks if not (t[0] == "c" and t[1] == e.name)])

    def emit(self):
        nc, E = self.nc, self.E
        rank = {}
        for n, e in E.items():
            rank[n] = {idx: r + 1 for r, idx in enumerate(sorted(e.miles))}

        def run(e, h):
            for item in e.ops:
                if item[0] == "wait":
                    t = item[1]
                    if t[0] == "c":
                        h.wait_ge(E[t[1]].sem, rank[t[1]][t[2]])
                    else:
                        h.wait_ge(t[1], t[2])
                elif item[0] == "op":
                    ins = item[1](h)
                    if item[2] in e.miles:
                        ins.then_inc(e.sem, 1)
                else:
                    _, out, in_, kw, dsem = item
                    h.dma_start(out=out, in_=in_, **kw).then_inc(dsem, 16)

        with nc.Block() as block:
            @block.tensor
            def _(h):
                run(E["pe"], h)

            @block.scalar
            def _(h):
                run(E["act"], h)

            @block.vector
            def _(h):
                run(E["dve"], h)

            @block.gpsimd
            def _(h):
                run(E["pool"], h)

            @block.sync
            def _(h):
                run(E["sp"], h)


def f_mm(out, lhsT, rhs, start=True, stop=True, skip=False):
    return lambda h: h.matmul(out, lhsT=lhsT, rhs=rhs, start=start, stop=stop, skip_group_check=skip)


def f_tr(out, in_, ident):
    return lambda h: h.transpose(out=out, in_=in_, identity=ident)


def f_act(out, in_, func, bias=None, scale=None, accum_out=None):
    kw = {}
    if bias is not None:
        kw["bias"] = bias
    if scale is not None:
        kw["scale"] = scale
    if accum_out is not None:
        kw["accum_out"] = accum_out
    return lambda h: h.activation(out=out, in_=in_, func=func, **kw)


def f_copy(out, in_):
    return lambda h: (h.copy(out=out, in_=in_) if hasattr(h, "activation") else h.tensor_copy(out=out, in_=in_))


def f_tt(out, in0, in1, op):
    return lambda h: h.tensor_tensor(out=out, in0=in0, in1=in1, op=op)


def f_ts(out, in0, s1, s2, op0, op1=None):
    if op1 is None:
        return lambda h: h.tensor_scalar(out=out, in0=in0, scalar1=s1, scalar2=None, op0=op0)
    return lambda h: h.tensor_scalar(out=out, in0=in0, scalar1=s1, scalar2=s2, op0=op0, op1=op1)


def f_stt(out, in0, scalar, in1, op0, op1):
    return lambda h: h.scalar_tensor_tensor(out=out, in0=in0, scalar=scalar, in1=in1, op0=op0, op1=op1)


def f_memset(ap, c):
    return lambda h: h.memset(ap, c)


def _consts():
    c = {}
    c["ident"] = np.eye(128, dtype=np.float32).astype(NPBF)
    inv = (1.0 / (500000.0 ** (np.arange(0, 16, 2, dtype=np.float32) / 16.0))).astype(np.float32)

    def tabs(pos):
        ang = pos.astype(np.float32)[:, None] * inv[None, :]
        cs, sn = np.cos(ang).astype(np.float32), np.sin(ang).astype(np.float32)
        Ct = np.ones((128, len(pos)), np.float32)
        St = np.zeros((128, len(pos)), np.float32)
        for half in (0, 64):
            Ct[half:half + 8] = cs.T
            Ct[half + 8:half + 16] = cs.T
            St[half:half + 8] = -sn.T
            St[half + 8:half + 16] = sn.T
        return Ct, St
    c["ropeC"], c["ropeS"] = tabs(np.arange(T))
    c["ropeCc"], c["ropeSc"] = tabs(np.arange(NCMP) * 16 + 31)
    RT = np.zeros((128, 128), np.float32)
    for half in (0, 64):
        for j in range(8):
            RT[half + j + 8, half + j] = 1.0
            RT[half + j, half + j + 8] = 1.0
    c["RT"] = RT.astype(NPBF)
    kk = np.arange(128)[:, None]
    qq = np.arange(128)[None, :]
    c["mcaus"] = (kk <= qq).astype(np.float32).astype(NPBF)
    c["mwlow"] = (kk > qq).astype(np.float32).astype(NPBF)
    n = np.arange(128)[:, None]
    t = np.arange(T)[None, :]
    c["cmpmask"] = ((16 * n + 31 <= t) & (n < NCMP)).astype(np.float32).astype(NPBF)
    c0 = np.arange(NCMP)[:, None] * 16
    s0 = np.arange(32)[None, :] * 64
    ov = np.clip(np.minimum(c0 + 32, s0 + 64) - np.maximum(c0, s0), 0, None) / 32.0
    c2s = np.zeros((128, 33), np.float32)
    c2s[:NCMP, 0] = 1.0
    c2s[:NCMP, 1:] = ov
    c["c2s33"] = c2s.astype(NPBF)
    E = np.zeros((128, T), np.float32)
    E[np.arange(T) // 64, np.arange(T)] = 1.0
    E[64 + np.arange(T) // 64, np.arange(T)] = 1.0
    c["E64"] = E.astype(NPBF)
    keep = np.zeros((128, 16, 32), np.float32)
    bias = np.zeros((128, 16, 32), np.float32)
    for i in range(16):
        tt = 128 * i + np.arange(128)
        cur = (tt // 64)[:, None]
        blk = np.arange(32)[None, :]
        forced = (blk == 0) | (blk == cur) | (blk == cur - 1)
        future = blk * 64 > tt[:, None]
        keep[:, i, :] = (~forced & ~future).astype(np.float32)
        bias[:, i, :] = np.where(forced, 10.0, np.where(future, -10.0, 0.0))
    c["selkeep"] = keep
    c["selbias"] = bias
    ph = np.zeros((128, 16), np.float32)
    for i in range(16):
        ph[:, i] = np.maximum(0, 511 - (128 * i + np.arange(128)))
    c["phantom"] = ph
    return c


CONST_SPECS = [("ident", [128, 128], BF16), ("ropeC", [128, T], F32), ("ropeS", [128, T], F32),
               ("ropeCc", [128, NCMP], F32), ("ropeSc", [128, NCMP], F32), ("RT", [128, 128], BF16),
               ("mcaus", [128, 128], BF16), ("mwlow", [128, 128], BF16), ("cmpmask", [128, T], BF16),
               ("c2s33", [128, 33], BF16), ("E64", [128, T], BF16), ("selkeep", [128, 16, 32], F32),
               ("selbias", [128, 16, 32], F32), ("phantom", [128, 16], F32)]

IN_SPECS = [("x", [T, D]), ("mem", [256, D]), ("g_mix", [D]), ("w_in", [D, D_IN]),
            ("cmp_pos_k", [32, 64]), ("cmp_pos_v", [32, 64]), ("w_cmp_k1", [2048, 256]), ("w_cmp_k2", [256, 64]),
            ("w_cmp_v1", [2048, 256]), ("w_cmp_v2", [256, 64]), ("conv_w", [4, D]), ("conv_b", [D]),
            ("w_rg_a", [16, 64, 64]), ("b_rg_a", [D]), ("w_rg_i", [16, 64, 64]), ("b_rg_i", [D]),
            ("rg_lambda", [D]), ("g_mem", [D]), ("w_mem_kv", [D, 512]), ("w_xo", [256, D]),
            ("w_o", [D, D]), ("g_mlp", [D]), ("w_up", [D, 4096]), ("w_down", [4096, D]), ("g_final", [D])]


def attention(stQ, S, nc, sb, sap, PP, A, dh, ident, QT, gates, KsT, KwT, Vsw, KcT, Vc, yT, GORD):
    with contextlib.ExitStack() as st:
        cmpm = sb(st, "cmpm", [128, T], BF16)
        E64t = sb(st, "E64t", [128, T], BF16)
        keep = sb(st, "keep", [128, 16, 32], F32)
        sbias = sb(st, "sbias", [128, 16, 32], F32)
        phant = sb(st, "phant", [128, 16], F32)
        mcaus = sb(st, "mcaus", [128, 128], BF16)
        mwlow = sb(st, "mwlow", [128, 128], BF16)
        for t_, nm in ((cmpm, "cmpmask"), (E64t, "E64"), (keep, "selkeep"), (sbias, "selbias"), (phant, "phantom"),
                       (mcaus, "mcaus"), (mwlow, "mwlow")):
            S.dma("sp", t_[:], A["c_" + nm], writes=["c_" + nm])
        PT = [sb(st, f"PT{i}", [128, 8, 128], BF16) for i in range(6)]
        selm = [sb(st, f"selm{i}", [128, 128], BF16) for i in range(2)]
        selT = [sb(st, f"selT{i}", [128, 128], BF16) for i in range(2)]
        for i in range(2):
            S.op("pool", f_memset(selm[i][:], 0.0), writes=[("selm", i)])
        ytiles = [sb(st, f"ytile{i}", [128, D], F32) for i in range(2)]
        ybf = [sb(st, f"ybf{i}", [128, D], BF16) for i in range(2)]
        scr = [sb(st, f"scr{i}", [128, 128], F32) for i in range(2)]
        scr2 = [sb(st, f"scr2{i}", [128, 4, 32], F32) for i in range(2)]
        tmpn = [sb(st, f"tmpn{i}", [128, 256], F32) for i in range(2)]
        ACCB = {"c": (4, 97), "s": (5, 65), "w": (6, 65)}
        BR = {"c": 0, "s": 1, "w": 2}

        def acc_slot(br, s_):
            j, w = ACCB[br]
            return PP[j // 2][:, j % 2, s_ * w:(s_ + 1) * w]

        def acc_data(br):
            j, w = ACCB[br]
            return sap(PP[j // 2], 1024, 0, 128, (j % 2) * 512, [[2 * w, 2], [w, 2], [1, 64]])

        def acc_sums(br):
            j, w = ACCB[br]
            return sap(PP[j // 2], 1024, 0, 128, (j % 2) * 512 + 64, [[w, 4]])

        def mask_sb(mt, np_):
            return bass.AP(mt, 0, [[128, np_], [0, 4], [1, 128]])

        steps = []
        parts = []
        deferred = []
        nig = [0]
        tmpc = [0]
        mreg = [0]

        for i in range(NT):
            for g in range(4):
                ig = nig[0]
                nig[0] += 1
                sk = ig % 2
                sc = scr[sk]
                gslot = GORD.index(g)
                qkeys = [("QT", 2 * g, i // 4), ("QT", 2 * g + 1, i // 4)]
                QA = QT[0:64, 2 * g:2 * g + 2, i * 128:(i + 1) * 128]
                QB = QT[64:128, 2 * g:2 * g + 2, i * 128:(i + 1) * 128]

                def mk_step(tiles, after, QA=QA, QB=QB, qkeys=qkeys):
                    nt_ = len(tiles)
                    np_ = tiles[0]["np"]
                    w_ = nt_ * 256

                    def scores(pj):
                        for j2, t in enumerate(tiles):
                            c0 = j2 * 256
                            last = t["bias"] is None
                            S.op("pe", f_mm(PP[pj][0:np_, 0, c0:c0 + 256], t["lA"], QA, start=True, stop=last),
                                 reads=t["kkeys"] + qkeys, writes=[("ps", 2 * pj)])
                            S.op("pe", f_mm(PP[pj][0:np_, 1, c0:c0 + 256], t["lB"], QB, start=True, stop=last),
                                 reads=t["kkeys"] + qkeys, writes=[("ps", 2 * pj + 1)])
                            if not last:
                                kt_b, sk_b = t["bias"]
                                for a_ in range(2):
                                    S.op("pe", f_mm(PP[pj][:, a_, c0:c0 + 256],
                                                    E64t[a_ * 64:(a_ + 1) * 64, kt_b * 128:(kt_b + 1) * 128],
                                                    sap(selT[sk_b], 128, a_ * 64, 64, 0, [[0, 2], [1, 128]]),
                                                    start=False, stop=True),
                                         reads=["c_E64", ("selT", sk_b)], writes=[("ps", 2 * pj + a_)])

                    def post(pj, pk):
                        S.op("act", f_act(sap(PT[pk], 1024, 0, np_, 0, [[512, 2], [1, w_]]),
                                          PP[pj][0:np_, :, 0:w_], AF.Exp, scale=0.125),
                             reads=[("ps", 2 * pj), ("ps", 2 * pj + 1)], writes=[("PT", pk)])
                        for j2, t in enumerate(tiles):
                            if t["mask"] is not None:
                                map_, mkey = t["mask"]
                                v_ = sap(PT[pk], 1024, 0, np_, j2 * 256, [[512, 2], [128, 2], [1, 128]])
                                S.op("dve", f_tt(v_, v_, map_, ALU.mult), reads=[("PT", pk), mkey], writes=[("PT", pk)])

                    def pv(pk):
                        for j2, t in enumerate(tiles):
                            for s_ in range(4):
                                S.op("pe", f_mm(acc_slot(t["br"], s_), PT[pk][0:np_, (s_ // 2) * 4 + j2 * 2 + (s_ % 2), :],
                                                t["vrhs"], start=(t["first"] and s_ == 0), stop=True, skip=True),
                                     reads=[("PT", pk)] + t["vkeys"], writes=[("acc", t["br"])])
                    return (scores, post, pv, after)

                def mask4(mt, np_):
                    return bass.AP(mt, 0, [[128, np_], [0, 2], [0, 2], [1, 128]])

                def gate_ap(br, i=i, g=g):
                    return sap(gates, NT * 48, 0, 128, i * 48 + g * 12 + BR[br], [[3, 2], [6, 2]])

                def y_ap(g=g, i=i):
                    return sap(ytiles[i % 2], D, 0, 128, g * 256, [[64, 2], [128, 2], [1, 64]])

                def mk_norm(br, i=i, g=g, sc=sc, sk=sk, y_ap=y_ap, gate_ap=gate_ap, part=0):
                    o = {"c": 0, "s": 12, "w": 24}[br]

                    def norm():
                        sm, rc, cf = sc[:, o:o + 4], sc[:, o + 4:o + 8], sc[:, o + 8:o + 12]
                        kk = ("scr", sk, br)
                        if part in (0, 1):
                            if br == "w":
                                S.op("dve", f_ts(sm, acc_sums(br), phant[:, i:i + 1], 1e-30, ALU.add, ALU.max),
                                     reads=[("acc", br), "c_phantom"], writes=[kk])
                            else:
                                S.op("dve", f_ts(sm, acc_sums(br), 1e-30, None, ALU.max), reads=[("acc", br)], writes=[kk])
                            S.op("dve", lambda h: h.reciprocal(out=rc, in_=sm), reads=[kk], writes=[kk])
                        if part == 1:
                            return
                        S.op("dve", f_tt(cf.rearrange("p (a b) -> p a b", a=2), rc.rearrange("p (a b) -> p a b", a=2),
                                         gate_ap(br), ALU.mult), reads=[kk, ("gates", i)], writes=[kk])
                        cfb = sap(sc, 128, 0, 128, o + 8, [[2, 2], [1, 2], [0, 64]])
                        if br == "c":
                            S.op("dve", f_tt(y_ap(), acc_data(br), cfb, ALU.mult), reads=[("acc", br), kk],
                                 writes=[("ytile", i % 2, g)])
                        else:
                            tk = tmpc[0] % 2
                            tmpc[0] += 1
                            tv = sap(tmpn[tk], 256, 0, 128, 0, [[128, 2], [64, 2], [1, 64]])
                            S.op("dve", f_tt(tv, acc_data(br), cfb, ALU.mult), reads=[("acc", br), kk],
                                 writes=[("tmpn", tk)])
                            S.op("pool", f_tt(y_ap(), y_ap(), tv, ALU.add), reads=[("tmpn", tk), ("ytile", i % 2, g)],
                                 writes=[("ytile", i % 2, g)])
                    return norm

                def mk_topk(i=i, g=g, sc=sc, sk=sk):
                    def topk():
                        j, w = ACCB["c"]
                        rc = sc[:, 4:8]
                        imp, adj, mx8 = sc[:, 40:72], sc[:, 72:104], sc[:, 104:112]
                        kk = ("scr", sk, "c")
                        kt_ = ("scr", sk, "t")
                        src4 = sap(PP[j // 2], 1024, 0, 128, (j % 2) * 512 + 65, [[w, 4], [1, 32]])
                        S.op("dve", f_tt(scr2[sk][:], src4, sap(sc, 128, 0, 128, 4, [[1, 4], [0, 32]]), ALU.mult),
                             reads=[("acc", "c"), kk], writes=[("scr2", sk)])
                        S.op("dve", lambda h: h.tensor_reduce(out=imp, in_=sap(scr2[sk], 128, 0, 128, 0, [[1, 32], [32, 4]]),
                                                              axis=mybir.AxisListType.X, op=ALU.add),
                             reads=[("scr2", sk)], writes=[kt_])
                        S.op("dve", f_tt(adj, imp, sbias[:, i, :], ALU.add), reads=[kt_, "c_selbias"], writes=[kt_])
                        S.op("dve", lambda h: h.max(out=mx8, in_=adj), reads=[kt_], writes=[kt_])
                        S.op("dve", f_ts(sap(selm[sk], 128, 0, 128, 0, [[64, 2], [1, 32]]),
                                         sap(sc, 128, 0, 128, 72, [[0, 2], [1, 32]]), mx8[:, 7:8], -30000.0, ALU.is_lt, ALU.mult),
                             reads=[kt_], writes=[("selm", sk)])
                        def tail(sk=sk):
                            trv = PP[3][:, 1, 0:64].bitcast(BF16)
                            S.op("pe", f_tr(trv, selm[sk][:], ident[:]), reads=[("selm", sk), "ident"], writes=["ps7"])
                            S.op("dve", f_copy(selT[sk][:], trv), reads=["ps7"], writes=[("selT", sk)])
                        deferred.append(("selT", i * 4 + g, tail))
                    return topk

                cm_ap = bass.AP(cmpm, i * 128, [[T, NCMP], [0, 2], [0, 2], [1, 128]])
                nc1_, nc_, tk_ = mk_norm("c", part=1), mk_norm("c", part=2), mk_topk()
                partF, partS = [], []
                partF.append(mk_step([dict(lA=KcT[0:64, gslot, 0:NCMP], lB=KcT[64:128, gslot, 0:NCMP], kkeys=["KcT"],
                                           np=NCMP, bias=None, mask=(cm_ap, "c_cmpmask"), br="c", vrhs=Vc[0:NCMP, g, :],
                                           vkeys=["Vc", "Vc2"], first=True)],
                                     (lambda nc1_=nc1_, nc_=nc_, tk_=tk_: (nc1_(), tk_(), nc_()))))
                wkts = [kt for kt in range(i - 4, i + 1) if kt >= 0]
                skts = [i] + list(range(i))
                for br_, kts, KT, vo in (("w", wkts, KwT, 4), ("s", skts, KsT, 0)):
                    tl_ = []
                    for n_, kt in enumerate(kts):
                        mk_, bs = None, None
                        if kt == i:
                            mk_ = (mask4(mcaus, 128), "c_mcaus")
                        elif br_ == "w" and kt == i - 4:
                            mk_ = (mask4(mwlow, 128), "c_mwlow")
                        elif br_ == "s":
                            bs = (kt, sk)
                        tl_.append(dict(lA=KT[0:64, g, kt * 128:(kt + 1) * 128], lB=KT[64:128, g, kt * 128:(kt + 1) * 128],
                                        kkeys=[("KsT" if br_ == "s" else "KwT", g, kt // 4)], np=128, bias=bs, mask=mk_, br=br_,
                                        vrhs=Vsw[:, kt, vo + g, :], vkeys=[("Vsw", kt), "Vsw1"], first=(n_ == 0)))
                    for p0 in range(0, len(tl_), 2):
                        lastp = p0 + 2 >= len(tl_)
                        (partF if br_ == "w" else partS).append(mk_step(tl_[p0:p0 + 2], mk_norm(br_) if lastp else None))
                if g == 3:
                    def fin(i=i):
                        yb = ybf[i % 2]
                        S.op("pool", f_copy(yb[:], ytiles[i % 2][:]), reads=[("ytile", i % 2, g_) for g_ in range(4)],
                             writes=[("ybf", i % 2)])

                        def tail(i=i, yb=yb):
                            trv = PP[3][:, 1, :].bitcast(BF16).rearrange("p (a b) -> p a b", a=8)
                            for kd in range(KD):
                                S.op("pe", f_tr(trv[:, kd, :], yb[:, kd * 128:(kd + 1) * 128], ident[:]),
                                     reads=[("ybf", i % 2), "ident"], writes=["ps7"])
                            S.op("dve", f_copy(yT[:, :, i * 128:(i + 1) * 128], trv), reads=["ps7"],
                                 writes=[("yT", i)])
                        deferred.append(("fin", None, tail))
                    prev = partS[-1]
                    pa = prev[3]
                    partS[-1] = (prev[0], prev[1], prev[2], (lambda pa=pa, fin=fin: ((pa() if pa else None), fin())))
                parts.append((partF, partS))

        idxS = {}
        idxF = {}
        for n_, (pF, pS) in enumerate(parts):
            idxF[n_] = len(steps)
            steps.extend(pF)
            if n_ >= 1:
                idxS[n_ - 1] = len(steps)
                steps.extend(parts[n_ - 1][1])
        idxS[len(parts) - 1] = len(steps)
        steps.extend(parts[-1][1])
        NPT = len(PT)
        DPV = 3
        n_st = len(steps)
        pending = []
        for j in range(n_st + DPV):
            for it_ in [p_ for p_ in pending if p_[0] <= j]:
                it_[1]()
                pending.remove(it_)
            if j < n_st:
                steps[j][0](j % 2)
            if 0 <= j - 1 < n_st:
                steps[j - 1][1]((j - 1) % 2, (j - 1) % NPT)
            k = j - DPV
            if 0 <= k < n_st:
                steps[k][2](k % NPT)
                if steps[k][3] is not None:
                    steps[k][3]()
                for kind, unit, fn in deferred:
                    due = j + 3
                    if kind == "selT":
                        due = idxS[unit]
                        if unit + 1 in idxF:
                            due = min(due, idxF[unit + 1] + DPV)
                    pending.append((max(due, j + 1), fn))
                del deferred[:]
        for it_ in pending:
            it_[1]()
        S.barrier()


def build(debug=(), stop_after=None):
    nc = bass.Bass("TRN2", target_bir_lowering=False)
    dh = {}
    for name, shape in IN_SPECS:
        dh[name] = nc.dram_tensor(name, shape, F32, kind="ExternalInput")
    for name, shape, dt in CONST_SPECS:
        dh["c_" + name] = nc.dram_tensor("c_" + name, shape, dt, kind="ExternalInput")
    OUT = nc.dram_tensor("out", [T, D], F32, kind="ExternalOutput")
    A = {k: v.ap() for k, v in dh.items()}
    dbg_outs = {}

    with contextlib.ExitStack() as st0:
        S = Sched(nc, st0)

        _nm = [0]

        def sb(st, name, shape, dt=F32):
            _nm[0] += 1
            return st.enter_context(nc.sbuf_tensor(f"{name}_{_nm[0]}", shape, dt))

        def dump(name, ap, shape, dt, keys):
            if name not in debug:
                return
            o = nc.dram_tensor("dbg_" + name, shape, dt, kind="ExternalOutput")
            dbg_outs[name] = o
            S.dma("sp", o.ap(), ap, reads=keys, writes=[("dbg", name)])

        def bcast_row(name):
            h = dh[name]
            return bass.AP(h, 0, [[0, 128], [1, h.shape[0]]])

        PP = [st0.enter_context(nc.psum_tensor(f"pp{i}", [128, 2, 512], F32)) for i in range(4)]

        def bank(j):
            return PP[j // 2][:, j % 2, :]

        def bkey(j):
            return ("ps", j)

        ident = sb(st0, "ident", [128, 128], BF16)
        S.dma("sp", ident[:], A["c_ident"], writes=["ident"])
        yT = sb(st0, "yT", [128, KD, T], BF16)
        eps_t = sb(st0, "eps_t", [128, 1], F32)
        S.op("dve", f_memset(eps_t[:], 1e-6), writes=["eps"])

        def norm_rows_T(st, src_rows, ntiles, gname, dstT, dkey, tag):
            xst = [sb(st, f"{tag}_xst{i}", [128, D], F32) for i in range(3)]
            gbc = sb(st, f"{tag}_gbc", [128, D], F32)
            xn = [sb(st, f"{tag}_xn{i}", [128, D], BF16) for i in range(2)]
            junk = sb(st, f"{tag}_junk", [128, D], BF16)
            stat = sb(st, f"{tag}_stat", [128, 4, ntiles], F32)
            S.dma("sp", gbc[:], bcast_row(gname), writes=[(tag, "gbc")])
            for tt in range(ntiles):
                xt = xst[tt % 3]
                xk = (tag, "xst", tt % 3)
                S.dma("sp", xt[:], src_rows(tt), writes=[xk])
                S.op("act", f_act(junk[:], xt[:], AF.Square, accum_out=stat[:, 0, tt:tt + 1]),
                     reads=[xk], writes=[(tag, "junk"), (tag, "st0", tt)])
                S.op("act", f_act(stat[:, 1, tt:tt + 1], stat[:, 0, tt:tt + 1], AF.Sqrt, bias=eps_t[:], scale=1.0 / D),
                     reads=[(tag, "st0", tt), "eps"], writes=[(tag, "st1", tt)])
                S.op("dve", lambda h, tt=tt: h.reciprocal(out=stat[:, 2, tt:tt + 1], in_=stat[:, 1, tt:tt + 1]),
                     reads=[(tag, "st1", tt)], writes=[(tag, "st2", tt)])
                xo = xn[tt % 2]
                S.op("dve", f_stt(xo[:], xt[:], stat[:, 2, tt:tt + 1], gbc[:], ALU.mult, ALU.mult),
                     reads=[xk, (tag, "st2", tt), (tag, "gbc")], writes=[(tag, "xn", tt % 2)])
                bj = tt % 2
                trv = bank(bj).bitcast(BF16).rearrange("p (a b) -> p a b", a=8)
                for kd in range(KD):
                    S.op("pe", f_tr(trv[:, kd, :], xo[:, kd * 128:(kd + 1) * 128], ident[:]),
                         reads=[(tag, "xn", tt % 2), "ident"], writes=[bkey(bj)])
                S.op("act", f_copy(dstT[:, :, tt * 128:(tt + 1) * 128], trv),
                     reads=[bkey(bj)], writes=[(dkey, tt)])

        with contextlib.ExitStack() as stU:
            uT = sb(stU, "uT", [128, KD, T], BF16)
            memT = sb(stU, "memT", [128, KD, 256], BF16)
            with contextlib.ExitStack() as st:
                norm_rows_T(st, lambda tt: A["x"][tt * 128:(tt + 1) * 128, :], NT, "g_mix", uT, "uT", "nx")
                norm_rows_T(st, lambda tt: A["mem"][tt * 128:(tt + 1) * 128, :], 2, "g_mem", memT, "memT", "nm")
                S.barrier()
            dump("uT", uT[:], [128, KD, T], BF16, [("uT", t) for t in range(NT)])
            WIN = A["w_in"].rearrange("(k p) c -> p k c", p=128)
            bank_rr = [0]

            def nb():
                j = bank_rr[0] % 8
                bank_rr[0] += 1
                return j

            def sap(t, pstride, p0, np_, off, dims):
                return bass.AP(t, p0 * pstride + off, [[pstride, np_]] + dims)

            def ukeys(tb):
                return [("uT", t) for t in range(tb * 4, tb * 4 + 4)]

            def tblk(tb):
                return slice(tb * 512, (tb + 1) * 512)

            def proj_fm(bj, wl, tb, wkeys, m=128):
                for kd in range(KD):
                    S.op("pe", f_mm(bank(bj)[0:m, :], wl(kd), uT[:, kd, tblk(tb)], start=(kd == 0), stop=(kd == KD - 1)),
                         reads=wkeys + ukeys(tb), writes=[bkey(bj)])

            def rope_tabs(st):
                rC = sb(st, "ropeC", [128, T], F32)
                rS = sb(st, "ropeS", [128, T], F32)
                S.dma("sp", rC[:], A["c_ropeC"], writes=["ropeC"])
                S.dma("sp", rS[:], A["c_ropeS"], writes=["ropeS"])
                return rC, rS

            def make_rope(st, tag, RTt):
                zb = [sb(st, f"{tag}_zb{i}", [128, 512], BF16) for i in range(2)]
                t1 = [sb(st, f"{tag}_t1{i}", [128, 512], F32) for i in range(2)]
                t2 = [sb(st, f"{tag}_t2{i}", [128, 512], F32) for i in range(2)]
                cnt = [0]

                def rope(bj, n, dst, Cap, Sap, dkeys, a3=None, ck="ropeC", sk="ropeS"):
                    k = cnt[0] % 2
                    cnt[0] += 1
                    zk, t1k, t2k = (tag, "zb", k), (tag, "t1", k), (tag, "t2", k)
                    ps = bank(bj)[:, 0:n]
                    S.op("act", f_copy(zb[k][:, 0:n], ps), reads=[bkey(bj)], writes=[zk])
                    br = nb()
                    S.op("pe", f_mm(bank(br)[:, 0:n], RTt[:], zb[k][:, 0:n]), reads=["RT", zk], writes=[bkey(br)])

                    def v(ap):
                        return ap if a3 is None else ap.rearrange("p (a b) -> p a b", a=a3)
                    S.op("dve", f_tt(v(t1[k][:, 0:n]), v(zb[k][:, 0:n]), Cap, ALU.mult), reads=[zk, ck], writes=[t1k])
                    S.op("dve", f_tt(v(t2[k][:, 0:n]), v(bank(br)[:, 0:n]), Sap, ALU.mult), reads=[bkey(br), sk], writes=[t2k])
                    S.op("dve", f_tt(dst, v(t1[k][:, 0:n]), v(t2[k][:, 0:n]), ALU.add), reads=[t1k, t2k], writes=dkeys)
                return rope

            GORD = [0, 2, 1, 3]
            STAGES = ["P0", "P1a", "P1b", "P1c", "P1d", "P1", "P2", "P3", "P4", "P5"]

            def upto(s_):
                return stop_after is None or STAGES.index(stop_after) >= STAGES.index(s_)
            with contextlib.ExitStack() as stATT:
                KsT = sb(stATT, "KsT", [128, 4, T], BF16)
                KwT = sb(stATT, "KwT", [128, 4, T], BF16)
                Vsw = sb(stATT, "Vsw", [128, NT, 8, 65], BF16)
                KcT = sb(stATT, "KcT", [128, 4, 128], BF16)
                Vc = sb(stATT, "Vc", [128, 4, 97], BF16)
                RTt = sb(stATT, "RTt", [128, 128], BF16)
                S.dma("sp", RTt[:], A["c_RT"], writes=["RT"])

                with contextlib.ExitStack() as st:
                    rC, rS = rope_tabs(st)
                    rope = make_rope(st, "r1", RTt)
                    rCc = sb(st, "rCc", [128, NCMP], F32)
                    rSc = sb(st, "rSc", [128, NCMP], F32)
                    S.dma("sp", rCc[:], A["c_ropeCc"], writes=["ropeCc"])
                    S.dma("sp", rSc[:], A["c_ropeSc"], writes=["ropeSc"])
                    wk = [sb(st, f"wk{i}", [128, KD, 128], BF16) for i in range(2)]
                    n_wk = 0
                    for (c0, dst, dk) in ((C_KS, KsT, "KsT"), (C_KW, KwT, "KwT")):
                        for g in range(4):
                            w = wk[n_wk % 2]
                            wkeys = [("wk", n_wk % 2, 0), ("wk", n_wk % 2, 1)]
                            n_wk += 1
                            src = WIN[:, :, c0 + g * 64:c0 + (g + 1) * 64]
                            S.dma("pool", w[:, :, 0:64], src, writes=[wkeys[0]])
                            S.dma("pool", w[:, :, 64:128], src, writes=[wkeys[1]])
                            for tb in range(4):
                                bj = nb()
                                proj_fm(bj, lambda kd, w=w: w[:, kd, :], tb, wkeys)
                                if "norope" in debug:
                                    S.op("act", f_copy(dst[:, g, tblk(tb)], bank(bj)), reads=[bkey(bj)], writes=[(dk, g, tb)])
                                else:
                                    rope(bj, 512, dst[:, g, tblk(tb)], rC[:, tblk(tb)], rS[:, tblk(tb)], [(dk, g, tb)])
                    wv = sb(st, "wv", [128, KD, 512], BF16)
                    if upto("P1b"):
                        S.dma("pool", wv[:, :, 0:256], WIN[:, :, C_VS:C_VS + 256], writes=[("wv", 0)])
                        S.dma("pool", wv[:, :, 256:512], WIN[:, :, C_VW:C_VW + 256], writes=[("wv", 1)])
                        S.op("pool", f_memset(Vsw[:, :, :, 64:65], 1.0), writes=["Vsw1"])
                        for tt in range(NT):
                            bj = nb()
                            for kd in range(KD):
                                S.op("pe", f_mm(bank(bj), uT[:, kd, tt * 128:(tt + 1) * 128], wv[:, kd, :],
                                                start=(kd == 0), stop=(kd == KD - 1)),
                                     reads=[("uT", tt), ("wv", 0), ("wv", 1)], writes=[bkey(bj)])
                            S.op("act", f_copy(Vsw[:, tt, :, 0:64], bank(bj).rearrange("p (a b) -> p a b", a=8)),
                                 reads=[bkey(bj)], writes=[("Vsw", tt)])
                    w1d = sb(st, "w1d", [128, 32, 256], BF16)
                    kcrT = sb(st, "kcrT", [128, 2, T], BF16)
                    wkc = sb(st, "wkc", [128, KD, 256], BF16)
                    posT = sb(st, "posT", [64, 32], BF16)
                    hidT = sb(st, "hidT", [128, 2, 4, NCMP], BF16)
                    biasK = sb(st, "biasK", [128, 2], F32)
                    w2d = sb(st, "w2d", [128, 2, 128], BF16)
                    for which in (("k", "v") if upto("P1d") else (("k",) if upto("P1c") else ())):
                        c0 = C_KC if which == "k" else C_VC
                        S.dma("pool", wkc[:], WIN[:, :, c0:c0 + 256], writes=["wkc"])
                        w1 = A["w_cmp_%s1" % which].rearrange("(l d) j -> d l j", d=64)
                        S.dma("pool", w1d[0:64], w1, writes=[("w1d", 0)])
                        S.dma("pool", w1d[64:128], w1, writes=[("w1d", 1)])
                        S.dma("pool", posT[:], A["cmp_pos_" + which].rearrange("l d -> d l"), writes=["posT"],
                              allow_slow_non_contiguous=True)
                        w2 = A["w_cmp_%s2" % which].rearrange("(jh p) d -> p jh d", p=128)
                        S.dma("pool", w2d[:, :, 0:64], w2, writes=[("w2d", 0)])
                        if which == "k":
                            S.dma("pool", w2d[:, :, 64:128], w2, writes=[("w2d", 1)])
                        for pair in range(2):
                            for tb in range(4):
                                bj = nb()
                                proj_fm(bj, lambda kd, pair=pair: wkc[:, kd, pair * 128:(pair + 1) * 128], tb, ["wkc"])
                                S.op("act", f_copy(kcrT[:, pair, tblk(tb)], bank(bj)), reads=[bkey(bj)],
                                     writes=[("kcrT", pair, tb)])
                        kcr_keys = [("kcrT", p_, tb) for p_ in range(2) for tb in range(4)]
                        bb = nb()
                        for jh in range(2):
                            for l in range(32):
                                S.op("pe", f_mm(bank(bb)[:, jh:jh + 1], w1d[0:64, l, jh * 128:(jh + 1) * 128],
                                                posT[0:64, l:l + 1], start=(l == 0), stop=(l == 31)),
                                     reads=[("w1d", 0), "posT"], writes=[bkey(bb)])
                        S.op("act", f_copy(biasK[:], bank(bb)[:, 0:2]), reads=[bkey(bb)], writes=["biasK"])
                        for jh in range(2):
                            bA, bB = nb(), nb()
                            for l in range(32):
                                rA = sap(kcrT, 2 * T, 0, 64, l, [[T, 2], [16, NCMP]])
                                rB = sap(kcrT, 2 * T, 64, 64, l, [[T, 2], [16, NCMP]])
                                S.op("pe", f_mm(bank(bA)[:, 0:254], w1d[0:64, l, jh * 128:(jh + 1) * 128], rA,
                                                start=(l == 0), stop=(l == 31)),
                                     reads=kcr_keys + [("w1d", 0)], writes=[bkey(bA)])
                                S.op("pe", f_mm(bank(bB)[:, 0:254], w1d[64:128, l, jh * 128:(jh + 1) * 128], rB,
                                                start=(l == 0), stop=(l == 31)),
                                     reads=kcr_keys + [("w1d", 1)], writes=[bkey(bB)])
                            for (bx, sl) in ((bA, 0), (bB, 1)):
                                S.op("act", f_act(hidT[:, jh, 2 * sl:2 * sl + 2, :],
                                                  bank(bx)[:, 0:254].rearrange("p (a b) -> p a b", a=2),
                                                  AF.Gelu_apprx_tanh, bias=biasK[:, jh:jh + 1]),
                                     reads=[bkey(bx), "biasK"], writes=[("hidT", jh, sl)])
                        hkeys = [("hidT", jh, sl) for jh in range(2) for sl in range(2)]
                        if which == "k":
                            bj = nb()
                            for jh in range(2):
                                S.op("pe", f_mm(bank(bj)[:, 0:508], w2d[:, jh, :], hidT[:, jh, :, :],
                                                start=(jh == 0), stop=(jh == 1)),
                                     reads=hkeys + [("w2d", 0), ("w2d", 1)], writes=[bkey(bj)])
                            rope(bj, 508, KcT[:, :, 0:NCMP],
                                 bass.AP(rCc, 0, [[NCMP, 128], [0, 4], [1, NCMP]]),
                                 bass.AP(rSc, 0, [[NCMP, 128], [0, 4], [1, NCMP]]), ["KcT"], a3=4,
                                 ck="ropeCc", sk="ropeSc")
                        else:
                            bj = nb()
                            first = True
                            for slot in range(4):
                                g = GORD[slot]
                                for jh in range(2):
                                    S.op("pe", f_mm(bank(bj)[0:NCMP, g * 64:(g + 1) * 64], hidT[:, jh, slot, :],
                                                    w2d[:, jh, 0:64], start=first, stop=True, skip=True),
                                         reads=hkeys + [("w2d", 0)], writes=[bkey(bj)])
                                    first = False
                            S.op("act", f_copy(Vc[0:NCMP, :, 0:64],
                                               bank(bj)[0:NCMP, 0:256].rearrange("p (a b) -> p a b", a=4)),
                                 reads=[bkey(bj)], writes=["Vc"])
                            S.dma("sp", Vc[:, :, 64:97], bass.AP(dh["c_c2s33"], 0, [[33, 128], [0, 4], [1, 33]]),
                                  writes=["Vc2"])
                    S.barrier()
                dump("KsT", KsT[:], [128, 4, T], BF16, [])
                dump("KwT", KwT[:], [128, 4, T], BF16, [])
                dump("Vsw", Vsw[:], [128, NT, 8, 65], BF16, [])
                dump("KcT", KcT[:], [128, 4, 128], BF16, [])
                dump("Vc", Vc[:], [128, 4, 97], BF16, [])

                with contextlib.ExitStack() as stQ:
                    if upto("P2"):
                        QT = sb(stQ, "QT", [128, 8, T], BF16)
                        gates = sb(stQ, "gates", [128, NT, 48], F32)
                        with contextlib.ExitStack() as st:
                            rC, rS = rope_tabs(st)
                            rope = make_rope(st, "r2", RTt)
                            wq = [sb(st, f"wq{i}", [128, KD, 512], BF16) for i in range(2)]
                            wg = sb(st, "wg", [128, KD, 48], BF16)
                            for hq in range(2):
                                S.dma("pool", wq[hq][:], WIN[:, :, hq * 512:(hq + 1) * 512], writes=[("wq", hq)])
                            S.dma("pool", wg[:], WIN[:, :, C_NG:C_NG + 48], writes=["wg"])
                            for pair in range(8):
                                w = wq[pair // 4]
                                for tb in range(4):
                                    bj = nb()
                                    proj_fm(bj, lambda kd, w=w, pair=pair: w[:, kd, (pair % 4) * 128:(pair % 4 + 1) * 128],
                                            tb, [("wq", pair // 4)])
                                    rope(bj, 512, QT[:, pair, tblk(tb)], rC[:, tblk(tb)], rS[:, tblk(tb)],
                                         [("QT", pair, tb)])
                            for tt in range(NT):
                                bj = nb()
                                for kd in range(KD):
                                    S.op("pe", f_mm(bank(bj)[:, 0:48], uT[:, kd, tt * 128:(tt + 1) * 128], wg[:, kd, :],
                                                    start=(kd == 0), stop=(kd == KD - 1)),
                                         reads=[("uT", tt), "wg"], writes=[bkey(bj)])
                                S.op("act", f_act(gates[:, tt, :], bank(bj)[:, 0:48], AF.Sigmoid), reads=[bkey(bj)],
                                     writes=[("gates", tt)])
                            S.barrier()
                        dump("QT", QT[:], [128, 8, T], BF16, [])
                        dump("gates", gates[:], [128, NT, 48], F32, [])
                        if upto("P3"):
                            attention(stQ, S, nc, sb, sap, PP, A, dh, ident, QT, gates, KsT, KwT, Vsw, KcT, Vc, yT, GORD)
                    S.barrier()
                S.barrier()
            dump("yT", yT[:], [128, KD, T], BF16, [])

            if upto("P4"):
              with contextlib.ExitStack() as stX:
                oxT = sb(stX, "oxT", [128, 2, T], BF16)
                ones1 = sb(stX, "ones1", [128, 1], F32)
                S.op("dve", f_memset(ones1[:], 1.0), writes=["ones1"])
                with contextlib.ExitStack() as st:
                    wkv = sb(st, "wkv", [128, KD, 512], BF16)
                    wqx = sb(st, "wqx", [128, KD, 256], BF16)
                    KxT = sb(st, "KxT", [128, 2, 256], BF16)
                    Vx = sb(st, "Vx", [128, 2, 4, 65], BF16)
                    QxT = sb(st, "QxT", [128, 2, T], BF16)
                    PTx = [sb(st, f"PTx{i}", [128, 8, 512], BF16) for i in range(2)]
                    oxtm = [sb(st, f"oxtm{i}", [128, 256], BF16) for i in range(2)]
                    sx = [sb(st, f"sx{i}", [128, 8], F32) for i in range(2)]
                    S.dma("pool", wkv[:], A["w_mem_kv"].rearrange("(k p) c -> p k c", p=128), writes=["wkv"])
                    S.dma("pool", wqx[:], WIN[:, :, C_QX:C_QX + 256], writes=["wqx"])
                    S.op("pool", f_memset(Vx[:, :, :, 64:65], 1.0), writes=["Vx1"])
                    mkeys = [("memT", 0), ("memT", 1)]
                    for pair in range(2):
                        bj = nb()
                        for kd in range(KD):
                            S.op("pe", f_mm(bank(bj)[:, 0:256], wkv[:, kd, pair * 128:(pair + 1) * 128], memT[:, kd, :],
                                            start=(kd == 0), stop=(kd == KD - 1)), reads=["wkv"] + mkeys, writes=[bkey(bj)])
                        S.op("act", f_copy(KxT[:, pair, :], bank(bj)[:, 0:256]), reads=[bkey(bj)], writes=[("KxT", pair)])
                    for mt in range(2):
                        bj = nb()
                        for kd in range(KD):
                            S.op("pe", f_mm(bank(bj)[:, 0:256], memT[:, kd, mt * 128:(mt + 1) * 128], wkv[:, kd, 256:512],
                                            start=(kd == 0), stop=(kd == KD - 1)), reads=["wkv"] + mkeys, writes=[bkey(bj)])
                        S.op("act", f_copy(Vx[:, mt, :, 0:64], bank(bj)[:, 0:256].rearrange("p (a b) -> p a b", a=4)),
                             reads=[bkey(bj)], writes=[("Vx", mt)])
                    for pair in range(2):
                        for tb in range(4):
                            bj = nb()
                            proj_fm(bj, lambda kd, pair=pair: wqx[:, kd, pair * 128:(pair + 1) * 128], tb, ["wqx"])
                            S.op("act", f_copy(QxT[:, pair, tblk(tb)], bank(bj)), reads=[bkey(bj)], writes=[("QxT", pair, tb)])
                    for qb in range(4):
                        px = PTx[qb % 2]
                        for h_ in range(4):
                            pair, half = h_ // 2, h_ % 2
                            for mt in range(2):
                                bj = nb()
                                S.op("pe", f_mm(bank(bj), KxT[half * 64:(half + 1) * 64, pair, mt * 128:(mt + 1) * 128],
                                                QxT[half * 64:(half + 1) * 64, pair, tblk(qb)]),
                                     reads=[("KxT", pair), ("QxT", pair, qb)], writes=[bkey(bj)])
                                S.op("act", f_act(px[:, h_ * 2 + mt, :], bank(bj), AF.Exp, scale=0.125), reads=[bkey(bj)],
                                     writes=[("PTx", qb % 2, h_ * 2 + mt)])
                        pkeys = [("PTx", qb % 2, j_) for j_ in range(8)]
                        for qt in range(4):
                            tt = qb * 4 + qt
                            bj = nb()
                            first = True
                            for h_ in range(4):
                                for mt in range(2):
                                    S.op("pe", f_mm(bank(bj)[:, h_ * 65:(h_ + 1) * 65], px[:, h_ * 2 + mt, qt * 128:(qt + 1) * 128],
                                                    Vx[:, mt, h_, :], start=first, stop=True, skip=True),
                                         reads=pkeys + [("Vx", mt), "Vx1"], writes=[bkey(bj)])
                                    first = False
                            k2 = tt % 2
                            sums = sap(PP[bj // 2], 1024, 0, 128, (bj % 2) * 512 + 64, [[65, 4]])
                            S.op("dve", lambda h, k2=k2, sums=sums: h.reciprocal(out=sx[k2][:, 0:4], in_=sums),
                                 reads=[bkey(bj)], writes=[("sx", k2)])
                            dat = sap(PP[bj // 2], 1024, 0, 128, (bj % 2) * 512, [[65, 4], [1, 64]])
                            rcb = sap(sx[k2], 8, 0, 128, 0, [[1, 4], [0, 64]])
                            S.op("dve", f_tt(oxtm[k2][:].rearrange("p (a b) -> p a b", a=4), dat, rcb, ALU.mult),
                                 reads=[bkey(bj), ("sx", k2)], writes=[("oxtm", k2)])
                            bt = nb()
                            trv = bank(bt).bitcast(BF16)[:, 0:256].rearrange("p (a b) -> p a b", a=2)
                            for kc in range(2):
                                S.op("pe", f_tr(trv[:, kc, :], oxtm[k2][:, kc * 128:(kc + 1) * 128], ident[:]),
                                     reads=[("oxtm", k2), "ident"], writes=[bkey(bt)])
                            S.op("dve", f_copy(oxT[:, :, tt * 128:(tt + 1) * 128], trv), reads=[bkey(bt)], writes=[("oxT", tt)])
                    S.barrier()
                dump("oxT", oxT[:], [128, 2, T], BF16, [])

                with contextlib.ExitStack() as st:
                    wbd = [sb(st, f"wbd{i}", [128, 8, 128], BF16) for i in range(2)]
                    wxo = sb(st, "wxo", [128, 2, D], BF16)
                    par = sb(st, "par", [128, 12, 8], F32)
                    S.dma("pool", wxo[:], A["w_xo"].rearrange("(kc p) d -> p kc d", p=128), writes=["wxo"])
                    for i_, nm in enumerate(("w_rg_a", "w_rg_i")):
                        S.op("pool", f_memset(wbd[i_][:], 0.0), writes=[("wbd", i_)])
                        for n_ in range(16):
                            c_, hf = n_ // 2, n_ % 2
                            S.dma("pool", wbd[i_][hf * 64:(hf + 1) * 64, c_, hf * 64:(hf + 1) * 64], A[nm][n_],
                                  reads=[], writes=[("wbd", i_)] if n_ == 0 else [("wbd", i_, n_)])
                    wbd_keys = [[("wbd", i_)] + [("wbd", i_, n_) for n_ in range(1, 16)] for i_ in range(2)]
                    S.dma("sp", par[:, 0:4, :], A["conv_w"].rearrange("k (c p) -> p k c", p=128), writes=[("par", 0)],
                          allow_slow_non_contiguous=True)
                    for j_, nm in ((4, "conv_b"), (5, "b_rg_a"), (6, "b_rg_i"), (7, "rg_lambda")):
                        S.dma("sp", par[:, j_, :], A[nm].rearrange("(c p) -> p c", p=128), writes=[("par", j_)],
                              allow_slow_non_contiguous=True)
                    S.op("act", f_act(par[:, 10, :], par[:, 7, :], AF.Exp, scale=-1.0), reads=[("par", 7)], writes=[("par", 10)])
                    S.op("act", f_act(par[:, 11, :], par[:, 10, :], AF.Ln, bias=ones1[:], scale=1.0), reads=[("par", 10), "ones1"],
                         writes=[("par", 11)])
                    S.op("dve", f_ts(par[:, 8, :], par[:, 11, :], -8.0, None, ALU.mult), reads=[("par", 11)], writes=[("par", 8)])
                    S.op("dve", f_ts(par[:, 9, :], par[:, 11, :], -16.0, None, ALU.mult), reads=[("par", 11)], writes=[("par", 9)])
                    pk_all = [("par", j_) for j_ in range(12)]
                    wr = [sb(st, f"wr{i}", [128, KD, 5, 128], BF16) for i in range(2)]
                    xr = sb(st, "xr", [128, T + 3], F32)
                    xc = sb(st, "xc", [128, T], F32)
                    xcb = sb(st, "xcb", [128, T], BF16)
                    Rr = sb(st, "Rr", [128, T], F32)
                    Aa = sb(st, "Aa", [128, T], F32)
                    GI = sb(st, "GI", [128, T], F32)
                    GL = sb(st, "GL", [128, T], F32)
                    GM = [sb(st, f"GM{i}", [128, T], F32) for i in range(3)]
                    ACC = sb(st, "ACC", [128, T], F32)
                    S.op("pool", f_memset(xr[:, 0:3], 0.0), writes=["xr0"])
                    ngm = [0]

                    def K4(nm):
                        return [(nm, tb) for tb in range(4)]
                    def load_wr(c):
                        cols = [C_XR + c * 128, C_GR + c * 128, C_MG + c * 128, C_MG + D + c * 128, C_MG + 2 * D + c * 128]
                        for j_ in range(5):
                            S.dma("pool", wr[c % 2][:, :, j_, :], WIN[:, :, cols[j_]:cols[j_] + 128], writes=[("wr", c % 2, j_)])
                    load_wr(0)
                    for c in range(8):
                        w = wr[c % 2]
                        wkeys = [("wr", c % 2, j_) for j_ in range(5)]
                        if c + 1 < 8:
                            load_wr(c + 1)

                        def proj_c(j_, tb, w=w, wkeys=wkeys):
                            bj = nb()
                            proj_fm(bj, lambda kd: w[:, kd, j_, :], tb, [wkeys[j_]])
                            return bj
                        ykeys = [("yT", t_) for t_ in range(NT)]
                        for tb in range(4):
                            bj = proj_c(0, tb)
                            S.op("act", f_copy(xr[:, 3 + tb * 512:3 + (tb + 1) * 512], bank(bj)), reads=[bkey(bj)],
                                 writes=[("xr", tb)])
                        S.op("dve", f_ts(xc[:], xr[:, 3:3 + T], par[:, 3, c:c + 1], par[:, 4, c:c + 1], ALU.mult, ALU.add),
                             reads=K4("xr") + pk_all, writes=K4("xc"))
                        for k_ in range(3):
                            S.op("dve", f_stt(xc[:], xr[:, k_:k_ + T], par[:, k_, c:c + 1], xc[:], ALU.mult, ALU.add),
                                 reads=K4("xr") + ["xr0"] + K4("xc") + pk_all, writes=K4("xc"))
                        S.op("dve", f_copy(xcb[:], xc[:]), reads=K4("xc"), writes=K4("xcb"))
                        for tb in range(4):
                            bj = proj_c(1, tb)
                            S.op("act", f_act(GL[:, tblk(tb)], bank(bj), AF.Gelu_apprx_tanh), reads=[bkey(bj)], writes=[("GL", tb)])
                        for b_ in range(3):
                            for tb in range(4):
                                bj = proj_c(2 + b_, tb)
                                S.op("act", f_act(GM[b_][:, tblk(tb)], bank(bj), AF.Sigmoid), reads=[bkey(bj)],
                                     writes=[("GM%d" % b_, tb)])
                        S.op("pool", f_tt(ACC[:], GM[0][:], yT[:, c, :], ALU.mult), reads=K4("GM0") + ykeys, writes=K4("ACC"))
                        for tb in range(4):
                            bj = nb()
                            for kc in range(2):
                                S.op("pe", f_mm(bank(bj), wxo[:, kc, c * 128:(c + 1) * 128], oxT[:, kc, tblk(tb)],
                                                start=(kc == 0), stop=(kc == 1)),
                                     reads=["wxo"] + [("oxT", t_) for t_ in range(tb * 4, tb * 4 + 4)], writes=[bkey(bj)])
                            S.op("dve", f_tt(GM[2][:, tblk(tb)], GM[2][:, tblk(tb)], bank(bj), ALU.mult),
                                 reads=[bkey(bj), ("GM2", tb)], writes=[("GM2", tb)])
                        for (i_, dstt, bcol) in ((0, Rr, 5), (1, GI, 6)):
                            for tb in range(4):
                                bj = nb()
                                S.op("pe", f_mm(bank(bj), wbd[i_][:, c, :], xcb[:, tblk(tb)]), reads=wbd_keys[i_] + K4("xcb"),
                                     writes=[bkey(bj)])
                                S.op("act", f_act(dstt[:, tblk(tb)], bank(bj), AF.Sigmoid, bias=par[:, bcol, c:c + 1]),
                                     reads=[bkey(bj)] + pk_all, writes=[("Rr" if i_ == 0 else "GI", tb)])
                        S.op("act", f_act(Aa[:], Rr[:], AF.Exp, scale=par[:, 8, c:c + 1]), reads=K4("Rr") + pk_all, writes=K4("Aa"))
                        S.op("act", f_act(Rr[:], Rr[:], AF.Exp, scale=par[:, 9, c:c + 1]), reads=K4("Rr") + pk_all, writes=K4("Rr"))
                        S.op("act", f_act(Rr[:], Rr[:], AF.Sqrt, bias=ones1[:], scale=-1.0), reads=K4("Rr") + ["ones1"], writes=K4("Rr"))
                        S.op("dve", f_tt(GI[:], GI[:], xc[:], ALU.mult), reads=K4("GI") + K4("xc"), writes=K4("GI"))
                        S.op("dve", f_tt(GI[:], GI[:], Rr[:], ALU.mult), reads=K4("GI") + K4("Rr"), writes=K4("GI"))
                        S.op("dve", lambda h: h.tensor_tensor_scan(out=xc[:], data0=Aa[:], data1=GI[:], initial=0.0,
                                                                    op0=ALU.mult, op1=ALU.add),
                             reads=K4("Aa") + K4("GI") + K4("xc"), writes=K4("xc"))
                        S.op("dve", f_tt(GL[:], GL[:], xc[:], ALU.mult), reads=K4("GL") + K4("xc"), writes=K4("GL"))
                        S.op("dve", f_tt(GM[1][:], GM[1][:], GL[:], ALU.mult), reads=K4("GM1") + K4("GL"), writes=K4("GM1"))
                        S.op("pool", f_tt(ACC[:], ACC[:], GM[2][:], ALU.add), reads=K4("GM2") + K4("ACC"), writes=K4("ACC"))
                        S.op("dve", f_tt(yT[:, c, :], ACC[:], GM[1][:], ALU.add), reads=K4("GM1") + K4("ACC") + ykeys,
                             writes=ykeys)
                    S.barrier()
              S.barrier()
            dump("ymT", yT[:], [128, KD, T], BF16, [])

        if stop_after is None or stop_after == "P5":
            TBK = 1024
            NFH = 2
            NTL = TBK // 128
            with contextlib.ExitStack() as st:
                wo = sb(st, "wo", [128, KD, D], BF16)
                WO = A["w_o"].rearrange("(k p) d -> p k d", p=128)
                for hf in range(2):
                    S.dma("pool", wo[:, :, hf * 512:(hf + 1) * 512], WO[:, :, hf * 512:(hf + 1) * 512], writes=[("wo", hf)])
                gml = sb(st, "gml", [128, D], F32)
                gfn = sb(st, "gfn", [128, D], F32)
                S.dma("sp", gml[:], bcast_row("g_mlp"), writes=["gml"])
                S.dma("sp", gfn[:], bcast_row("g_final"), writes=["gfn"])
                h2 = sb(st, "h2", [128, NTL, D], F32)
                vT = sb(st, "vT", [128, KD, TBK], BF16)
                ffT = sb(st, "ffT", [128, 16, TBK], BF16)
                wb = [sb(st, f"wb{i}", [128, 4096], BF16) for i in range(5)]
                xs = [sb(st, f"xs{i}", [128, D], F32) for i in range(2)]
                xnb = [sb(st, f"xnb{i}", [128, D], BF16) for i in range(2)]
                rl = [sb(st, f"rl{i}", [128, 512], F32) for i in range(2)]
                junk5 = sb(st, "junk5", [128, D], BF16)
                st5 = sb(st, "st5", [128, 6, 16], F32)
                WUP = A["w_up"].rearrange("(k p) f -> p k f", p=128)
                WDN = A["w_down"].rearrange("(fc p) d -> p fc d", p=128)
                nwb = [0]
                nxs = [0]
                nrl = [0]

                def rms_stats(src, tl, col0, tag):
                    S.op("act", f_act(junk5[:], src, AF.Square, accum_out=st5[:, col0, tl:tl + 1]),
                         reads=[("h2", tl)], writes=["junk5", (tag, 0, tl)])
                    S.op("act", f_act(st5[:, col0 + 1, tl:tl + 1], st5[:, col0, tl:tl + 1], AF.Sqrt, bias=eps_t[:], scale=1.0 / D),
                         reads=[(tag, 0, tl), "eps"], writes=[(tag, 1, tl)])
                    S.op("dve", lambda h: h.reciprocal(out=st5[:, col0 + 2, tl:tl + 1], in_=st5[:, col0 + 1, tl:tl + 1]),
                         reads=[(tag, 1, tl)], writes=[(tag, 2, tl)])

                wsched = []
                for blk in range(T // TBK):
                    for fh in range(NFH):
                        for wbi in range(4):
                            wsched.append(("up", fh * 2048 + wbi * 512))
                        for wbi in range(4):
                            wsched.append(("dn", fh * 16 + wbi * 4))
                nld = [0]

                def load_next():
                    k_ = nld[0]
                    if k_ >= len(wsched):
                        return
                    nld[0] += 1
                    kind, off = wsched[k_]
                    if kind == "up":
                        S.dma("pool", wb[k_ % 5][:].rearrange("p (k f) -> p k f", k=KD), WUP[:, :, off:off + 512], writes=[("wb", k_ % 5)])
                    else:
                        S.dma("pool", wb[k_ % 5][:].rearrange("p (k f) -> p k f", k=4), WDN[:, off:off + 4, :], writes=[("wb", k_ % 5)])
                for _ in range(3):
                    load_next()

                for blk in range(T // TBK):
                    t0 = blk * TBK
                    for tl in range(NTL):
                        tt = blk * NTL + tl
                        xk = nxs[0] % 2
                        nxs[0] += 1
                        S.dma("sp", xs[xk][:], A["x"][tt * 128:(tt + 1) * 128, :], writes=[("xs", xk)])
                        for dh in range(2):
                            bj = nb()
                            for kd in range(KD):
                                S.op("pe", f_mm(bank(bj), yT[:, kd, tt * 128:(tt + 1) * 128], wo[:, kd, dh * 512:(dh + 1) * 512],
                                                start=(kd == 0), stop=(kd == KD - 1)),
                                     reads=[("yT", t_) for t_ in range(NT)] + [("wo", dh)], writes=[bkey(bj)])
                            S.op("dve", f_tt(h2[:, tl, dh * 512:(dh + 1) * 512], bank(bj), xs[xk][:, dh * 512:(dh + 1) * 512], ALU.add),
                                 reads=[bkey(bj), ("xs", xk)], writes=[("h2", tl)])
                        rms_stats(h2[:, tl, :], tl, 0, "sa")
                        nk = tl % 2
                        S.op("dve", f_stt(xnb[nk][:], h2[:, tl, :], st5[:, 2, tl:tl + 1], gml[:], ALU.mult, ALU.mult),
                             reads=[("h2", tl), ("sa", 2, tl), "gml"], writes=[("xnb", nk)])
                        bj = nb()
                        trv = bank(bj).bitcast(BF16).rearrange("p (a b) -> p a b", a=8)
                        for kd in range(KD):
                            S.op("pe", f_tr(trv[:, kd, :], xnb[nk][:, kd * 128:(kd + 1) * 128], ident[:]),
                                 reads=[("xnb", nk), "ident"], writes=[bkey(bj)])
                        S.op("act", f_copy(vT[:, :, tl * 128:(tl + 1) * 128], trv), reads=[bkey(bj)], writes=[("vT", tl)])
                    vkeys = [("vT", tl) for tl in range(NTL)]
                    for fh in range(NFH):
                        for wbi in range(4):
                            wk_ = nwb[0] % 5
                            nwb[0] += 1
                            load_next()
                            wt = wb[wk_][:].rearrange("p (k f) -> p k f", k=KD)
                            for fl in range(4):
                                fc = wbi * 4 + fl
                                for tb in range(TBK // 512):
                                    bj = nb()
                                    for kd in range(KD):
                                        S.op("pe", f_mm(bank(bj), wt[:, kd, fl * 128:(fl + 1) * 128], vT[:, kd, tblk(tb)],
                                                        start=(kd == 0), stop=(kd == KD - 1)),
                                             reads=[("wb", wk_)] + vkeys, writes=[bkey(bj)])
                                    rk = nrl[0] % 2
                                    nrl[0] += 1
                                    S.op("act", f_act(rl[rk][:], bank(bj), AF.Relu), reads=[bkey(bj)], writes=[("rl", rk)])
                                    S.op("dve", f_tt(ffT[:, fc, tblk(tb)], rl[rk][:], rl[rk][:], ALU.mult), reads=[("rl", rk)],
                                         writes=[("ffT", fc, tb)])
                        for wbi in range(4):
                            wk_ = nwb[0] % 5
                            nwb[0] += 1
                            load_next()
                            wt = wb[wk_][:].rearrange("p (k f) -> p k f", k=4)
                            for tl in range(NTL):
                                for dh in range(2):
                                    bj = nb()
                                    for fl in range(4):
                                        fc = wbi * 4 + fl
                                        S.op("pe", f_mm(bank(bj), ffT[:, fc, tl * 128:(tl + 1) * 128], wt[:, fl, dh * 512:(dh + 1) * 512],
                                                        start=(fl == 0), stop=(fl == 3)),
                                             reads=[("wb", wk_), ("ffT", fc, tl // 4)], writes=[bkey(bj)])
                                    S.op("dve", f_tt(h2[:, tl, dh * 512:(dh + 1) * 512], h2[:, tl, dh * 512:(dh + 1) * 512], bank(bj), ALU.add),
                                         reads=[bkey(bj), ("h2", tl)], writes=[("h2", tl)])
                    for tl in range(NTL):
                        tt = blk * NTL + tl
                        rms_stats(h2[:, tl, :], tl, 3, "sb")
                        xk = nxs[0] % 2
                        nxs[0] += 1
                        S.op("dve", f_stt(xs[xk][:], h2[:, tl, :], st5[:, 5, tl:tl + 1], gfn[:], ALU.mult, ALU.mult),
                             reads=[("h2", tl), ("sb", 2, tl), "gfn"], writes=[("xs", xk)])
                        S.dma("sp", OUT.ap()[tt * 128:(tt + 1) * 128, :], xs[xk][:], reads=[("xs", xk)], writes=[("out", tt)])
                S.barrier()

        S.wait_keys("sp", [("dbg", n) for n in dbg_outs] + [("out", t) for t in range(NT)])
        S.emit()
    return nc, dbg_outs


def kernel(**inputs):
    nc, _ = build()
    consts = _consts()
    shared = {}
    for name, shape in IN_SPECS:
        if name in ("x", "mem"):
            continue
        a = np.asarray(inputs[name], dtype=np.float32)
        if name != "g_final":
            a = a[0]
        shared[name] = np.ascontiguousarray(a.reshape(shape))
    for name, shape, dt in CONST_SPECS:
        shared["c_" + name] = consts[name]
    x = np.asarray(inputs["x"], dtype=np.float32)
    mem = np.asarray(inputs["mem"], dtype=np.float32)
    in_maps = []
    for b in range(8):
        m = dict(shared)
        m["x"] = np.ascontiguousarray(x[b])
        m["mem"] = np.ascontiguousarray(mem[b])
        in_maps.append(m)
    res = run_bass_kernel_spmd(nc, in_maps, core_ids=list(range(8)))
    return np.stack([np.asarray(res.results[b]["out"], dtype=np.float32) for b in range(8)], axis=0)
```

```python
import contextlib
import numpy as np
import ml_dtypes
import concourse.bass as bass
import concourse.mybir as mybir
from concourse.bass_utils import run_bass_kernel_spmd

F32 = mybir.dt.float32
BF16 = mybir.dt.bfloat16
AF = mybir.ActivationFunctionType
ALU = mybir.AluOpType
NPBF = ml_dtypes.bfloat16

T = 2048
D = 1024
NT = 16
KD = 8
NCMP = 127
C_Q, C_KC, C_VC, C_KS, C_VS, C_KW, C_VW, C_NG, C_XR, C_GR, C_QX, C_MG = (
    0, 1024, 1280, 1536, 1792, 2048, 2304, 2560, 2608, 3632, 4656, 4912)
D_IN = 7984
SLOT_HH = [0, 2, 1, 3]


class _Eng:
    def __init__(self, name):
        self.name = name
        self.sem = None
        self.count = 0
        self.seen = {}
        self.ops = []


class Sched:
    NS = 6

    def __init__(self, nc, stack):
        self.nc = nc
        self.E = {}
        for n in ("pe", "act", "dve", "pool", "sp"):
            e = _Eng(n)
            e.sem = stack.enter_context(nc.semaphore("sem_" + n))
            e.miles = set()
            self.E[n] = e
        self.dsem = {}
        self.dk = {}
        for q in ("sp", "pool", "act"):
            self.dsem[q] = [stack.enter_context(nc.semaphore(f"dq_{q}_{i}")) for i in range(self.NS)]
            self.dk[q] = 0
        self.last_w = {}
        self.readers = {}

    def _toks(self, reads, writes):
        toks = []
        for k in reads:
            t = self.last_w.get(k)
            if t is not None:
                toks.append(t)
        for k in writes:
            t = self.last_w.get(k)
            if t is not None:
                toks.append(t)
            r = self.readers.get(k)
            if r:
                toks.extend(r.values())
        return toks

    def _waits(self, e, toks):
        need = {}
        for t in toks:
            if t[0] == "c":
                if t[1] == e.name and e.name == "pe":
                    continue
                key, v = t[1], t[2]
            else:
                key, v = t[1].num, t[2]
            if e.seen.get(key, 0) >= v:
                continue
            if key not in need or need[key][2] < v:
                need[key] = t
        for key, t in need.items():
            e.ops.append(("wait", t))
            e.seen[key] = t[2]
            if t[0] == "c":
                self.E[t[1]].miles.add(t[2])

    def _update(self, tok, reads, writes):
        ok = (tok[0], tok[1] if tok[0] == "c" else tok[1].num)
        for k in reads:
            self.readers.setdefault(k, {})[ok] = tok
        for k in writes:
            self.last_w[k] = tok
            self.readers[k] = {}

    def op(self, eng, fn, reads=(), writes=()):
        e = self.E[eng]
        self._waits(e, self._toks(reads, writes))
        e.count += 1
        e.ops.append(("op", fn, e.count))
        self._update(("c", e.name, e.count), reads, writes)

    def dma(self, q, out, in_, reads=(), writes=(), **kw):
        e = self.E[q]
        k = self.dk[q]
        self.dk[q] += 1
        slot, rnd = k % self.NS, k // self.NS
        dsem = self.dsem[q][slot]
        toks = self._toks(reads, writes)
        if rnd > 0:
            toks.append(("d", dsem, 16 * rnd))
        self._waits(e, toks)
        e.ops.append(("dma", out, in_, kw, dsem))
        self._update(("d", dsem, 16 * (rnd + 1)), reads, writes)

    def wait_keys(self, eng, keys):
        self._waits(self.E[eng], [self.last_w[k] for k in keys if k in self.last_w])

    def barrier(self):
        toks = []
        for e in self.E.values():
            if e.count > 0:
                toks.append(("c", e.name, e.count))
        for q in self.dsem:
            k = self.dk[q]
            for slot in range(self.NS):
                n = (k - slot + self.NS - 1) // self.NS
                if n > 0:
                    toks.append(("d", self.dsem[q][slot], 16 * n))
        for e in self.E.values():
            self._waits(e, [t for t in toks if not (t[0] == "c" and t[1] == e.name)])

    def emit(self):
        nc, E = self.nc, self.E
        rank = {}
        for n, e in E.items():
            rank[n] = {idx: r + 1 for r, idx in enumerate(sorted(e.miles))}

        def run(e, h):
            for item in e.ops:
                if item[0] == "wait":
                    t = item[1]
                    if t[0] == "c":
                        h.wait_ge(E[t[1]].sem, rank[t[1]][t[2]])
                    else:
                        h.wait_ge(t[1], t[2])
                elif item[0] == "op":
                    ins = item[1](h)
                    if item[2] in e.miles:
                        ins.then_inc(e.sem, 1)
                else:
                    _, out, in_, kw, dsem = item
                    h.dma_start(out=out, in_=in_, **kw).then_inc(dsem, 16)

        with nc.Block() as block:
            @block.tensor
            def _(h):
                run(E["pe"], h)

            @block.scalar
            def _(h):
                run(E["act"], h)

            @block.vector
            def _(h):
                run(E["dve"], h)

            @block.gpsimd
            def _(h):
                run(E["pool"], h)

            @block.sync
            def _(h):
                run(E["sp"], h)


def f_mm(out, lhsT, rhs, start=True, stop=True, skip=False):
    return lambda h: h.matmul(out, lhsT=lhsT, rhs=rhs, start=start, stop=stop, skip_group_check=skip)


def f_tr(out, in_, ident):
    return lambda h: h.transpose(out=out, in_=in_, identity=ident)


def f_act(out, in_, func, bias=None, scale=None, accum_out=None):
    kw = {}
    if bias is not None:
        kw["bias"] = bias
    if scale is not None:
        kw["scale"] = scale
    if accum_out is not None:
        kw["accum_out"] = accum_out
    return lambda h: h.activation(out=out, in_=in_, func=func, **kw)


def f_copy(out, in_):
    return lambda h: (h.copy(out=out, in_=in_) if hasattr(h, "activation") else h.tensor_copy(out=out, in_=in_))


def f_tt(out, in0, in1, op):
    return lambda h: h.tensor_tensor(out=out, in0=in0, in1=in1, op=op)


def f_ts(out, in0, s1, s2, op0, op1=None):
    if op1 is None:
        return lambda h: h.tensor_scalar(out=out, in0=in0, scalar1=s1, scalar2=None, op0=op0)
    return lambda h: h.tensor_scalar(out=out, in0=in0, scalar1=s1, scalar2=s2, op0=op0, op1=op1)


def f_stt(out, in0, scalar, in1, op0, op1):
    return lambda h: h.scalar_tensor_tensor(out=out, in0=in0, scalar=scalar, in1=in1, op0=op0, op1=op1)


def f_memset(ap, c):
    return lambda h: h.memset(ap, c)


def _consts():
    c = {}
    c["ident"] = np.eye(128, dtype=np.float32).astype(NPBF)
    inv = (1.0 / (500000.0 ** (np.arange(0, 16, 2, dtype=np.float32) / 16.0))).astype(np.float32)

    def tabs(pos):
        ang = pos.astype(np.float32)[:, None] * inv[None, :]
        cs, sn = np.cos(ang).astype(np.float32), np.sin(ang).astype(np.float32)
        Ct = np.ones((128, len(pos)), np.float32)
        St = np.zeros((128, len(pos)), np.float32)
        for half in (0, 64):
            Ct[half:half + 8] = cs.T
            Ct[half + 8:half + 16] = cs.T
            St[half:half + 8] = -sn.T
            St[half + 8:half + 16] = sn.T
        return Ct, St
    c["ropeC"], c["ropeS"] = tabs(np.arange(T))
    c["ropeCc"], c["ropeSc"] = tabs(np.arange(NCMP) * 16 + 31)
    RT = np.zeros((128, 128), np.float32)
    for half in (0, 64):
        for j in range(8):
            RT[half + j + 8, half + j] = 1.0
            RT[half + j, half + j + 8] = 1.0
    c["RT"] = RT.astype(NPBF)
    kk = np.arange(128)[:, None]
    qq = np.arange(128)[None, :]
    c["mcaus"] = (kk <= qq).astype(np.float32).astype(NPBF)
    c["mwlow"] = (kk > qq).astype(np.float32).astype(NPBF)
    n = np.arange(128)[:, None]
    t = np.arange(T)[None, :]
    c["cmpmask"] = ((16 * n + 31 <= t) & (n < NCMP)).astype(np.float32).astype(NPBF)
    c0 = np.arange(NCMP)[:, None] * 16
    s0 = np.arange(32)[None, :] * 64
    ov = np.clip(np.minimum(c0 + 32, s0 + 64) - np.maximum(c0, s0), 0, None) / 32.0
    c2s = np.zeros((128, 33), np.float32)
    c2s[:NCMP, 0] = 1.0
    c2s[:NCMP, 1:] = ov
    c["c2s33"] = c2s.astype(NPBF)
    E = np.zeros((128, T), np.float32)
    E[np.arange(T) // 64, np.arange(T)] = 1.0
    E[64 + np.arange(T) // 64, np.arange(T)] = 1.0
    c["E64"] = E.astype(NPBF)
    keep = np.zeros((128, 16, 32), np.float32)
    bias = np.zeros((128, 16, 32), np.float32)
    for i in range(16):
        tt = 128 * i + np.arange(128)
        cur = (tt // 64)[:, None]
        blk = np.arange(32)[None, :]
        forced = (blk == 0) | (blk == cur) | (blk == cur - 1)
        future = blk * 64 > tt[:, None]
        keep[:, i, :] = (~forced & ~future).astype(np.float32)
        bias[:, i, :] = np.where(forced, 10.0, np.where(future, -10.0, 0.0))
    c["selkeep"] = keep
    c["selbias"] = bias
    ph = np.zeros((128, 16), np.float32)
    for i in range(16):
        ph[:, i] = np.maximum(0, 511 - (128 * i + np.arange(128)))
    c["phantom"] = ph
    return c


CONST_SPECS = [("ident", [128, 128], BF16), ("ropeC", [128, T], F32), ("ropeS", [128, T], F32),
               ("ropeCc", [128, NCMP], F32), ("ropeSc", [128, NCMP], F32), ("RT", [128, 128], BF16),
               ("mcaus", [128, 128], BF16), ("mwlow", [128, 128], BF16), ("cmpmask", [128, T], BF16),
               ("c2s33", [128, 33], BF16), ("E64", [128, T], BF16), ("selkeep", [128, 16, 32], F32),
               ("selbias", [128, 16, 32], F32), ("phantom", [128, 16], F32)]

IN_SPECS = [("x", [T, D]), ("mem", [256, D]), ("g_mix", [D]), ("w_in", [D, D_IN]),
            ("cmp_pos_k", [32, 64]), ("cmp_pos_v", [32, 64]), ("w_cmp_k1", [2048, 256]), ("w_cmp_k2", [256, 64]),
            ("w_cmp_v1", [2048, 256]), ("w_cmp_v2", [256, 64]), ("conv_w", [4, D]), ("conv_b", [D]),
            ("w_rg_a", [16, 64, 64]), ("b_rg_a", [D]), ("w_rg_i", [16, 64, 64]), ("b_rg_i", [D]),
            ("rg_lambda", [D]), ("g_mem", [D]), ("w_mem_kv", [D, 512]), ("w_xo", [256, D]),
            ("w_o", [D, D]), ("g_mlp", [D]), ("w_up", [D, 4096]), ("w_down", [4096, D]), ("g_final", [D])]


def attention(stQ, S, nc, sb, sap, PP, A, dh, ident, QT, gates, KsT, KwT, Vsw, KcT, Vc, yT, GORD):
    with contextlib.ExitStack() as st:
        cmpm = sb(st, "cmpm", [128, T], BF16)
        E64t = sb(st, "E64t", [128, T], BF16)
        keep = sb(st, "keep", [128, 16, 32], F32)
        sbias = sb(st, "sbias", [128, 16, 32], F32)
        phant = sb(st, "phant", [128, 16], F32)
        mcaus = sb(st, "mcaus", [128, 128], BF16)
        mwlow = sb(st, "mwlow", [128, 128], BF16)
        for t_, nm in ((cmpm, "cmpmask"), (E64t, "E64"), (keep, "selkeep"), (sbias, "selbias"), (phant, "phantom"),
                       (mcaus, "mcaus"), (mwlow, "mwlow")):
            S.dma("sp", t_[:], A["c_" + nm], writes=["c_" + nm])
        PT = [sb(st, f"PT{i}", [128, 8, 128], BF16) for i in range(6)]
        selm = [sb(st, f"selm{i}", [128, 128], BF16) for i in range(2)]
        selT = [sb(st, f"selT{i}", [128, 128], BF16) for i in range(2)]
        for i in range(2):
            S.op("pool", f_memset(selm[i][:], 0.0), writes=[("selm", i)])
        ytiles = [sb(st, f"ytile{i}", [128, D], F32) for i in range(2)]
        ybf = [sb(st, f"ybf{i}", [128, D], BF16) for i in range(2)]
        scr = [sb(st, f"scr{i}", [128, 128], F32) for i in range(2)]
        scr2 = [sb(st, f"scr2{i}", [128, 4, 32], F32) for i in range(2)]
        tmpn = [sb(st, f"tmpn{i}", [128, 256], F32) for i in range(2)]
        ACCB = {"c": (4, 97), "s": (5, 65), "w": (6, 65)}
        BR = {"c": 0, "s": 1, "w": 2}

        def acc_slot(br, s_):
            j, w = ACCB[br]
            return PP[j // 2][:, j % 2, s_ * w:(s_ + 1) * w]

        def acc_data(br):
            j, w = ACCB[br]
            return sap(PP[j // 2], 1024, 0, 128, (j % 2) * 512, [[2 * w, 2], [w, 2], [1, 64]])

        def acc_sums(br):
            j, w = ACCB[br]
            return sap(PP[j // 2], 1024, 0, 128, (j % 2) * 512 + 64, [[w, 4]])

        def mask_sb(mt, np_):
            return bass.AP(mt, 0, [[128, np_], [0, 4], [1, 128]])

        steps = []
        parts = []
        deferred = []
        nig = [0]
        tmpc = [0]
        mreg = [0]

        for i in range(NT):
            for g in range(4):
                ig = nig[0]
                nig[0] += 1
                sk = ig % 2
                sc = scr[sk]
                gslot = GORD.index(g)
                qkeys = [("QT", 2 * g, i // 4), ("QT", 2 * g + 1, i // 4)]
                QA = QT[0:64, 2 * g:2 * g + 2, i * 128:(i + 1) * 128]
                QB = QT[64:128, 2 * g:2 * g + 2, i * 128:(i + 1) * 128]

                def mk_step(tiles, after, QA=QA, QB=QB, qkeys=qkeys):
                    nt_ = len(tiles)
                    np_ = tiles[0]["np"]
                    w_ = nt_ * 256

                    def scores(pj):
                        for j2, t in enumerate(tiles):
                            c0 = j2 * 256
                            last = t["bias"] is None
                            S.op("pe", f_mm(PP[pj][0:np_, 0, c0:c0 + 256], t["lA"], QA, start=True, stop=last),
                                 reads=t["kkeys"] + qkeys, writes=[("ps", 2 * pj)])
                            S.op("pe", f_mm(PP[pj][0:np_, 1, c0:c0 + 256], t["lB"], QB, start=True, stop=last),
                                 reads=t["kkeys"] + qkeys, writes=[("ps", 2 * pj + 1)])
                            if not last:
                                kt_b, sk_b = t["bias"]
                                for a_ in range(2):
                                    S.op("pe", f_mm(PP[pj][:, a_, c0:c0 + 256],
                                                    E64t[a_ * 64:(a_ + 1) * 64, kt_b * 128:(kt_b + 1) * 128],
                                                    sap(selT[sk_b], 128, a_ * 64, 64, 0, [[0, 2], [1, 128]]),
                                                    start=False, stop=True),
                                         reads=["c_E64", ("selT", sk_b)], writes=[("ps", 2 * pj + a_)])

                    def post(pj, pk):
                        S.op("act", f_act(sap(PT[pk], 1024, 0, np_, 0, [[512, 2], [1, w_]]),
                                          PP[pj][0:np_, :, 0:w_], AF.Exp, scale=0.125),
                             reads=[("ps", 2 * pj), ("ps", 2 * pj + 1)], writes=[("PT", pk)])
                        for j2, t in enumerate(tiles):
                            if t["mask"] is not None:
                                map_, mkey = t["mask"]
                                v_ = sap(PT[pk], 1024, 0, np_, j2 * 256, [[512, 2], [128, 2], [1, 128]])
                                S.op("dve", f_tt(v_, v_, map_, ALU.mult), reads=[("PT", pk), mkey], writes=[("PT", pk)])

                    def pv(pk):
                        for j2, t in enumerate(tiles):
                            for s_ in range(4):
                                S.op("pe", f_mm(acc_slot(t["br"], s_), PT[pk][0:np_, (s_ // 2) * 4 + j2 * 2 + (s_ % 2), :],
                                                t["vrhs"], start=(t["first"] and s_ == 0), stop=True, skip=True),
                                     reads=[("PT", pk)] + t["vkeys"], writes=[("acc", t["br"])])
                    return (scores, post, pv, after)

                def mask4(mt, np_):
                    return bass.AP(mt, 0, [[128, np_], [0, 2], [0, 2], [1, 128]])

                def gate_ap(br, i=i, g=g):
                    return sap(gates, NT * 48, 0, 128, i * 48 + g * 12 + BR[br], [[3, 2], [6, 2]])

                def y_ap(g=g, i=i):
                    return sap(ytiles[i % 2], D, 0, 128, g * 256, [[64, 2], [128, 2], [1, 64]])

                def mk_norm(br, i=i, g=g, sc=sc, sk=sk, y_ap=y_ap, gate_ap=gate_ap, part=0):
                    o = {"c": 0, "s": 12, "w": 24}[br]

                    def norm():
                        sm, rc, cf = sc[:, o:o + 4], sc[:, o + 4:o + 8], sc[:, o + 8:o + 12]
                        kk = ("scr", sk, br)
                        if part in (0, 1):
                            if br == "w":
                                S.op("dve", f_ts(sm, acc_sums(br), phant[:, i:i + 1], 1e-30, ALU.add, ALU.max),
                                     reads=[("acc", br), "c_phantom"], writes=[kk])
                            else:
                                S.op("dve", f_ts(sm, acc_sums(br), 1e-30, None, ALU.max), reads=[("acc", br)], writes=[kk])
                            S.op("dve", lambda h: h.reciprocal(out=rc, in_=sm), reads=[kk], writes=[kk])
                        if part == 1:
                            return
                        S.op("dve", f_tt(cf.rearrange("p (a b) -> p a b", a=2), rc.rearrange("p (a b) -> p a b", a=2),
                                         gate_ap(br), ALU.mult), reads=[kk, ("gates", i)], writes=[kk])
                        cfb = sap(sc, 128, 0, 128, o + 8, [[2, 2], [1, 2], [0, 64]])
                        if br == "c":
                            S.op("dve", f_tt(y_ap(), acc_data(br), cfb, ALU.mult), reads=[("acc", br), kk],
                                 writes=[("ytile", i % 2, g)])
                        else:
                            tk = tmpc[0] % 2
                            tmpc[0] += 1
                            tv = sap(tmpn[tk], 256, 0, 128, 0, [[128, 2], [64, 2], [1, 64]])
                            S.op("dve", f_tt(tv, acc_data(br), cfb, ALU.mult), reads=[("acc", br), kk],
                                 writes=[("tmpn", tk)])
                            S.op("pool", f_tt(y_ap(), y_ap(), tv, ALU.add), reads=[("tmpn", tk), ("ytile", i % 2, g)],
                                 writes=[("ytile", i % 2, g)])
                    return norm

                def mk_topk(i=i, g=g, sc=sc, sk=sk):
                    def topk():
                        j, w = ACCB["c"]
                        rc = sc[:, 4:8]
                        imp, adj, mx8 = sc[:, 40:72], sc[:, 72:104], sc[:, 104:112]
                        kk = ("scr", sk, "c")
                        kt_ = ("scr", sk, "t")
                        src4 = sap(PP[j // 2], 1024, 0, 128, (j % 2) * 512 + 65, [[w, 4], [1, 32]])
                        S.op("dve", f_tt(scr2[sk][:], src4, sap(sc, 128, 0, 128, 4, [[1, 4], [0, 32]]), ALU.mult),
                             reads=[("acc", "c"), kk], writes=[("scr2", sk)])
                        S.op("dve", lambda h: h.tensor_reduce(out=imp, in_=sap(scr2[sk], 128, 0, 128, 0, [[1, 32], [32, 4]]),
                                                              axis=mybir.AxisListType.X, op=ALU.add),
                             reads=[("scr2", sk)], writes=[kt_])
                        S.op("dve", f_tt(adj, imp, sbias[:, i, :], ALU.add), reads=[kt_, "c_selbias"], writes=[kt_])
                        S.op("dve", lambda h: h.max(out=mx8, in_=adj), reads=[kt_], writes=[kt_])
                        S.op("dve", f_ts(sap(selm[sk], 128, 0, 128, 0, [[64, 2], [1, 32]]),
                                         sap(sc, 128, 0, 128, 72, [[0, 2], [1, 32]]), mx8[:, 7:8], -30000.0, ALU.is_lt, ALU.mult),
                             reads=[kt_], writes=[("selm", sk)])
                        def tail(sk=sk):
                            trv = PP[3][:, 1, 0:64].bitcast(BF16)
                            S.op("pe", f_tr(trv, selm[sk][:], ident[:]), reads=[("selm", sk), "ident"], writes=["ps7"])
                            S.op("dve", f_copy(selT[sk][:], trv), reads=["ps7"], writes=[("selT", sk)])
                        deferred.append(("selT", i * 4 + g, tail))
                    return topk

                cm_ap = bass.AP(cmpm, i * 128, [[T, NCMP], [0, 2], [0, 2], [1, 128]])
                nc1_, nc_, tk_ = mk_norm("c", part=1), mk_norm("c", part=2), mk_topk()
                partF, partS = [], []
                partF.append(mk_step([dict(lA=KcT[0:64, gslot, 0:NCMP], lB=KcT[64:128, gslot, 0:NCMP], kkeys=["KcT"],
                                           np=NCMP, bias=None, mask=(cm_ap, "c_cmpmask"), br="c", vrhs=Vc[0:NCMP, g, :],
                                           vkeys=["Vc", "Vc2"], first=True)],
                                     (lambda nc1_=nc1_, nc_=nc_, tk_=tk_: (nc1_(), tk_(), nc_()))))
                wkts = [kt for kt in range(i - 4, i + 1) if kt >= 0]
                skts = [i] + list(range(i))
                for br_, kts, KT, vo in (("w", wkts, KwT, 4), ("s", skts, KsT, 0)):
                    tl_ = []
                    for n_, kt in enumerate(kts):
                        mk_, bs = None, None
                        if kt == i:
                            mk_ = (mask4(mcaus, 128), "c_mcaus")
                        elif br_ == "w" and kt == i - 4:
                            mk_ = (mask4(mwlow, 128), "c_mwlow")
                        elif br_ == "s":
                            bs = (kt, sk)
                        tl_.append(dict(lA=KT[0:64, g, kt * 128:(kt + 1) * 128], lB=KT[64:128, g, kt * 128:(kt + 1) * 128],
                                        kkeys=[("KsT" if br_ == "s" else "KwT", g, kt // 4)], np=128, bias=bs, mask=mk_, br=br_,
                                        vrhs=Vsw[:, kt, vo + g, :], vkeys=[("Vsw", kt), "Vsw1"], first=(n_ == 0)))
                    for p0 in range(0, len(tl_), 2):
                        lastp = p0 + 2 >= len(tl_)
                        (partF if br_ == "w" else partS).append(mk_step(tl_[p0:p0 + 2], mk_norm(br_) if lastp else None))
                if g == 3:
                    def fin(i=i):
                        yb = ybf[i % 2]
                        S.op("act", f_copy(yb[:], ytiles[i % 2][:]), reads=[("ytile", i % 2, g_) for g_ in range(4)],
                             writes=[("ybf", i % 2)])

                        def tail(i=i, yb=yb):
                            trv = PP[3][:, 1, :].bitcast(BF16).rearrange("p (a b) -> p a b", a=8)
                            for kd in range(KD):
                                S.op("pe", f_tr(trv[:, kd, :], yb[:, kd * 128:(kd + 1) * 128], ident[:]),
                                     reads=[("ybf", i % 2), "ident"], writes=["ps7"])
                            S.op("dve", f_copy(yT[:, :, i * 128:(i + 1) * 128], trv), reads=["ps7"],
                                 writes=[("yT", i)])
                        deferred.append(("fin", None, tail))
                    prev = partS[-1]
                    pa = prev[3]
                    partS[-1] = (prev[0], prev[1], prev[2], (lambda pa=pa, fin=fin: ((pa() if pa else None), fin())))
                parts.append((partF, partS))

        idxS = {}
        idxF = {}
        for n_, (pF, pS) in enumerate(parts):
            idxF[n_] = len(steps)
            steps.extend(pF)
            if n_ >= 1:
                idxS[n_ - 1] = len(steps)
                steps.extend(parts[n_ - 1][1])
        idxS[len(parts) - 1] = len(steps)
        steps.extend(parts[-1][1])
        NPT = len(PT)
        DPV = 3
        n_st = len(steps)
        pending = []
        for j in range(n_st + DPV):
            for it_ in [p_ for p_ in pending if p_[0] <= j]:
                it_[1]()
                pending.remove(it_)
            if j < n_st:
                steps[j][0](j % 2)
            if 0 <= j - 1 < n_st:
                steps[j - 1][1]((j - 1) % 2, (j - 1) % NPT)
            k = j - DPV
            if 0 <= k < n_st:
                steps[k][2](k % NPT)
                if steps[k][3] is not None:
                    steps[k][3]()
                for kind, unit, fn in deferred:
                    due = j + 3
                    if kind == "selT":
                        due = idxS[unit]
                        if unit + 1 in idxF:
                            due = min(due, idxF[unit + 1] + DPV)
                    pending.append((max(due, j + 1), fn))
                del deferred[:]
        for it_ in pending:
            it_[1]()
        S.barrier()


def build(debug=(), stop_after=None):
    nc = bass.Bass("TRN2", target_bir_lowering=False)
    dh = {}
    for name, shape in IN_SPECS:
        dh[name] = nc.dram_tensor(name, shape, F32, kind="ExternalInput")
    for name, shape, dt in CONST_SPECS:
        dh["c_" + name] = nc.dram_tensor("c_" + name, shape, dt, kind="ExternalInput")
    OUT = nc.dram_tensor("out", [T, D], F32, kind="ExternalOutput")
    A = {k: v.ap() for k, v in dh.items()}
    dbg_outs = {}

    with contextlib.ExitStack() as st0:
        S = Sched(nc, st0)

        _nm = [0]

        def sb(st, name, shape, dt=F32):
            _nm[0] += 1
            return st.enter_context(nc.sbuf_tensor(f"{name}_{_nm[0]}", shape, dt))

        def dump(name, ap, shape, dt, keys):
            if name not in debug:
                return
            o = nc.dram_tensor("dbg_" + name, shape, dt, kind="ExternalOutput")
            dbg_outs[name] = o
            S.dma("sp", o.ap(), ap, reads=keys, writes=[("dbg", name)])

        def bcast_row(name):
            h = dh[name]
            return bass.AP(h, 0, [[0, 128], [1, h.shape[0]]])

        PP = [st0.enter_context(nc.psum_tensor(f"pp{i}", [128, 2, 512], F32)) for i in range(4)]

        def bank(j):
            return PP[j // 2][:, j % 2, :]

        def bkey(j):
            return ("ps", j)

        ident = sb(st0, "ident", [128, 128], BF16)
        S.dma("sp", ident[:], A["c_ident"], writes=["ident"])
        yT = sb(st0, "yT", [128, KD, T], BF16)
        eps_t = sb(st0, "eps_t", [128, 1], F32)
        S.op("dve", f_memset(eps_t[:], 1e-6), writes=["eps"])

        def norm_rows_T(st, src_rows, ntiles, gname, dstT, dkey, tag):
            xst = [sb(st, f"{tag}_xst{i}", [128, D], F32) for i in range(3)]
            gbc = sb(st, f"{tag}_gbc", [128, D], F32)
            xn = [sb(st, f"{tag}_xn{i}", [128, D], BF16) for i in range(2)]
            junk = sb(st, f"{tag}_junk", [128, D], BF16)
            stat = sb(st, f"{tag}_stat", [128, 4, ntiles], F32)
            S.dma("sp", gbc[:], bcast_row(gname), writes=[(tag, "gbc")])
            for tt in range(ntiles):
                xt = xst[tt % 3]
                xk = (tag, "xst", tt % 3)
                S.dma("sp", xt[:], src_rows(tt), writes=[xk])
                S.op("act", f_act(junk[:], xt[:], AF.Square, accum_out=stat[:, 0, tt:tt + 1]),
                     reads=[xk], writes=[(tag, "junk"), (tag, "st0", tt)])
                S.op("act", f_act(stat[:, 1, tt:tt + 1], stat[:, 0, tt:tt + 1], AF.Sqrt, bias=eps_t[:], scale=1.0 / D),
                     reads=[(tag, "st0", tt), "eps"], writes=[(tag, "st1", tt)])
                S.op("dve", lambda h, tt=tt: h.reciprocal(out=stat[:, 2, tt:tt + 1], in_=stat[:, 1, tt:tt + 1]),
                     reads=[(tag, "st1", tt)], writes=[(tag, "st2", tt)])
                xo = xn[tt % 2]
                S.op("dve", f_stt(xo[:], xt[:], stat[:, 2, tt:tt + 1], gbc[:], ALU.mult, ALU.mult),
                     reads=[xk, (tag, "st2", tt), (tag, "gbc")], writes=[(tag, "xn", tt % 2)])
                bj = tt % 2
                trv = bank(bj).bitcast(BF16).rearrange("p (a b) -> p a b", a=8)
                for kd in range(KD):
                    S.op("pe", f_tr(trv[:, kd, :], xo[:, kd * 128:(kd + 1) * 128], ident[:]),
                         reads=[(tag, "xn", tt % 2), "ident"], writes=[bkey(bj)])
                S.op("act", f_copy(dstT[:, :, tt * 128:(tt + 1) * 128], trv),
                     reads=[bkey(bj)], writes=[(dkey, tt)])

        with contextlib.ExitStack() as stU:
            uT = sb(stU, "uT", [128, KD, T], BF16)
            memT = sb(stU, "memT", [128, KD, 256], BF16)
            with contextlib.ExitStack() as st:
                norm_rows_T(st, lambda tt: A["x"][tt * 128:(tt + 1) * 128, :], NT, "g_mix", uT, "uT", "nx")
                norm_rows_T(st, lambda tt: A["mem"][tt * 128:(tt + 1) * 128, :], 2, "g_mem", memT, "memT", "nm")
                S.barrier()
            dump("uT", uT[:], [128, KD, T], BF16, [("uT", t) for t in range(NT)])
            WIN = A["w_in"].rearrange("(k p) c -> p k c", p=128)
            bank_rr = [0]

            def nb():
                j = bank_rr[0] % 8
                bank_rr[0] += 1
                return j

            def sap(t, pstride, p0, np_, off, dims):
                return bass.AP(t, p0 * pstride + off, [[pstride, np_]] + dims)

            def ukeys(tb):
                return [("uT", t) for t in range(tb * 4, tb * 4 + 4)]

            def tblk(tb):
                return slice(tb * 512, (tb + 1) * 512)

            def proj_fm(bj, wl, tb, wkeys, m=128):
                for kd in range(KD):
                    S.op("pe", f_mm(bank(bj)[0:m, :], wl(kd), uT[:, kd, tblk(tb)], start=(kd == 0), stop=(kd == KD - 1)),
                         reads=wkeys + ukeys(tb), writes=[bkey(bj)])

            def rope_tabs(st):
                rC = sb(st, "ropeC", [128, T], F32)
                rS = sb(st, "ropeS", [128, T], F32)
                S.dma("sp", rC[:], A["c_ropeC"], writes=["ropeC"])
                S.dma("sp", rS[:], A["c_ropeS"], writes=["ropeS"])
                return rC, rS

            def make_rope(st, tag, RTt):
                zb = [sb(st, f"{tag}_zb{i}", [128, 512], BF16) for i in range(2)]
                t1 = [sb(st, f"{tag}_t1{i}", [128, 512], F32) for i in range(2)]
                t2 = [sb(st, f"{tag}_t2{i}", [128, 512], F32) for i in range(2)]
                cnt = [0]

                def rope(bj, n, dst, Cap, Sap, dkeys, a3=None, ck="ropeC", sk="ropeS"):
                    k = cnt[0] % 2
                    cnt[0] += 1
                    zk, t1k, t2k = (tag, "zb", k), (tag, "t1", k), (tag, "t2", k)
                    ps = bank(bj)[:, 0:n]
                    S.op("act", f_copy(zb[k][:, 0:n], ps), reads=[bkey(bj)], writes=[zk])
                    br = nb()
                    S.op("pe", f_mm(bank(br)[:, 0:n], RTt[:], zb[k][:, 0:n]), reads=["RT", zk], writes=[bkey(br)])

                    def v(ap):
                        return ap if a3 is None else ap.rearrange("p (a b) -> p a b", a=a3)
                    S.op("dve", f_tt(v(t1[k][:, 0:n]), v(zb[k][:, 0:n]), Cap, ALU.mult), reads=[zk, ck], writes=[t1k])
                    S.op("dve", f_tt(v(t2[k][:, 0:n]), v(bank(br)[:, 0:n]), Sap, ALU.mult), reads=[bkey(br), sk], writes=[t2k])
                    S.op("dve", f_tt(dst, v(t1[k][:, 0:n]), v(t2[k][:, 0:n]), ALU.add), reads=[t1k, t2k], writes=dkeys)
                return rope

            GORD = [0, 2, 1, 3]
            STAGES = ["P0", "P1a", "P1b", "P1c", "P1d", "P1", "P2", "P3", "P4", "P5"]

            def upto(s_):
                return stop_after is None or STAGES.index(stop_after) >= STAGES.index(s_)
            with contextlib.ExitStack() as stATT:
                KsT = sb(stATT, "KsT", [128, 4, T], BF16)
                KwT = sb(stATT, "KwT", [128, 4, T], BF16)
                Vsw = sb(stATT, "Vsw", [128, NT, 8, 65], BF16)
                KcT = sb(stATT, "KcT", [128, 4, 128], BF16)
                Vc = sb(stATT, "Vc", [128, 4, 97], BF16)
                RTt = sb(stATT, "RTt", [128, 128], BF16)
                S.dma("sp", RTt[:], A["c_RT"], writes=["RT"])

                with contextlib.ExitStack() as st:
                    rC, rS = rope_tabs(st)
                    rope = make_rope(st, "r1", RTt)
                    rCc = sb(st, "rCc", [128, NCMP], F32)
                    rSc = sb(st, "rSc", [128, NCMP], F32)
                    S.dma("sp", rCc[:], A["c_ropeCc"], writes=["ropeCc"])
                    S.dma("sp", rSc[:], A["c_ropeSc"], writes=["ropeSc"])
                    wk = [sb(st, f"wk{i}", [128, KD, 128], BF16) for i in range(2)]
                    n_wk = 0
                    for (c0, dst, dk) in ((C_KS, KsT, "KsT"), (C_KW, KwT, "KwT")):
                        for g in range(4):
                            w = wk[n_wk % 2]
                            wkeys = [("wk", n_wk % 2, 0), ("wk", n_wk % 2, 1)]
                            n_wk += 1
                            src = WIN[:, :, c0 + g * 64:c0 + (g + 1) * 64]
                            S.dma("pool", w[:, :, 0:64], src, writes=[wkeys[0]])
                            S.dma("pool", w[:, :, 64:128], src, writes=[wkeys[1]])
                            for tb in range(4):
                                bj = nb()
                                proj_fm(bj, lambda kd, w=w: w[:, kd, :], tb, wkeys)
                                if "norope" in debug:
                                    S.op("act", f_copy(dst[:, g, tblk(tb)], bank(bj)), reads=[bkey(bj)], writes=[(dk, g, tb)])
                                else:
                                    rope(bj, 512, dst[:, g, tblk(tb)], rC[:, tblk(tb)], rS[:, tblk(tb)], [(dk, g, tb)])
                    wv = sb(st, "wv", [128, KD, 512], BF16)
                    if upto("P1b"):
                        S.dma("pool", wv[:, :, 0:256], WIN[:, :, C_VS:C_VS + 256], writes=[("wv", 0)])
                        S.dma("pool", wv[:, :, 256:512], WIN[:, :, C_VW:C_VW + 256], writes=[("wv", 1)])
                        S.op("pool", f_memset(Vsw[:, :, :, 64:65], 1.0), writes=["Vsw1"])
                        for tt in range(NT):
                            bj = nb()
                            for kd in range(KD):
                                S.op("pe", f_mm(bank(bj), uT[:, kd, tt * 128:(tt + 1) * 128], wv[:, kd, :],
                                                start=(kd == 0), stop=(kd == KD - 1)),
                                     reads=[("uT", tt), ("wv", 0), ("wv", 1)], writes=[bkey(bj)])
                            S.op("act", f_copy(Vsw[:, tt, :, 0:64], bank(bj).rearrange("p (a b) -> p a b", a=8)),
                                 reads=[bkey(bj)], writes=[("Vsw", tt)])
                    w1d = sb(st, "w1d", [128, 32, 256], BF16)
                    kcrT = sb(st, "kcrT", [128, 2, T], BF16)
                    wkc = sb(st, "wkc", [128, KD, 256], BF16)
                    posT = sb(st, "posT", [64, 32], BF16)
                    hidT = sb(st, "hidT", [128, 2, 4, NCMP], BF16)
                    biasK = sb(st, "biasK", [128, 2], F32)
                    w2d = sb(st, "w2d", [128, 2, 128], BF16)
                    for which in (("k", "v") if upto("P1d") else (("k",) if upto("P1c") else ())):
                        c0 = C_KC if which == "k" else C_VC
                        S.dma("pool", wkc[:], WIN[:, :, c0:c0 + 256], writes=["wkc"])
                        w1 = A["w_cmp_%s1" % which].rearrange("(l d) j -> d l j", d=64)
                        S.dma("pool", w1d[0:64], w1, writes=[("w1d", 0)])
                        S.dma("pool", w1d[64:128], w1, writes=[("w1d", 1)])
                        S.dma("pool", posT[:], A["cmp_pos_" + which].rearrange("l d -> d l"), writes=["posT"],
                              allow_slow_non_contiguous=True)
                        w2 = A["w_cmp_%s2" % which].rearrange("(jh p) d -> p jh d", p=128)
                        S.dma("pool", w2d[:, :, 0:64], w2, writes=[("w2d", 0)])
                        if which == "k":
                            S.dma("pool", w2d[:, :, 64:128], w2, writes=[("w2d", 1)])
                        for pair in range(2):
                            for tb in range(4):
                                bj = nb()
                                proj_fm(bj, lambda kd, pair=pair: wkc[:, kd, pair * 128:(pair + 1) * 128], tb, ["wkc"])
                                S.op("act", f_copy(kcrT[:, pair, tblk(tb)], bank(bj)), reads=[bkey(bj)],
                                     writes=[("kcrT", pair, tb)])
                        kcr_keys = [("kcrT", p_, tb) for p_ in range(2) for tb in range(4)]
                        bb = nb()
                        for jh in range(2):
                            for l in range(32):
                                S.op("pe", f_mm(bank(bb)[:, jh:jh + 1], w1d[0:64, l, jh * 128:(jh + 1) * 128],
                                                posT[0:64, l:l + 1], start=(l == 0), stop=(l == 31)),
                                     reads=[("w1d", 0), "posT"], writes=[bkey(bb)])
                        S.op("act", f_copy(biasK[:], bank(bb)[:, 0:2]), reads=[bkey(bb)], writes=["biasK"])
                        for jh in range(2):
                            bA, bB = nb(), nb()
                            for l in range(32):
                                rA = sap(kcrT, 2 * T, 0, 64, l, [[T, 2], [16, NCMP]])
                                rB = sap(kcrT, 2 * T, 64, 64, l, [[T, 2], [16, NCMP]])
                                S.op("pe", f_mm(bank(bA)[:, 0:254], w1d[0:64, l, jh * 128:(jh + 1) * 128], rA,
                                                start=(l == 0), stop=(l == 31)),
                                     reads=kcr_keys + [("w1d", 0)], writes=[bkey(bA)])
                                S.op("pe", f_mm(bank(bB)[:, 0:254], w1d[64:128, l, jh * 128:(jh + 1) * 128], rB,
                                                start=(l == 0), stop=(l == 31)),
                                     reads=kcr_keys + [("w1d", 1)], writes=[bkey(bB)])
                            for (bx, sl) in ((bA, 0), (bB, 1)):
                                S.op("act", f_act(hidT[:, jh, 2 * sl:2 * sl + 2, :],
                                                  bank(bx)[:, 0:254].rearrange("p (a b) -> p a b", a=2),
                                                  AF.Gelu_apprx_tanh, bias=biasK[:, jh:jh + 1]),
                                     reads=[bkey(bx), "biasK"], writes=[("hidT", jh, sl)])
                        hkeys = [("hidT", jh, sl) for jh in range(2) for sl in range(2)]
                        if which == "k":
                            bj = nb()
                            for jh in range(2):
                                S.op("pe", f_mm(bank(bj)[:, 0:508], w2d[:, jh, :], hidT[:, jh, :, :],
                                                start=(jh == 0), stop=(jh == 1)),
                                     reads=hkeys + [("w2d", 0), ("w2d", 1)], writes=[bkey(bj)])
                            rope(bj, 508, KcT[:, :, 0:NCMP],
                                 bass.AP(rCc, 0, [[NCMP, 128], [0, 4], [1, NCMP]]),
                                 bass.AP(rSc, 0, [[NCMP, 128], [0, 4], [1, NCMP]]), ["KcT"], a3=4,
                                 ck="ropeCc", sk="ropeSc")
                        else:
                            bj = nb()
                            first = True
                            for slot in range(4):
                                g = GORD[slot]
                                for jh in range(2):
                                    S.op("pe", f_mm(bank(bj)[0:NCMP, g * 64:(g + 1) * 64], hidT[:, jh, slot, :],
                                                    w2d[:, jh, 0:64], start=first, stop=True, skip=True),
                                         reads=hkeys + [("w2d", 0)], writes=[bkey(bj)])
                                    first = False
                            S.op("act", f_copy(Vc[0:NCMP, :, 0:64],
                                               bank(bj)[0:NCMP, 0:256].rearrange("p (a b) -> p a b", a=4)),
                                 reads=[bkey(bj)], writes=["Vc"])
                            S.dma("sp", Vc[:, :, 64:97], bass.AP(dh["c_c2s33"], 0, [[33, 128], [0, 4], [1, 33]]),
                                  writes=["Vc2"])
                    S.barrier()
                dump("KsT", KsT[:], [128, 4, T], BF16, [])
                dump("KwT", KwT[:], [128, 4, T], BF16, [])
                dump("Vsw", Vsw[:], [128, NT, 8, 65], BF16, [])
                dump("KcT", KcT[:], [128, 4, 128], BF16, [])
                dump("Vc", Vc[:], [128, 4, 97], BF16, [])

                with contextlib.ExitStack() as stQ:
                    if upto("P2"):
                        QT = sb(stQ, "QT", [128, 8, T], BF16)
                        gates = sb(stQ, "gates", [128, NT, 48], F32)
                        with contextlib.ExitStack() as st:
                            rC, rS = rope_tabs(st)
                            rope = make_rope(st, "r2", RTt)
                            wq = [sb(st, f"wq{i}", [128, KD, 512], BF16) for i in range(2)]
                            wg = sb(st, "wg", [128, KD, 48], BF16)
                            for hq in range(2):
                                S.dma("pool", wq[hq][:], WIN[:, :, hq * 512:(hq + 1) * 512], writes=[("wq", hq)])
                            S.dma("pool", wg[:], WIN[:, :, C_NG:C_NG + 48], writes=["wg"])
                            for pair in range(8):
                                w = wq[pair // 4]
                                for tb in range(4):
                                    bj = nb()
                                    proj_fm(bj, lambda kd, w=w, pair=pair: w[:, kd, (pair % 4) * 128:(pair % 4 + 1) * 128],
                                            tb, [("wq", pair // 4)])
                                    rope(bj, 512, QT[:, pair, tblk(tb)], rC[:, tblk(tb)], rS[:, tblk(tb)],
                                         [("QT", pair, tb)])
                            for tt in range(NT):
                                bj = nb()
                                for kd in range(KD):
                                    S.op("pe", f_mm(bank(bj)[:, 0:48], uT[:, kd, tt * 128:(tt + 1) * 128], wg[:, kd, :],
                                                    start=(kd == 0), stop=(kd == KD - 1)),
                                         reads=[("uT", tt), "wg"], writes=[bkey(bj)])
                                S.op("act", f_act(gates[:, tt, :], bank(bj)[:, 0:48], AF.Sigmoid), reads=[bkey(bj)],
                                     writes=[("gates", tt)])
                            S.barrier()
                        dump("QT", QT[:], [128, 8, T], BF16, [])
                        dump("gates", gates[:], [128, NT, 48], F32, [])
                        if upto("P3"):
                            attention(stQ, S, nc, sb, sap, PP, A, dh, ident, QT, gates, KsT, KwT, Vsw, KcT, Vc, yT, GORD)
                    S.barrier()
                S.barrier()
            dump("yT", yT[:], [128, KD, T], BF16, [])

            if upto("P4"):
              with contextlib.ExitStack() as stX:
                oxT = sb(stX, "oxT", [128, 2, T], BF16)
                ones1 = sb(stX, "ones1", [128, 1], F32)
                S.op("dve", f_memset(ones1[:], 1.0), writes=["ones1"])
                with contextlib.ExitStack() as st:
                    wkv = sb(st, "wkv", [128, KD, 512], BF16)
                    wqx = sb(st, "wqx", [128, KD, 256], BF16)
                    KxT = sb(st, "KxT", [128, 2, 256], BF16)
                    Vx = sb(st, "Vx", [128, 2, 4, 65], BF16)
                    QxT = sb(st, "QxT", [128, 2, T], BF16)
                    PTx = [sb(st, f"PTx{i}", [128, 8, 512], BF16) for i in range(2)]
                    oxtm = [sb(st, f"oxtm{i}", [128, 256], BF16) for i in range(2)]
                    sx = [sb(st, f"sx{i}", [128, 8], F32) for i in range(2)]
                    S.dma("pool", wkv[:], A["w_mem_kv"].rearrange("(k p) c -> p k c", p=128), writes=["wkv"])
                    S.dma("pool", wqx[:], WIN[:, :, C_QX:C_QX + 256], writes=["wqx"])
                    S.op("pool", f_memset(Vx[:, :, :, 64:65], 1.0), writes=["Vx1"])
                    mkeys = [("memT", 0), ("memT", 1)]
                    for pair in range(2):
                        bj = nb()
                        for kd in range(KD):
                            S.op("pe", f_mm(bank(bj)[:, 0:256], wkv[:, kd, pair * 128:(pair + 1) * 128], memT[:, kd, :],
                                            start=(kd == 0), stop=(kd == KD - 1)), reads=["wkv"] + mkeys, writes=[bkey(bj)])
                        S.op("act", f_copy(KxT[:, pair, :], bank(bj)[:, 0:256]), reads=[bkey(bj)], writes=[("KxT", pair)])
                    for mt in range(2):
                        bj = nb()
                        for kd in range(KD):
                            S.op("pe", f_mm(bank(bj)[:, 0:256], memT[:, kd, mt * 128:(mt + 1) * 128], wkv[:, kd, 256:512],
                                            start=(kd == 0), stop=(kd == KD - 1)), reads=["wkv"] + mkeys, writes=[bkey(bj)])
                        S.op("act", f_copy(Vx[:, mt, :, 0:64], bank(bj)[:, 0:256].rearrange("p (a b) -> p a b", a=4)),
                             reads=[bkey(bj)], writes=[("Vx", mt)])
                    for pair in range(2):
                        for tb in range(4):
                            bj = nb()
                            proj_fm(bj, lambda kd, pair=pair: wqx[:, kd, pair * 128:(pair + 1) * 128], tb, ["wqx"])
                            S.op("act", f_copy(QxT[:, pair, tblk(tb)], bank(bj)), reads=[bkey(bj)], writes=[("QxT", pair, tb)])
                    for qb in range(4):
                        px = PTx[qb % 2]
                        for h_ in range(4):
                            pair, half = h_ // 2, h_ % 2
                            for mt in range(2):
                                bj = nb()
                                S.op("pe", f_mm(bank(bj), KxT[half * 64:(half + 1) * 64, pair, mt * 128:(mt + 1) * 128],
                                                QxT[half * 64:(half + 1) * 64, pair, tblk(qb)]),
                                     reads=[("KxT", pair), ("QxT", pair, qb)], writes=[bkey(bj)])
                                S.op("act", f_act(px[:, h_ * 2 + mt, :], bank(bj), AF.Exp, scale=0.125), reads=[bkey(bj)],
                                     writes=[("PTx", qb % 2, h_ * 2 + mt)])
                        pkeys = [("PTx", qb % 2, j_) for j_ in range(8)]
                        for qt in range(4):
                            tt = qb * 4 + qt
                            bj = nb()
                            first = True
                            for h_ in range(4):
                                for mt in range(2):
                                    S.op("pe", f_mm(bank(bj)[:, h_ * 65:(h_ + 1) * 65], px[:, h_ * 2 + mt, qt * 128:(qt + 1) * 128],
                                                    Vx[:, mt, h_, :], start=first, stop=True, skip=True),
                                         reads=pkeys + [("Vx", mt), "Vx1"], writes=[bkey(bj)])
                                    first = False
                            k2 = tt % 2
                            sums = sap(PP[bj // 2], 1024, 0, 128, (bj % 2) * 512 + 64, [[65, 4]])
                            S.op("dve", lambda h, k2=k2, sums=sums: h.reciprocal(out=sx[k2][:, 0:4], in_=sums),
                                 reads=[bkey(bj)], writes=[("sx", k2)])
                            dat = sap(PP[bj // 2], 1024, 0, 128, (bj % 2) * 512, [[65, 4], [1, 64]])
                            rcb = sap(sx[k2], 8, 0, 128, 0, [[1, 4], [0, 64]])
                            S.op("dve", f_tt(oxtm[k2][:].rearrange("p (a b) -> p a b", a=4), dat, rcb, ALU.mult),
                                 reads=[bkey(bj), ("sx", k2)], writes=[("oxtm", k2)])
                            bt = nb()
                            trv = bank(bt).bitcast(BF16)[:, 0:256].rearrange("p (a b) -> p a b", a=2)
                            for kc in range(2):
                                S.op("pe", f_tr(trv[:, kc, :], oxtm[k2][:, kc * 128:(kc + 1) * 128], ident[:]),
                                     reads=[("oxtm", k2), "ident"], writes=[bkey(bt)])
                            S.op("dve", f_copy(oxT[:, :, tt * 128:(tt + 1) * 128], trv), reads=[bkey(bt)], writes=[("oxT", tt)])
                    S.barrier()
                dump("oxT", oxT[:], [128, 2, T], BF16, [])

                with contextlib.ExitStack() as st:
                    wbd = [sb(st, f"wbd{i}", [128, 8, 128], BF16) for i in range(2)]
                    wxo = sb(st, "wxo", [128, 2, D], BF16)
                    par = sb(st, "par", [128, 12, 8], F32)
                    S.dma("pool", wxo[:], A["w_xo"].rearrange("(kc p) d -> p kc d", p=128), writes=["wxo"])
                    for i_, nm in enumerate(("w_rg_a", "w_rg_i")):
                        S.op("pool", f_memset(wbd[i_][:], 0.0), writes=[("wbd", i_)])
                        for n_ in range(16):
                            c_, hf = n_ // 2, n_ % 2
                            S.dma("pool", wbd[i_][hf * 64:(hf + 1) * 64, c_, hf * 64:(hf + 1) * 64], A[nm][n_],
                                  reads=[], writes=[("wbd", i_)] if n_ == 0 else [("wbd", i_, n_)])
                    wbd_keys = [[("wbd", i_)] + [("wbd", i_, n_) for n_ in range(1, 16)] for i_ in range(2)]
                    S.dma("sp", par[:, 0:4, :], A["conv_w"].rearrange("k (c p) -> p k c", p=128), writes=[("par", 0)],
                          allow_slow_non_contiguous=True)
                    for j_, nm in ((4, "conv_b"), (5, "b_rg_a"), (6, "b_rg_i"), (7, "rg_lambda")):
                        S.dma("sp", par[:, j_, :], A[nm].rearrange("(c p) -> p c", p=128), writes=[("par", j_)],
                              allow_slow_non_contiguous=True)
                    S.op("act", f_act(par[:, 10, :], par[:, 7, :], AF.Exp, scale=-1.0), reads=[("par", 7)], writes=[("par", 10)])
                    S.op("act", f_act(par[:, 11, :], par[:, 10, :], AF.Ln, bias=ones1[:], scale=1.0), reads=[("par", 10), "ones1"],
                         writes=[("par", 11)])
                    S.op("dve", f_ts(par[:, 8, :], par[:, 11, :], -8.0, None, ALU.mult), reads=[("par", 11)], writes=[("par", 8)])
                    S.op("dve", f_ts(par[:, 9, :], par[:, 11, :], -16.0, None, ALU.mult), reads=[("par", 11)], writes=[("par", 9)])
                    pk_all = [("par", j_) for j_ in range(12)]
                    wr = [sb(st, f"wr{i}", [128, KD, 5, 128], BF16) for i in range(2)]
                    xr = sb(st, "xr", [128, T + 3], F32)
                    xc = sb(st, "xc", [128, T], F32)
                    xcb = sb(st, "xcb", [128, T], BF16)
                    Rr = sb(st, "Rr", [128, T], F32)
                    Aa = sb(st, "Aa", [128, T], F32)
                    GI = sb(st, "GI", [128, T], F32)
                    GL = sb(st, "GL", [128, T], F32)
                    GM = [sb(st, f"GM{i}", [128, T], F32) for i in range(3)]
                    ACC = sb(st, "ACC", [128, T], F32)
                    S.op("pool", f_memset(xr[:, 0:3], 0.0), writes=["xr0"])
                    ngm = [0]

                    def K4(nm):
                        return [(nm, tb) for tb in range(4)]
                    def load_wr(c):
                        cols = [C_XR + c * 128, C_GR + c * 128, C_MG + c * 128, C_MG + D + c * 128, C_MG + 2 * D + c * 128]
                        for j_ in range(5):
                            S.dma("pool", wr[c % 2][:, :, j_, :], WIN[:, :, cols[j_]:cols[j_] + 128], writes=[("wr", c % 2, j_)])
                    load_wr(0)
                    for c in range(8):
                        w = wr[c % 2]
                        wkeys = [("wr", c % 2, j_) for j_ in range(5)]
                        if c + 1 < 8:
                            load_wr(c + 1)

                        def proj_c(j_, tb, w=w, wkeys=wkeys):
                            bj = nb()
                            proj_fm(bj, lambda kd: w[:, kd, j_, :], tb, [wkeys[j_]])
                            return bj
                        ykeys = [("yT", t_) for t_ in range(NT)]
                        for tb in range(4):
                            bj = proj_c(0, tb)
                            S.op("act", f_copy(xr[:, 3 + tb * 512:3 + (tb + 1) * 512], bank(bj)), reads=[bkey(bj)],
                                 writes=[("xr", tb)])
                        S.op("dve", f_ts(xc[:], xr[:, 3:3 + T], par[:, 3, c:c + 1], par[:, 4, c:c + 1], ALU.mult, ALU.add),
                             reads=K4("xr") + pk_all, writes=K4("xc"))
                        for k_ in range(3):
                            S.op("dve", f_stt(xc[:], xr[:, k_:k_ + T], par[:, k_, c:c + 1], xc[:], ALU.mult, ALU.add),
                                 reads=K4("xr") + ["xr0"] + K4("xc") + pk_all, writes=K4("xc"))
                        S.op("dve", f_copy(xcb[:], xc[:]), reads=K4("xc"), writes=K4("xcb"))
                        for tb in range(4):
                            bj = proj_c(1, tb)
                            S.op("act", f_act(GL[:, tblk(tb)], bank(bj), AF.Gelu_apprx_tanh), reads=[bkey(bj)], writes=[("GL", tb)])
                        for b_ in range(3):
                            for tb in range(4):
                                bj = proj_c(2 + b_, tb)
                                S.op("act", f_act(GM[b_][:, tblk(tb)], bank(bj), AF.Sigmoid), reads=[bkey(bj)],
                                     writes=[("GM%d" % b_, tb)])
                        S.op("pool", f_tt(ACC[:], GM[0][:], yT[:, c, :], ALU.mult), reads=K4("GM0") + ykeys, writes=K4("ACC"))
                        for tb in range(4):
                            bj = nb()
                            for kc in range(2):
                                S.op("pe", f_mm(bank(bj), wxo[:, kc, c * 128:(c + 1) * 128], oxT[:, kc, tblk(tb)],
                                                start=(kc == 0), stop=(kc == 1)),
                                     reads=["wxo"] + [("oxT", t_) for t_ in range(tb * 4, tb * 4 + 4)], writes=[bkey(bj)])
                            S.op("dve", f_tt(GM[2][:, tblk(tb)], GM[2][:, tblk(tb)], bank(bj), ALU.mult),
                                 reads=[bkey(bj), ("GM2", tb)], writes=[("GM2", tb)])
                        for (i_, dstt, bcol) in ((0, Rr, 5), (1, GI, 6)):
                            for tb in range(4):
                                bj = nb()
                                S.op("pe", f_mm(bank(bj), wbd[i_][:, c, :], xcb[:, tblk(tb)]), reads=wbd_keys[i_] + K4("xcb"),
                                     writes=[bkey(bj)])
                                S.op("act", f_act(dstt[:, tblk(tb)], bank(bj), AF.Sigmoid, bias=par[:, bcol, c:c + 1]),
                                     reads=[bkey(bj)] + pk_all, writes=[("Rr" if i_ == 0 else "GI", tb)])
                        S.op("act", f_act(Aa[:], Rr[:], AF.Exp, scale=par[:, 8, c:c + 1]), reads=K4("Rr") + pk_all, writes=K4("Aa"))
                        S.op("act", f_act(Rr[:], Rr[:], AF.Exp, scale=par[:, 9, c:c + 1]), reads=K4("Rr") + pk_all, writes=K4("Rr"))
                        S.op("act", f_act(Rr[:], Rr[:], AF.Sqrt, bias=ones1[:], scale=-1.0), reads=K4("Rr") + ["ones1"], writes=K4("Rr"))
                        S.op("dve", f_tt(GI[:], GI[:], xc[:], ALU.mult), reads=K4("GI") + K4("xc"), writes=K4("GI"))
                        S.op("dve", f_tt(GI[:], GI[:], Rr[:], ALU.mult), reads=K4("GI") + K4("Rr"), writes=K4("GI"))
                        S.op("dve", lambda h: h.tensor_tensor_scan(out=xc[:], data0=Aa[:], data1=GI[:], initial=0.0,
                                                                    op0=ALU.mult, op1=ALU.add),
                             reads=K4("Aa") + K4("GI") + K4("xc"), writes=K4("xc"))
                        S.op("dve", f_tt(GL[:], GL[:], xc[:], ALU.mult), reads=K4("GL") + K4("xc"), writes=K4("GL"))
                        S.op("dve", f_tt(GM[1][:], GM[1][:], GL[:], ALU.mult), reads=K4("GM1") + K4("GL"), writes=K4("GM1"))
                        S.op("pool", f_tt(ACC[:], ACC[:], GM[2][:], ALU.add), reads=K4("GM2") + K4("ACC"), writes=K4("ACC"))
                        S.op("dve", f_tt(yT[:, c, :], ACC[:], GM[1][:], ALU.add), reads=K4("GM1") + K4("ACC") + ykeys,
                             writes=ykeys)
                    S.barrier()
              S.barrier()
            dump("ymT", yT[:], [128, KD, T], BF16, [])

        if stop_after is None or stop_after == "P5":
            TBK = 1024
            NFH = 2
            NTL = TBK // 128
            with contextlib.ExitStack() as st:
                wo = sb(st, "wo", [128, KD, D], BF16)
                WO = A["w_o"].rearrange("(k p) d -> p k d", p=128)
                for hf in range(2):
                    S.dma("pool", wo[:, :, hf * 512:(hf + 1) * 512], WO[:, :, hf * 512:(hf + 1) * 512], writes=[("wo", hf)])
                gml = sb(st, "gml", [128, D], F32)
                gfn = sb(st, "gfn", [128, D], F32)
                S.dma("sp", gml[:], bcast_row("g_mlp"), writes=["gml"])
                S.dma("sp", gfn[:], bcast_row("g_final"), writes=["gfn"])
                h2 = sb(st, "h2", [128, NTL, D], F32)
                vT = sb(st, "vT", [128, KD, TBK], BF16)
                ffT = sb(st, "ffT", [128, 16, TBK], BF16)
                wb = [sb(st, f"wb{i}", [128, 4096], BF16) for i in range(5)]
                xs = [sb(st, f"xs{i}", [128, D], F32) for i in range(2)]
                xnb = [sb(st, f"xnb{i}", [128, D], BF16) for i in range(2)]
                rl = [sb(st, f"rl{i}", [128, 512], F32) for i in range(3)]
                junk5 = sb(st, "junk5", [128, D], BF16)
                st5 = sb(st, "st5", [128, 6, 16], F32)
                WUP = A["w_up"].rearrange("(k p) f -> p k f", p=128)
                WDN = A["w_down"].rearrange("(fc p) d -> p fc d", p=128)
                nwb = [0]
                nxs = [0]
                nrl = [0]

                def rms_stats(src, tl, col0, tag):
                    S.op("act", f_act(junk5[:], src, AF.Square, accum_out=st5[:, col0, tl:tl + 1]),
                         reads=[("h2", tl)], writes=["junk5", (tag, 0, tl)])
                    S.op("act", f_act(st5[:, col0 + 1, tl:tl + 1], st5[:, col0, tl:tl + 1], AF.Sqrt, bias=eps_t[:], scale=1.0 / D),
                         reads=[(tag, 0, tl), "eps"], writes=[(tag, 1, tl)])
                    S.op("dve", lambda h: h.reciprocal(out=st5[:, col0 + 2, tl:tl + 1], in_=st5[:, col0 + 1, tl:tl + 1]),
                         reads=[(tag, 1, tl)], writes=[(tag, 2, tl)])

                wsched = []
                for blk in range(T // TBK):
                    for fh in range(NFH):
                        for wbi in range(4):
                            wsched.append(("up", fh * 2048 + wbi * 512))
                        for wbi in range(4):
                            wsched.append(("dn", fh * 16 + wbi * 4))
                nld = [0]

                def load_next():
                    k_ = nld[0]
                    if k_ >= len(wsched):
                        return
                    nld[0] += 1
                    kind, off = wsched[k_]
                    if kind == "up":
                        S.dma("pool", wb[k_ % 5][:].rearrange("p (k f) -> p k f", k=KD), WUP[:, :, off:off + 512], writes=[("wb", k_ % 5)])
                    else:
                        S.dma("pool", wb[k_ % 5][:].rearrange("p (k f) -> p k f", k=4), WDN[:, off:off + 4, :], writes=[("wb", k_ % 5)])
                for _ in range(3):
                    load_next()

                for blk in range(T // TBK):
                    t0 = blk * TBK
                    for tl in range(NTL):
                        tt = blk * NTL + tl
                        xk = nxs[0] % 2
                        nxs[0] += 1
                        S.dma("sp", xs[xk][:], A["x"][tt * 128:(tt + 1) * 128, :], writes=[("xs", xk)])
                        for dh in range(2):
                            bj = nb()
                            for kd in range(KD):
                                S.op("pe", f_mm(bank(bj), yT[:, kd, tt * 128:(tt + 1) * 128], wo[:, kd, dh * 512:(dh + 1) * 512],
                                                start=(kd == 0), stop=(kd == KD - 1)),
                                     reads=[("yT", t_) for t_ in range(NT)] + [("wo", dh)], writes=[bkey(bj)])
                            S.op("dve", f_tt(h2[:, tl, dh * 512:(dh + 1) * 512], bank(bj), xs[xk][:, dh * 512:(dh + 1) * 512], ALU.add),
                                 reads=[bkey(bj), ("xs", xk)], writes=[("h2", tl)])
                        rms_stats(h2[:, tl, :], tl, 0, "sa")
                        nk = tl % 2
                        S.op("dve", f_stt(xnb[nk][:], h2[:, tl, :], st5[:, 2, tl:tl + 1], gml[:], ALU.mult, ALU.mult),
                             reads=[("h2", tl), ("sa", 2, tl), "gml"], writes=[("xnb", nk)])
                        bj = nb()
                        trv = bank(bj).bitcast(BF16).rearrange("p (a b) -> p a b", a=8)
                        for kd in range(KD):
                            S.op("pe", f_tr(trv[:, kd, :], xnb[nk][:, kd * 128:(kd + 1) * 128], ident[:]),
                                 reads=[("xnb", nk), "ident"], writes=[bkey(bj)])
                        S.op("act", f_copy(vT[:, :, tl * 128:(tl + 1) * 128], trv), reads=[bkey(bj)], writes=[("vT", tl)])
                    vkeys = [("vT", tl) for tl in range(NTL)]
                    for fh in range(NFH):
                        for wbi in range(4):
                            wk_ = nwb[0] % 5
                            nwb[0] += 1
                            load_next()
                            wt = wb[wk_][:].rearrange("p (k f) -> p k f", k=KD)
                            for fl in range(4):
                                fc = wbi * 4 + fl
                                for tb in range(TBK // 512):
                                    bj = nb()
                                    for kd in range(KD):
                                        S.op("pe", f_mm(bank(bj), wt[:, kd, fl * 128:(fl + 1) * 128], vT[:, kd, tblk(tb)],
                                                        start=(kd == 0), stop=(kd == KD - 1)),
                                             reads=[("wb", wk_)] + vkeys, writes=[bkey(bj)])
                                    rk = nrl[0] % 3
                                    nrl[0] += 1
                                    S.op("act", f_act(rl[rk][:], bank(bj), AF.Relu), reads=[bkey(bj)], writes=[("rl", rk)])
                                    S.op("pool" if nrl[0] % 4 == 0 else "dve", f_tt(ffT[:, fc, tblk(tb)], rl[rk][:], rl[rk][:], ALU.mult),
                                         reads=[("rl", rk)], writes=[("ffT", fc, tb)])
                        for wbi in range(4):
                            wk_ = nwb[0] % 5
                            nwb[0] += 1
                            load_next()
                            wt = wb[wk_][:].rearrange("p (k f) -> p k f", k=4)
                            for tl in range(NTL):
                                for dh in range(2):
                                    bj = nb()
                                    for fl in range(4):
                                        fc = wbi * 4 + fl
                                        S.op("pe", f_mm(bank(bj), ffT[:, fc, tl * 128:(tl + 1) * 128], wt[:, fl, dh * 512:(dh + 1) * 512],
                                                        start=(fl == 0), stop=(fl == 3)),
                                             reads=[("wb", wk_), ("ffT", fc, tl // 4)], writes=[bkey(bj)])
                                    S.op("dve", f_tt(h2[:, tl, dh * 512:(dh + 1) * 512], h2[:, tl, dh * 512:(dh + 1) * 512], bank(bj), ALU.add),
                                         reads=[bkey(bj), ("h2", tl)], writes=[("h2", tl)])
                    for tl in range(NTL):
                        tt = blk * NTL + tl
                        rms_stats(h2[:, tl, :], tl, 3, "sb")
                        xk = nxs[0] % 2
                        nxs[0] += 1
                        S.op("dve", f_stt(xs[xk][:], h2[:, tl, :], st5[:, 5, tl:tl + 1], gfn[:], ALU.mult, ALU.mult),
                             reads=[("h2", tl), ("sb", 2, tl), "gfn"], writes=[("xs", xk)])
                        S.dma("sp", OUT.ap()[tt * 128:(tt + 1) * 128, :], xs[xk][:], reads=[("xs", xk)], writes=[("out", tt)])
                S.barrier()

        S.wait_keys("sp", [("dbg", n) for n in dbg_outs] + [("out", t) for t in range(NT)])
        S.emit()
    return nc, dbg_outs


def kernel(**inputs):
    nc, _ = build()
    consts = _consts()
    shared = {}
    for name, shape in IN_SPECS:
        if name in ("x", "mem"):
            continue
        a = np.asarray(inputs[name], dtype=np.float32)
        if name != "g_final":
            a = a[0]
        shared[name] = np.ascontiguousarray(a.reshape(shape))
    for name, shape, dt in CONST_SPECS:
        shared["c_" + name] = consts[name]
    x = np.asarray(inputs["x"], dtype=np.float32)
    mem = np.asarray(inputs["mem"], dtype=np.float32)
    in_maps = []
    for b in range(8):
        m = dict(shared)
        m["x"] = np.ascontiguousarray(x[b])
        m["mem"] = np.ascontiguousarray(mem[b])
        in_maps.append(m)
    res = run_bass_kernel_spmd(nc, in_maps, core_ids=list(range(8)))
    return np.stack([np.asarray(res.results[b]["out"], dtype=np.float32) for b in range(8)], axis=0)
```
